# Optimizing a Trainium2 kernel written in Bass

```python
import jax
import jax.numpy as jnp
from jax import lax
import numpy as np


D_MODEL = 1024
BATCH = 4
SEQ = 8192
DEPTH = 4

N_MIXERS = 3
CHUNK = 64
NORM_EPS = 1e-6
RET_HEADS = 4
RET_QK_DIM = D_MODEL // RET_HEADS
RET_V_DIM = 2 * D_MODEL // RET_HEADS
RET_IN = 2 * RET_HEADS * RET_QK_DIM + 2 * RET_HEADS * RET_V_DIM
ROPE_BASE = 10000.0
CONV_WIDTH = 31
GLA_HEADS = 4
GLA_K_DIM = D_MODEL // 2 // GLA_HEADS
GLA_V_DIM = D_MODEL // GLA_HEADS
GLA_IN = 2 * GLA_HEADS * GLA_K_DIM + 2 * GLA_HEADS * GLA_V_DIM
GLA_GATE_RANK = 16
GLA_GATE_NORMALIZER = 16.0
D_FF = 4 * D_MODEL
N_RET = (DEPTH + 2) // 3
N_CONV = (DEPTH + 1) // 3
N_GLA = DEPTH // 3

F32 = jnp.float32

kernel_name = "hybrid_retention_conformer_gla_encoder"


def rmsnorm(x, gain=None):
    xf = x.astype(F32)
    y = xf * lax.rsqrt(jnp.mean(xf * xf, axis=-1, keepdims=True) + NORM_EPS)
    if gain is not None:
        y = y * gain.astype(F32)
    return y.astype(x.dtype)


def layernorm(x, gain, bias):
    xf = x.astype(F32)
    mu = jnp.mean(xf, axis=-1, keepdims=True)
    var = jnp.mean(jnp.square(xf - mu), axis=-1, keepdims=True)
    y = (xf - mu) * lax.rsqrt(var + NORM_EPS) * gain.astype(F32) + bias.astype(F32)
    return y.astype(x.dtype)


def split_heads(t, n):
    b, l, _ = t.shape
    return t.reshape(b, l, n, -1).transpose(0, 2, 1, 3)


def merge_heads(t):
    b, n, l, d = t.shape
    return t.transpose(0, 2, 1, 3).reshape(b, l, n * d)


def rotary(t, positions):
    d = t.shape[-1]
    half = d // 2
    inv_freq = ROPE_BASE ** (-jnp.arange(half, dtype=F32) / half)
    ang = positions.astype(F32)[:, None, :, None] * inv_freq
    cos, sin = jnp.cos(ang), jnp.sin(ang)
    tf = t.astype(F32)
    t1, t2 = tf[..., :half], tf[..., half:]
    return jnp.concatenate([t1 * cos - t2 * sin, t2 * cos + t1 * sin], axis=-1).astype(t.dtype)


def gated_linear_scan(q, k, v, log_g, strict):
    dtype = v.dtype
    bsz, nh, seq, dk = q.shape
    dv = v.shape[-1]
    n = seq // CHUNK
    q = q.astype(F32).reshape(bsz, nh, n, CHUNK, dk)
    k = k.astype(F32).reshape(bsz, nh, n, CHUNK, dk)
    v = v.astype(F32).reshape(bsz, nh, n, CHUNK, dv)
    lg = log_g.astype(F32)
    lg = lg.reshape(lg.shape[0], nh, n, CHUNK, lg.shape[-1])
    c = jnp.cumsum(lg, axis=3)
    c_last = c[:, :, :, -1:, :]
    q_dec = q * jnp.exp(c)
    k_dec = k * jnp.exp(-c)
    k_end = k * jnp.exp(c_last - c)
    mask = jnp.tril(jnp.ones((CHUNK, CHUNK), dtype=bool), k=-1 if strict else 0)
    scores = jnp.einsum("bhncd,bhnsd->bhncs", q_dec, k_dec)
    o_intra = jnp.einsum("bhncs,bhnsv->bhncv", jnp.where(mask, scores, 0.0), v)
    chunk_decay = jnp.exp(c_last[:, :, :, 0, :])

    def step(state, inp):
        qd, ke, vv, dec = inp
        o = jnp.einsum("bhcd,bhdv->bhcv", qd, state)
        state = dec[..., None] * state + jnp.einsum("bhcd,bhcv->bhdv", ke, vv)
        return state, o

    state0 = jnp.zeros((bsz, nh, dk, dv), F32)
    xs = (jnp.moveaxis(q_dec, 2, 0), jnp.moveaxis(k_end, 2, 0),
          jnp.moveaxis(v, 2, 0), jnp.moveaxis(chunk_decay, 2, 0))
    _, o_inter = lax.scan(step, state0, xs)
    o = o_intra + jnp.moveaxis(o_inter, 0, 2)
    return o.reshape(bsz, nh, seq, dv).astype(dtype)


def bidirectional_scan(q, k, v, log_g_fwd, log_g_bwd):
    fwd = gated_linear_scan(q, k, v, log_g_fwd, strict=False)
    rev = lambda t: jnp.flip(t, axis=2)
    bwd = rev(gated_linear_scan(rev(q), rev(k), rev(v), rev(log_g_bwd), strict=True))
    return fwd + bwd


def retention(u, positions, w_in, decay_logit, w_out):
    seq = u.shape[1]
    qk = RET_HEADS * RET_QK_DIM
    vd = RET_HEADS * RET_V_DIM
    q, k, v, g = jnp.split(u @ w_in, [qk, 2 * qk, 2 * qk + vd], axis=-1)
    q = rotary(split_heads(q, RET_HEADS), positions) * (RET_QK_DIM ** -0.5)
    k = rotary(split_heads(k, RET_HEADS), positions)
    v = split_heads(v, RET_HEADS)
    log_gamma = jax.nn.log_sigmoid(decay_logit.astype(F32))
    lg = lambda d: jnp.broadcast_to(log_gamma[d][None, :, None, None], (1, RET_HEADS, seq, 1))
    o = bidirectional_scan(q, k, v, lg(0), lg(1))
    o = merge_heads(rmsnorm(o)) * jax.nn.silu(g)
    return o @ w_out


def conformer_conv(u, w_in, b_in, w_dw, b_dw, ln_gain, ln_bias, w_out, b_out):
    a, gate = jnp.split(u @ w_in + b_in, 2, axis=-1)
    h = a * jax.nn.sigmoid(gate)
    pad = CONV_WIDTH // 2
    h = lax.conv_general_dilated(
        h, w_dw[:, None, :].astype(h.dtype), window_strides=(1,), padding=[(pad, pad)],
        dimension_numbers=("NWC", "WIO", "NWC"), feature_group_count=D_MODEL) + b_dw
    h = jax.nn.silu(layernorm(h, ln_gain, ln_bias))
    return h @ w_out + b_out


def gla(u, w_in, gate_w1, gate_w2, gate_b, norm_gain, w_out):
    kd = GLA_HEADS * GLA_K_DIM
    vd = GLA_HEADS * GLA_V_DIM
    q, k, v, r = jnp.split(u @ w_in, [kd, 2 * kd, 2 * kd + vd], axis=-1)
    q = split_heads(q, GLA_HEADS) * (GLA_K_DIM ** -0.5)
    k = split_heads(k, GLA_HEADS)
    v = split_heads(v, GLA_HEADS)

    def log_gate(d):
        logits = (u @ gate_w1[d]) @ gate_w2[d] + gate_b[d]
        return split_heads(jax.nn.log_sigmoid(logits.astype(F32)) / GLA_GATE_NORMALIZER, GLA_HEADS)

    o = bidirectional_scan(q, k, v, log_gate(0), log_gate(1))
    o = merge_heads(rmsnorm(o, norm_gain)) * jax.nn.silu(r)
    return o @ w_out


def sq_relu_mlp(u, w_up, w_down):
    return jnp.square(jax.nn.relu(u @ w_up)) @ w_down


def setup_inputs(seed: int = 0) -> dict:
    key = jax.random.key(seed)
    ks = jax.random.split(key, 24)
    nrm = lambda k, shape, scale: jax.random.normal(k, shape, F32) * scale
    x = jax.random.normal(ks[0], (BATCH, SEQ, D_MODEL), F32)
    positions = (jnp.arange(SEQ, dtype=jnp.int32)[None, :]
                 + jax.random.randint(ks[1], (BATCH, 1), 0, SEQ, dtype=jnp.int32))
    norm_gains = 1.0 + nrm(ks[2], (DEPTH, 4, D_MODEL), 0.1)
    ret_w_in = nrm(ks[3], (N_RET, D_MODEL, RET_IN), D_MODEL ** -0.5)
    a = 5.0 + jnp.arange(RET_HEADS, dtype=F32)
    base_logit = jnp.log(2.0 ** a - 1.0)
    ret_decay_logit = base_logit + nrm(ks[4], (N_RET, 2, RET_HEADS), 0.05)
    ret_w_out = nrm(ks[5], (N_RET, RET_HEADS * RET_V_DIM, D_MODEL), (RET_HEADS * RET_V_DIM) ** -0.5)
    conv_w_in = nrm(ks[6], (N_CONV, D_MODEL, 2 * D_MODEL), D_MODEL ** -0.5)
    conv_b_in = nrm(ks[7], (N_CONV, 2 * D_MODEL), 0.02)
    conv_w_dw = nrm(ks[8], (N_CONV, CONV_WIDTH, D_MODEL), CONV_WIDTH ** -0.5)
    conv_b_dw = nrm(ks[9], (N_CONV, D_MODEL), 0.02)
    conv_ln_gain = 1.0 + nrm(ks[10], (N_CONV, D_MODEL), 0.1)
    conv_ln_bias = nrm(ks[11], (N_CONV, D_MODEL), 0.02)
    conv_w_out = nrm(ks[12], (N_CONV, D_MODEL, D_MODEL), D_MODEL ** -0.5)
    conv_b_out = nrm(ks[13], (N_CONV, D_MODEL), 0.02)
    gla_w_in = nrm(ks[14], (N_GLA, D_MODEL, GLA_IN), D_MODEL ** -0.5)
    gla_gate_w1 = nrm(ks[15], (N_GLA, 2, D_MODEL, GLA_GATE_RANK), D_MODEL ** -0.5)
    gla_gate_w2 = nrm(ks[16], (N_GLA, 2, GLA_GATE_RANK, GLA_HEADS * GLA_K_DIM), GLA_GATE_RANK ** -0.5)
    gla_gate_b = nrm(ks[17], (N_GLA, 2, GLA_HEADS * GLA_K_DIM), 0.1)
    gla_norm_gain = 1.0 + nrm(ks[18], (N_GLA, GLA_V_DIM), 0.1)
    gla_w_out = nrm(ks[19], (N_GLA, GLA_HEADS * GLA_V_DIM, D_MODEL), (GLA_HEADS * GLA_V_DIM) ** -0.5)
    mlp_w_up = nrm(ks[20], (DEPTH, D_MODEL, D_FF), D_MODEL ** -0.5)
    mlp_w_down = nrm(ks[21], (DEPTH, D_FF, D_MODEL), D_FF ** -0.5)
    return {
        "x": x, "positions": positions, "norm_gains": norm_gains,
        "ret_w_in": ret_w_in, "ret_decay_logit": ret_decay_logit, "ret_w_out": ret_w_out,
        "conv_w_in": conv_w_in, "conv_b_in": conv_b_in, "conv_w_dw": conv_w_dw,
        "conv_b_dw": conv_b_dw, "conv_ln_gain": conv_ln_gain, "conv_ln_bias": conv_ln_bias,
        "conv_w_out": conv_w_out, "conv_b_out": conv_b_out,
        "gla_w_in": gla_w_in, "gla_gate_w1": gla_gate_w1, "gla_gate_w2": gla_gate_w2,
        "gla_gate_b": gla_gate_b, "gla_norm_gain": gla_norm_gain, "gla_w_out": gla_w_out,
        "mlp_w_up": mlp_w_up, "mlp_w_down": mlp_w_down,
    }


def reference(x, positions, norm_gains, ret_w_in, ret_decay_logit, ret_w_out,
              conv_w_in, conv_b_in, conv_w_dw, conv_b_dw, conv_ln_gain, conv_ln_bias,
              conv_w_out, conv_b_out, gla_w_in, gla_gate_w1, gla_gate_w2, gla_gate_b,
              gla_norm_gain, gla_w_out, mlp_w_up, mlp_w_down):
    h = x
    for i in range(DEPTH):
        kind = i % N_MIXERS
        j = i // N_MIXERS
        u = rmsnorm(h, norm_gains[i, 0])
        if kind == 0:
            y = retention(u, positions, ret_w_in[j], ret_decay_logit[j], ret_w_out[j])
        elif kind == 1:
            y = conformer_conv(u, conv_w_in[j], conv_b_in[j], conv_w_dw[j], conv_b_dw[j],
                               conv_ln_gain[j], conv_ln_bias[j], conv_w_out[j], conv_b_out[j])
        else:
            y = gla(u, gla_w_in[j], gla_gate_w1[j], gla_gate_w2[j], gla_gate_b[j],
                    gla_norm_gain[j], gla_w_out[j])
        h = h + rmsnorm(y, norm_gains[i, 1])
        u = rmsnorm(h, norm_gains[i, 2])
        h = h + rmsnorm(sq_relu_mlp(u, mlp_w_up[i], mlp_w_down[i]), norm_gains[i, 3])
    return h
```

```python
import numpy as np
from contextlib import ExitStack
import concourse.bass as bass
import concourse.mybir as mybir
from concourse.bass_utils import run_bass_kernel_spmd

F32 = mybir.dt.float32
BF16 = mybir.dt.bfloat16
I32 = mybir.dt.int32
AF = mybir.ActivationFunctionType
ALU = mybir.AluOpType

ENGS = ("pe", "act", "dve", "pool", "sp")


def _key(x):
    if isinstance(x, (str, tuple)):
        return x
    t = getattr(x, "tensor", None)
    return t.name if t is not None else x.name


class _Op:
    __slots__ = ("eng", "fn", "deps", "sig", "dma", "waits")


class Prog:
    N_DMA_SEMS = {"sp": 16, "pool": 12, "act": 4}

    def __init__(self, nc):
        self.nc = nc
        self.es = ExitStack()
        self.sem = {e: self.es.enter_context(nc.semaphore("s_" + e)) for e in ENGS if e != "sp"}
        self.sem["sp"] = self.es.enter_context(nc.semaphore("s_sp"))
        self.cnt = {e: 0 for e in ENGS}
        self.dsem = {q: [self.es.enter_context(nc.semaphore("d_%s%d" % (q, i))) for i in range(n)]
                     for q, n in self.N_DMA_SEMS.items()}
        self.dcnt = {q: [0] * n for q, n in self.N_DMA_SEMS.items()}
        self.dlast = {q: [None] * n for q, n in self.N_DMA_SEMS.items()}
        self.drr = {q: 0 for q in self.N_DMA_SEMS}
        self.known = {e: {} for e in ENGS}
        self.res_w = {}
        self.res_r = {}
        self.ops = []
        self.phase_es = None
        self.uid = 0
        self.last_sig = {e: None for e in ENGS}
        self.n_inst = 0

    def begin_phase(self):
        self.phase_es = ExitStack()
        self.nph = getattr(self, "nph", 0) + 1
        self.sem["pe"] = self.es.enter_context(self.nc.semaphore("s_pe%d" % self.nph))
        self.cnt["pe"] = 0

    def sb(self, name, shape, dtype):
        self.uid += 1
        return self.phase_es.enter_context(self.nc.sbuf_tensor("%s_%d" % (name, self.uid), list(shape), dtype))

    def ps(self, name, shape, dtype):
        self.uid += 1
        return self.phase_es.enter_context(self.nc.psum_tensor("%s_%d" % (name, self.uid), list(shape), dtype))

    def _deps(self, r, w):
        deps = []
        for x in r:
            k = _key(x)
            s = self.res_w.get(k)
            if s is not None:
                deps.append(s)
        for x in w:
            k = _key(x)
            s = self.res_w.get(k)
            if s is not None:
                deps.append(s)
            deps.extend(self.res_r.get(k, ()))
        return deps

    def _commit(self, sig, r, w):
        for x in r:
            self.res_r.setdefault(_key(x), []).append(sig)
        for x in w:
            k = _key(x)
            self.res_w[k] = sig
            self.res_r[k] = []

    def I(self, eng, fn, w=(), r=()):
        deps = self._deps(r, w)
        if eng == "pe":
            deps = [d for d in deps if d[0] is not self.sem["pe"]]
        self.cnt[eng] += 1
        sig = (self.sem[eng], self.cnt[eng])
        op = _Op()
        op.eng, op.fn, op.sig, op.dma = eng, fn, sig, False
        op.waits = self._waits(eng, deps)
        self.ops.append(op)
        self._commit(sig, r, w)
        self.last_sig[eng] = sig
        return sig

    def D(self, q, out, in_, w=(), r=(), **kw):
        return self.Dfn(q, lambda e: e.dma_start(out=out, in_=in_, **kw), w=w, r=r)

    def Dfn(self, q, fn, w=(), r=()):
        deps = self._deps(r, w)
        i = self.drr[q]
        self.drr[q] = (i + 1) % len(self.dsem[q])
        if self.dlast[q][i] is not None:
            deps.append(self.dlast[q][i])
        self.dcnt[q][i] += 1
        sig = (self.dsem[q][i], 16 * self.dcnt[q][i])
        self.dlast[q][i] = sig
        op = _Op()
        op.eng, op.sig, op.dma = q, sig, True
        op.fn = fn
        op.waits = self._waits(q, deps)
        self.ops.append(op)
        self._commit(sig, r, w)
        return sig

    def _waits(self, eng, deps):
        best = {}
        for (s, v) in deps:
            if best.get(id(s), (None, 0))[1] < v:
                best[id(s)] = (s, v)
        out = []
        kn = self.known[eng]
        for sid, (s, v) in best.items():
            if kn.get(sid, 0) >= v:
                continue
            kn[sid] = v
            out.append((s, v))
        return out

    def barrier(self):
        sigs = [s for s in self.last_sig.values() if s is not None]
        for q in self.dlast:
            sigs.extend(s for s in self.dlast[q] if s is not None)
        for e in ENGS:
            ws = self._waits(e, sigs)
            if ws:
                op = _Op()
                op.eng, op.fn, op.sig, op.dma, op.waits = e, None, None, False, ws
                self.ops.append(op)

    def wait_all(self, eng):
        sigs = [s for s in self.last_sig.values() if s is not None]
        for q in self.dlast:
            sigs.extend(s for s in self.dlast[q] if s is not None)
        ws = self._waits(eng, sigs)
        if ws:
            op = _Op()
            op.eng, op.fn, op.sig, op.dma, op.waits = eng, None, None, False, ws
            self.ops.append(op)

    def end_phase(self, final=False):
        self.barrier()
        if final:
            self.wait_all("sp")
        ops = self.ops
        self.ops = []
        per = {e: [o for o in ops if o.eng == e] for e in ENGS}
        self.n_inst += len(ops)
        if not hasattr(self, "remap"):
            self.remap = {}
            self.newcnt = {}
        eng_ids = set(id(x) for x in self.sem.values())
        targets = set()
        for o in ops:
            for (sm, v) in o.waits:
                if id(sm) in eng_ids:
                    targets.add((id(sm), v))
        for o in ops:
            if o.fn is None or o.dma:
                continue
            sm, v = o.sig
            if (id(sm), v) in targets:
                self.newcnt[id(sm)] = self.newcnt.get(id(sm), 0) + 1
                self.remap[(id(sm), v)] = self.newcnt[id(sm)]
        remap = self.remap

        def replay(e, lst):
            for o in lst:
                for (sm, v) in o.waits:
                    e.wait_ge(sm, remap.get((id(sm), v), v) if id(sm) in eng_ids else v)
                if o.fn is not None:
                    ins = o.fn(e)
                    if o.dma:
                        ins.then_inc(o.sig[0], 16)
                    elif (id(o.sig[0]), o.sig[1]) in remap:
                        ins.then_inc(o.sig[0], 1)

        with self.nc.Block() as block:
            if per["pe"]:
                block.tensor(lambda e: replay(e, per["pe"]))
            if per["act"]:
                block.scalar(lambda e: replay(e, per["act"]))
            if per["dve"]:
                block.vector(lambda e: replay(e, per["dve"]))
            if per["pool"]:
                block.gpsimd(lambda e: replay(e, per["pool"]))
            if per["sp"]:
                block.sync(lambda e: replay(e, per["sp"]))
        self.phase_es.close()
        self.phase_es = None
        self.res_w = {k: v for k, v in self.res_w.items() if isinstance(k, tuple)}
        self.res_r = {k: v for k, v in self.res_r.items() if isinstance(k, tuple)}

    def close(self):
        self.es.close()

PI = float(np.pi)
EPS = 1e-6


class Ctx:
    pass


def _alt(P, name, shape, dtype, n=2):
    return [P.sb("%s%d" % (name, i), shape, dtype) for i in range(n)]


def _palt(P, name, shape, dtype, n=2):
    return [P.ps("%s%d" % (name, i), shape, dtype) for i in range(n)]


def load_consts(P, C, names):
    out = {}
    for nm in names:
        off, w = C.cst_off[nm]
        t = P.sb("c_" + nm, [128, w], F32)
        P.D("sp", t[:], C.cst[:, off:off + w], w=[t], allow_slow_non_contiguous=True)
        out[nm] = t
    return out


def make_ident(P, dtype=BF16):
    idb = P.sb("ident", [128, 128], dtype)
    P.I("dve", lambda e: e.memset(idb[:], 0.0), w=[idb])
    P.I("pool", lambda e: e.affine_select(out=idb[:], in_=idb[:], pattern=[[-1, 128]], compare_op=ALU.not_equal,
                                          fill=1.0, base=0, channel_multiplier=1), w=[idb], r=[idb])
    return idb


def make_ones(P):
    o = P.sb("ones", [128, 128], BF16)
    P.I("dve", lambda e: e.memset(o[:], 1.0), w=[o])
    return o


def load_gcol(P, C, li, i):
    g = P.sb("gcol", [128, 8], F32)
    o = (li * 4 + i) * 8
    P.D("sp", g[:], C.gcol[:, o:o + 8], w=[g])
    return g


class Norm:
    def __init__(self, P, nk=8, G=512):
        self.P = P
        self.ones = make_ones(P)
        self.sq = P.sb("nsq", [128, nk, G], BF16)
        self.ps = P.ps("nps", [128, G], F32)
        self.rstd = P.sb("nrstd", [128, G], F32)
        self.nk = nk

    def stats_a(self, x):
        self.P.I("act", lambda e: e.activation(out=self.sq[:], in_=x[:], func=AF.Square), w=[self.sq], r=[x])

    def stats(self, x, d_model, skip_a=False):
        P = self.P
        nk = self.nk
        if not skip_a:
            self.stats_a(x)
        for kc in range(nk):
            P.I("pe", lambda e, kc=kc: e.matmul(self.ps[:], self.ones[:], self.sq[:, kc, :], start=(kc == 0), stop=(kc == nk - 1)),
                w=[self.ps], r=[self.sq, self.ones])
        P.I("act", lambda e: e.activation(out=self.rstd[:], in_=self.ps[:], func=AF.Sqrt, scale=1.0 / d_model, bias=EPS),
            w=[self.rstd], r=[self.ps])
        P.I("dve", lambda e: e.reciprocal(out=self.rstd[:], in_=self.rstd[:]), w=[self.rstd], r=[self.rstd])

    def pre(self, hT, gcol, uT, skip_a=False):
        P = self.P
        self.stats(hT, 1024.0, skip_a)
        for kc in range(8):
            P.I("dve", lambda e, kc=kc: e.scalar_tensor_tensor(out=uT[:, kc, :], in0=hT[:, kc, :], scalar=gcol[:, kc:kc + 1],
                                                              in1=self.rstd[:], op0=ALU.mult, op1=ALU.mult),
                w=[uT], r=[hT, gcol, self.rstd])

    def post(self, yo, gcol, hT, skip_a=False):
        P = self.P
        self.stats(yo, 1024.0, skip_a)
        for kc in range(8):
            P.I("dve", lambda e, kc=kc: e.scalar_tensor_tensor(out=yo[:, kc, :], in0=yo[:, kc, :], scalar=gcol[:, kc:kc + 1],
                                                              in1=self.rstd[:], op0=ALU.mult, op1=ALU.mult),
                w=[yo], r=[yo, gcol, self.rstd])
        P.I("pool", lambda e: e.tensor_tensor(out=hT[:], in0=hT[:], in1=yo[:], op=ALU.add), w=[hT], r=[hT, yo])


def hview(ap, g, G=512):
    return ap[:, :, g * G:(g + 1) * G].rearrange("k p t -> p k t")


def load_w(P, W, src, nk, ncols, blk=512, key="W"):
    for cb in range(ncols // blk):
        for k0 in range(0, nk, 8):
            k1 = min(nk, k0 + 8)
            P.D("pool", W[:, k0:k1, cb * blk:(cb + 1) * blk],
                src[k0 * 128:k1 * 128, cb * blk:(cb + 1) * blk].rearrange("(kc p) n -> p kc n", p=128),
                w=[(key, cb)])


def phase_outproj(P, C, li, gi, w_src, nk, y_src, h_src, h_dst, bias_src=None):
    P.begin_phase()
    W = P.sb("W", [128, nk, 1024], BF16)
    load_w(P, W, w_src, nk, 1024)
    gcol = load_gcol(P, C, li, gi)
    nrm = Norm(P)
    bcol = None
    if bias_src is not None:
        bcol = P.sb("bcol", [128, 8], F32)
        P.D("sp", bcol[:], bias_src, w=[bcol])
    yT = _alt(P, "yT", [128, nk, 512], BF16)
    hT = _alt(P, "hT", [128, 8, 512], F32)
    yos = _alt(P, "yo", [128, 8, 512], F32)
    pso = _palt(P, "pso", [128, 512], F32, 3)

    def finish_a(g):
        nrm.stats_a(yos[g % 2])

    def finish(g):
        nrm.post(yos[g % 2], gcol, hT[g % 2], skip_a=True)
        P.D("sp", hview(h_dst, g), hT[g % 2][:], r=[hT[g % 2]], w=[("h", g)])

    def load_y(g):
        P.D("sp", yT[g % 2][:], y_src[:, :, g * 512:(g + 1) * 512].rearrange("k p t -> p k t"), w=[yT[g % 2]], r=[("ysrc", g)])

    load_y(0)
    for g in range(C.NG + 1):
        if g < C.NG:
            y = yT[g % 2]
            h = hT[g % 2]
            yo = yos[g % 2]
            if g + 1 < C.NG:
                load_y(g + 1)
            P.D("sp", h[:], hview(h_src, g), w=[h], r=[("h", g)])
            for oc in range(8):
                ps = pso[oc % 3]
                for kc in range(nk):
                    P.I("pe", lambda e, ps=ps, kc=kc, oc=oc, y=y: e.matmul(ps[:], W[:, kc, oc * 128:(oc + 1) * 128], y[:, kc, :],
                                                                         start=(kc == 0), stop=(kc == nk - 1)),
                        w=[ps], r=[y, ("W", oc // 4)])
                if bcol is None:
                    P.I("act", lambda e, ps=ps, oc=oc, yo=yo: e.activation(out=yo[:, oc, :], in_=ps[:], func=AF.Copy), w=[yo], r=[ps])
                else:
                    P.I("act", lambda e, ps=ps, oc=oc, yo=yo: e.activation(out=yo[:, oc, :], in_=ps[:], func=AF.Identity,
                                                                        bias=bcol[:, oc:oc + 1]), w=[yo], r=[ps, bcol])
                if oc == 0 and g > 0:
                    finish_a(g - 1)
                if oc == 3 and g > 0:
                    finish(g - 1)
        else:
            finish_a(g - 1)
            finish(g - 1)
    P.end_phase()


def phase_mlp_up(P, C, li, h_src):
    P.begin_phase()
    W = P.sb("W", [128, 8, 4096], BF16)
    load_w(P, W, C.mlp_w_up[li], 8, 4096)
    gcol = load_gcol(P, C, li, 2)
    nrm = Norm(P)
    hT = _alt(P, "hT", [128, 8, 512], F32)
    uT = _alt(P, "uT", [128, 8, 512], BF16)
    hid = _alt(P, "hid", [128, 32, 512], BF16)
    psu = _palt(P, "psu", [128, 512], F32, 4)
    rtmp = _alt(P, "rtmp", [128, 512], F32)
    def prep_l(g):
        P.D("sp", hT[g % 2][:], hview(h_src, g), w=[hT[g % 2]], r=[("h", g)])

    def prep_a(g):
        nrm.stats_a(hT[g % 2])

    def prep_b(g):
        nrm.pre(hT[g % 2], gcol, uT[g % 2], skip_a=True)

    prep_l(0)
    prep_a(0)
    prep_b(0)
    for g in range(C.NG):
        u = uT[g % 2]
        hd = hid[g % 2]
        for hc in range(32):
            if hc == 0 and g + 1 < C.NG:
                prep_l(g + 1)
            if hc == 16 and g + 1 < C.NG:
                prep_a(g + 1)
            if hc == 22 and g + 1 < C.NG:
                prep_b(g + 1)
            ps = psu[hc % 4]
            for kc in range(8):
                P.I("pe", lambda e, ps=ps, kc=kc, hc=hc, u=u: e.matmul(ps[:], W[:, kc, hc * 128:(hc + 1) * 128], u[:, kc, :],
                                                                     start=(kc == 0), stop=(kc == 7)),
                    w=[ps], r=[u, ("W", hc // 4)])
            rt = rtmp[hc % 2]
            P.I("act", lambda e, ps=ps, rt=rt: e.activation(out=rt[:], in_=ps[:], func=AF.Relu), w=[rt], r=[ps])
            ve = "dve" if hc % 2 == 0 else "pool"
            P.I(ve, lambda e, rt=rt, hc=hc, hd=hd: e.tensor_tensor(out=hd[:, hc, :], in0=rt[:], in1=rt[:], op=ALU.mult),
                w=[(hd.name, hc)], r=[rt])
        P.D("sp", C.hid[:, :, g * 512:(g + 1) * 512].rearrange("k p t -> p k t"), hd[:],
            r=[(hd.name, hc) for hc in range(32)], w=[("ysrc", g)])
    P.end_phase()


def phase_mlp_down(P, C, li, h_src, h_dst):
    phase_outproj(P, C, li, 3, C.mlp_w_down[li], 32, C.hid, h_src, h_dst)


def ret_lg(P, C, j):
    dl = P.sb("dl", [128, 8], F32)
    lg = P.sb("lg", [128, 8], F32)
    P.D("sp", dl[:], C.ret_dl[j], w=[dl])
    P.I("act", lambda e: e.activation(out=lg[:], in_=dl[:], func=AF.Exp, scale=-1.0), w=[lg], r=[dl])
    P.I("act", lambda e: e.activation(out=lg[:], in_=lg[:], func=AF.Ln, bias=1.0), w=[lg], r=[lg])
    P.I("dve", lambda e: e.tensor_scalar(out=lg[:], in0=lg[:], scalar1=-1.0, scalar2=None, op0=ALU.mult), w=[lg], r=[lg])
    return lg


def rope_tables(P, C, g, K, cos, sin):
    pi_, pf, ang, ki, kf = K["pi"], K["pf"], K["ang"], K["ki"], K["kf"]
    P.D("sp", pi_[:], C.posr[:, g * 512:(g + 1) * 512], w=[pi_])
    P.I("dve", lambda e: e.tensor_copy(out=pf[:], in_=pi_[:]), w=[pf], r=[pi_])
    P.I("dve", lambda e: e.tensor_scalar(out=ang[:], in0=pf[:], scalar1=K["invf"][:, 0:1], scalar2=None, op0=ALU.mult),
        w=[ang], r=[pf, K["invf"]])
    P.I("dve", lambda e: e.tensor_scalar(out=ki[:], in0=ang[:], scalar1=float(1 / (2 * np.pi)), scalar2=None, op0=ALU.mult),
        w=[ki], r=[ang])
    P.I("dve", lambda e: e.tensor_copy(out=kf[:], in_=ki[:]), w=[kf], r=[ki])
    P.I("dve", lambda e: e.scalar_tensor_tensor(out=ang[:], in0=kf[:], scalar=float(-2 * np.pi), in1=ang[:], op0=ALU.mult, op1=ALU.add),
        w=[ang], r=[ang, kf])
    for (dst, shift) in ((sin, 0.0), (cos, PI / 2)):
        if shift != 0.0:
            P.I("dve", lambda e, dst=dst, shift=shift: e.tensor_scalar(out=dst[:], in0=ang[:], scalar1=shift, scalar2=None, op0=ALU.add),
                w=[dst], r=[ang])
            src = dst
        else:
            src = ang
        P.I("dve", lambda e, src=src: e.tensor_scalar(out=kf[:], in0=src[:], scalar1=PI, scalar2=float(2 * np.pi), op0=ALU.is_gt, op1=ALU.mult),
            w=[kf], r=[src])
        P.I("dve", lambda e, src=src, dst=dst: e.tensor_tensor(out=dst[:], in0=src[:], in1=kf[:], op=ALU.subtract), w=[dst], r=[src, kf])
        P.I("act", lambda e, dst=dst: e.activation(out=dst[:], in_=dst[:], func=AF.Sin), w=[dst], r=[dst])


def phase_ret_qk(P, C, li, j, h_src):
    P.begin_phase()
    W = P.sb("W", [128, 8, 2048], BF16)
    load_w(P, W, C.ret_w_in[j][:, 0:2048], 8, 2048)
    gcol = load_gcol(P, C, li, 0)
    nrm = Norm(P)
    idb = make_ident(P)
    cs = load_consts(P, C, ["EF", "EB", "c127", "cs", "invf"])
    lg = ret_lg(P, C, j)
    aF = P.sb("aF", [128, 4, 512], F32)
    aB = P.sb("aB", [128, 4, 512], F32)
    colE = P.sb("colE", [128, 8], F32)
    for h in range(4):
        P.I("act", lambda e, h=h: e.activation(out=aF[:, h, :], in_=cs["EF"][:], func=AF.Exp, scale=lg[:, h:h + 1]), w=[aF], r=[cs["EF"], lg])
        P.I("act", lambda e, h=h: e.activation(out=aB[:, h, :], in_=cs["EB"][:], func=AF.Exp, scale=lg[:, 4 + h:5 + h]), w=[aB], r=[cs["EB"], lg])
        P.I("act", lambda e, h=h: e.activation(out=colE[:, h:h + 1], in_=cs["c127"][:], func=AF.Exp, scale=lg[:, h:h + 1]), w=[colE], r=[cs["c127"], lg])
        P.I("act", lambda e, h=h: e.activation(out=colE[:, 4 + h:5 + h], in_=cs["cs"][:], func=AF.Exp, scale=lg[:, 4 + h:5 + h]), w=[colE], r=[cs["cs"], lg])
    P.I("dve", lambda e: e.tensor_scalar(out=aF[:], in0=aF[:], scalar1=0.0625, scalar2=None, op0=ALU.mult), w=[aF], r=[aF])
    P.I("dve", lambda e: e.tensor_scalar(out=aB[:], in0=aB[:], scalar1=0.0625, scalar2=None, op0=ALU.mult), w=[aB], r=[aB])
    K = {"pi": P.sb("pi", [128, 512], I32), "pf": P.sb("pf", [128, 512], F32), "ang": P.sb("ang", [128, 512], F32),
         "ki": P.sb("ki", [128, 512], I32), "kf": P.sb("kf", [128, 512], F32), "invf": cs["invf"]}
    cos = P.sb("cos", [128, 512], F32)
    sin = P.sb("sin", [128, 512], F32)
    hT = _alt(P, "hT", [128, 8, 512], F32)
    uT = _alt(P, "uT", [128, 8, 512], BF16)
    psq = _palt(P, "psq", [128, 2, 512], F32, 2)
    pst = _palt(P, "pst", [128, 256], BF16, 2)
    t12 = _alt(P, "t12", [128, 2, 512], F32)
    tmp = _alt(P, "tmp", [128, 4, 512], F32)
    rr = _alt(P, "rr", [128, 2, 512], F32)
    kb = _alt(P, "kb", [128, 2, 512], BF16)
    qo = _alt(P, "qo", [128, 4, 512], BF16)
    kE = _alt(P, "kE", [128, 2, 4, 256], BF16)
    it = 0
    def prep_l(g):
        P.D("sp", hT[g % 2][:], hview(h_src, g), w=[hT[g % 2]], r=[("h", g)])

    def prep_a(g):
        nrm.stats_a(hT[g % 2])

    def prep_b(g):
        nrm.pre(hT[g % 2], gcol, uT[g % 2], skip_a=True)

    prep_l(0)
    prep_a(0)
    prep_b(0)
    for g in range(C.NG):
        u = uT[g % 2]
        rope_tables(P, C, g, K, cos, sin)
        for h in range(4):
            for qk in range(2):
                if h == 0 and qk == 0 and g + 1 < C.NG:
                    prep_l(g + 1)
                if h == 2 and qk == 0 and g + 1 < C.NG:
                    prep_a(g + 1)
                if h == 3 and qk == 0 and g + 1 < C.NG:
                    prep_b(g + 1)
                base = qk * 1024 + h * 256
                ps = psq[it % 2]
                t = t12[it % 2]
                tm = tmp[it % 2]
                r_ = rr[it % 2]
                ve = "pool" if it % 3 == 2 else "dve"
                it += 1
                for c in range(2):
                    for kc in range(8):
                        P.I("pe", lambda e, ps=ps, c=c, kc=kc, base=base, u=u: e.matmul(ps[:, c, :], W[:, kc, base + c * 128:base + (c + 1) * 128],
                                                                                     u[:, kc, :], start=(kc == 0), stop=(kc == 7)),
                            w=[ps], r=[u, ("W", base // 512)])
                P.I("act", lambda e, ps=ps, t=t: e.activation(out=t[:], in_=ps[:], func=AF.Copy), w=[t], r=[ps])
                P.I(ve, lambda e, t=t, tm=tm: e.tensor_tensor(out=tm[:, 0, :], in0=t[:, 0, :], in1=cos[:], op=ALU.mult), w=[tm], r=[t, cos])
                P.I(ve, lambda e, t=t, tm=tm: e.tensor_tensor(out=tm[:, 1, :], in0=t[:, 1, :], in1=sin[:], op=ALU.mult), w=[tm], r=[t, sin])
                P.I(ve, lambda e, t=t, tm=tm: e.tensor_tensor(out=tm[:, 2, :], in0=t[:, 1, :], in1=cos[:], op=ALU.mult), w=[tm], r=[t, cos])
                P.I(ve, lambda e, t=t, tm=tm: e.tensor_tensor(out=tm[:, 3, :], in0=t[:, 0, :], in1=sin[:], op=ALU.mult), w=[tm], r=[t, sin])
                if qk == 0:
                    q = qo[h % 2]
                    P.I(ve, lambda e, tm=tm, r_=r_: e.tensor_tensor(out=r_[:, 0, :], in0=tm[:, 0, :], in1=tm[:, 1, :], op=ALU.subtract), w=[r_], r=[tm])
                    P.I(ve, lambda e, tm=tm, r_=r_: e.tensor_tensor(out=r_[:, 1, :], in0=tm[:, 2, :], in1=tm[:, 3, :], op=ALU.add), w=[r_], r=[tm])
                    for c in range(2):
                        P.I(ve, lambda e, r_=r_, q=q, c=c, h=h: e.tensor_tensor(out=q[:, c, :], in0=r_[:, c, :], in1=aF[:, h, :], op=ALU.mult), w=[q], r=[r_, aF])
                        P.I(ve, lambda e, r_=r_, q=q, c=c, h=h: e.tensor_tensor(out=q[:, 2 + c, :], in0=r_[:, c, :], in1=aB[:, h, :], op=ALU.mult), w=[q], r=[r_, aB])
                    P.D("sp", C.qF[h * 2:h * 2 + 2, :, g * 512:(g + 1) * 512].rearrange("c p t -> p c t"), q[:, 0:2, :], r=[q], w=[("qF", g, h)])
                    P.D("sp", C.qB[h * 2:h * 2 + 2, :, g * 512:(g + 1) * 512].rearrange("c p t -> p c t"), q[:, 2:4, :], r=[q], w=[("qB", g, h)])
                else:
                    k = kb[h % 2]
                    ke = kE[h % 2]
                    P.I(ve, lambda e, tm=tm, k=k: e.tensor_tensor(out=k[:, 0, :], in0=tm[:, 0, :], in1=tm[:, 1, :], op=ALU.subtract), w=[k], r=[tm])
                    P.I(ve, lambda e, tm=tm, k=k: e.tensor_tensor(out=k[:, 1, :], in0=tm[:, 2, :], in1=tm[:, 3, :], op=ALU.add), w=[k], r=[tm])
                    P.D("sp", C.kT[h * 2:h * 2 + 2, :, g * 512:(g + 1) * 512].rearrange("c p t -> p c t"), k[:], r=[k], w=[("kT", g, h)])
                    for tt in range(4):
                        pt = pst[tt % 2]
                        for c in range(2):
                            P.I("pe", lambda e, pt=pt, k=k, c=c, tt=tt: e.transpose(out=pt[:, c * 128:(c + 1) * 128], in_=k[:, c, tt * 128:(tt + 1) * 128],
                                                                                 identity=idb[:]), w=[pt], r=[k, idb])
                        P.I("act", lambda e, pt=pt, ke=ke, tt=tt, h=h: e.activation(out=ke[:, 0, tt, :], in_=pt[:], func=AF.Copy, scale=colE[:, h:h + 1]),
                            w=[ke], r=[pt, colE])
                        P.I("act", lambda e, pt=pt, ke=ke, tt=tt, h=h: e.activation(out=ke[:, 1, tt, :], in_=pt[:], func=AF.Copy, scale=colE[:, 4 + h:5 + h]),
                            w=[ke], r=[pt, colE])
                    rows = slice(g * 512, (g + 1) * 512)
                    P.D("sp", C.kEF[rows, h * 256:(h + 1) * 256].rearrange("(tt p) n -> p tt n", p=128), ke[:, 0, :, :], r=[ke], w=[("kEF", g, h)])
                    P.D("sp", C.kEB[rows, h * 256:(h + 1) * 256].rearrange("(tt p) n -> p tt n", p=128), ke[:, 1, :, :], r=[ke], w=[("kEB", g, h)])
    P.end_phase()


def phase_vg(P, C, li, h_src, w_src, ncols_v, v_dst, g_dst, G=None):
    P.begin_phase()
    nc_ = 2 * ncols_v
    W = P.sb("W", [128, 8, nc_], BF16)
    load_w(P, W, w_src, 8, nc_)
    gcol = load_gcol(P, C, li, 0)
    nrm = Norm(P)
    hT = _alt(P, "hT", [128, 8, 512], F32)
    uT = _alt(P, "uT", [128, 8, 512], BF16)
    vo = _alt(P, "vo", [128, nc_], BF16)
    psv = _palt(P, "psv", [128, 512], F32, 4)
    it = 0
    def prep_l(g):
        P.D("sp", hT[g % 2][:], hview(h_src, g), w=[hT[g % 2]], r=[("h", g)])

    def prep_a(g):
        nrm.stats_a(hT[g % 2])

    def prep_b(g):
        nrm.pre(hT[g % 2], gcol, uT[g % 2], skip_a=True)

    prep_l(0)
    prep_a(0)
    prep_b(0)
    for g in range(C.NG):
        u = uT[g % 2]
        for tt in range(4):
            if tt == 0 and g + 1 < C.NG:
                prep_l(g + 1)
            if tt == 2 and g + 1 < C.NG:
                prep_a(g + 1)
            if tt == 3 and g + 1 < C.NG:
                prep_b(g + 1)
            o = vo[tt % 2]
            for cb in range(nc_ // 512):
                ps = psv[it % 4]
                it += 1
                for kc in range(8):
                    P.I("pe", lambda e, ps=ps, kc=kc, tt=tt, cb=cb, u=u: e.matmul(ps[:], u[:, kc, tt * 128:(tt + 1) * 128], W[:, kc, cb * 512:(cb + 1) * 512],
                                                                               start=(kc == 0), stop=(kc == 7)), w=[ps], r=[u, ("W", cb)])
                fn = AF.Copy if cb * 512 < ncols_v else AF.Silu
                P.I("act", lambda e, ps=ps, o=o, cb=cb, fn=fn: e.activation(out=o[:, cb * 512:(cb + 1) * 512], in_=ps[:], func=fn), w=[o], r=[ps])
            rows = slice(g * 512 + tt * 128, g * 512 + (tt + 1) * 128)
            P.D("sp", v_dst[rows, :], o[:, 0:ncols_v], r=[o], w=[("v", g, tt)])
            P.D("sp", g_dst[rows, :], o[:, ncols_v:nc_], r=[o], w=[("sg", g, tt)])
    P.end_phase()


def exchange_states(P, C, S, nhc, DV, xin_t=None, xout_t=None):
    rows = nhc * 128
    xin_t = C.xin if xin_t is None else xin_t
    xout_t = C.xout if xout_t is None else xout_t
    P.D("sp", xin_t.rearrange("(k p) n -> p k n", p=128), S[:, 0:nhc, 0:DV],
        r=[("S", i) for i in range(nhc)], w=[("xin",)])
    xin = xin_t
    xout = xout_t
    P.Dfn('pool', lambda e: e.collective_compute("AllGather", ALU.bypass,
                                          replica_groups=[[0, 1], [2, 3], [4, 5], [6, 7]],
                                          ins=[xin], outs=[xout]), r=[("xin",)], w=[("xout",)])


def sweep(P, C, cfg, second):
    H, NC, DV = cfg["H"], cfg["NC"], cfg["DV"]
    HC = H * NC
    NG = C.NG
    pairs = cfg["pairs"] if not second else []
    npair = len(pairs)
    S = P.sb("S", [128, HC, DV], F32)
    Sb = P.sb("Sb", [128, HC, DV], BF16)
    idb = cfg["idb"]
    P.I("dve", lambda e: e.memset(S[:], 0.0), w=[("S", i) for i in range(HC)])
    P.I("pool", lambda e: e.memset(Sb[:], 0.0), w=[("Sb", i) for i in range(HC)])
    Qg = _alt(P, "Qg", [128, HC, 512], BF16)
    Kg = [_alt(P, "Kg%d" % i, [128, HC, 512], BF16) for i in range(npair)]
    Q2g = [_alt(P, "Q2g%d" % i, [128, HC, 512], BF16) for i in range(1, npair)]
    KEg = _alt(P, "KEg", [128, 4, HC * 128], BF16)
    Vg = _alt(P, "Vg", [128, 4, H * DV], BF16)
    if second:
        P1g = _alt(P, "P1g", [128, 4, H * DV], BF16)
    else:
        P1t = _alt(P, "P1t", [128, H * DV], BF16)
    psS = _palt(P, "psS", [128, 128], F32, 2) if not second else None
    psO = _palt(P, "psO", [128, DV], F32, 2)
    psU = _palt(P, "psU", [128, DV], F32, 4)
    scm = _alt(P, "scm", [128, 128], BF16)
    sct = _alt(P, "sct", [128, 128], F32)
    sct2 = _alt(P, "sct2", [128, 128], F32)
    order = list(range(NG)) if not second else list(reversed(range(NG)))

    def loads(gi):
        g = order[gi]
        b = gi % 2
        tsl = slice(g * 512, (g + 1) * 512)
        P.D("sp", Qg[b][:], cfg["Q"][:, :, tsl].rearrange("k p t -> p k t"), w=[Qg[b]], r=[("q",)])
        for i, (K_ap, Q_ap, mk) in enumerate(pairs):
            P.D("sp", Kg[i][b][:], K_ap[:, :, tsl].rearrange("k p t -> p k t"), w=[Kg[i][b]])
            if i > 0:
                P.D("sp", Q2g[i - 1][b][:], Q_ap[:, :, tsl].rearrange("k p t -> p k t"), w=[Q2g[i - 1][b]])
        P.D("sp", KEg[b][:], cfg["KE"][tsl, :].rearrange("(tt p) n -> p tt n", p=128), w=[KEg[b]])
        P.D("sp", Vg[b][:], cfg["V"][tsl, :].rearrange("(tt p) n -> p tt n", p=128), w=[Vg[b]])
        if second:
            P.D("sp", P1g[b][:], cfg["P1"][tsl, :].rearrange("(tt p) n -> p tt n", p=128), w=[P1g[b]])

    steps = []
    for gi, g in enumerate(order):
        tts = list(range(4)) if not second else [3, 2, 1, 0]
        for tt in tts:
            for h in range(H):
                steps.append((gi, g, tt, h))

    def stageA(i):
        gi, g, tt, h = steps[i]
        b = gi % 2
        csl = slice(tt * 128, (tt + 1) * 128)
        sm = scm[i % 2]
        for pi_, (K_ap, Q_ap, mk) in enumerate(pairs):
            pS = psS[pi_ % 2] if npair > 1 else psS[i % 2]
            qq = Qg[b] if pi_ == 0 else Q2g[pi_ - 1][b]
            for c in range(NC):
                P.I("pe", lambda e, pS=pS, kk=Kg[pi_][b], qq=qq, hc=h * NC + c, c=c, csl=csl:
                    e.matmul(pS[:], kk[:, hc, csl], qq[:, hc, csl], start=(c == 0), stop=(c == NC - 1)), w=[pS], r=[Kg[pi_][b], qq])
            mt = cfg["M"][h] if mk is None else cfg["masks"][mk]
            if pi_ == 0 and npair == 1:
                P.I("dve", lambda e, pS=pS, sm=sm, mt=mt: e.tensor_tensor(out=sm[:], in0=pS[:], in1=mt[:], op=ALU.mult), w=[sm], r=[pS, mt])
            elif pi_ == 0:
                st_ = sct[i % 2]
                P.I("dve", lambda e, pS=pS, st_=st_, mt=mt: e.tensor_tensor(out=st_[:], in0=pS[:], in1=mt[:], op=ALU.mult), w=[st_], r=[pS, mt])
            else:
                st_ = sct[i % 2]
                st2 = sct2[i % 2]
                P.I("dve", lambda e, pS=pS, st2=st2, mt=mt: e.tensor_tensor(out=st2[:], in0=pS[:], in1=mt[:], op=ALU.mult), w=[st2], r=[pS, mt])
                P.I("pool", lambda e, st_=st_, st2=st2, sm=sm: e.tensor_tensor(out=sm[:], in0=st_[:], in1=st2[:], op=ALU.add), w=[sm], r=[st_, st2])

    pending = []
    loads(0)
    if not second:
        stageA(0)
    for i, (gi, g, tt, h) in enumerate(steps):
        b = gi % 2
        n = g * 4 + tt
        csl = slice(tt * 128, (tt + 1) * 128)
        first_of_group = (i % (4 * H) == 0)
        last_of_group = (i % (4 * H) == 4 * H - 1)
        if first_of_group:
            if gi + 1 < NG:
                loads(gi + 1)
            if second:
                cfg["epi_load"](g, b)
        if not second and i + 1 < len(steps):
            stageA(i + 1)
        po = psO[i % 2]
        vsl = Vg[b][:, tt, h * DV:(h + 1) * DV]
        if not second:
            sm = scm[i % 2]
            P.I("pe", lambda e, po=po, sm=sm, vsl=vsl: e.matmul(po[:], sm[:], vsl, start=True, stop=False), w=[po], r=[sm, Vg[b]])
        else:
            P.I("pe", lambda e, po=po, b=b, tt=tt, h=h: e.matmul(po[:], idb[:], P1g[b][:, tt, h * DV:(h + 1) * DV], start=True, stop=False),
                w=[po], r=[idb, P1g[b]])
        for c in range(NC):
            hc = h * NC + c
            P.I("pe", lambda e, po=po, b=b, hc=hc, c=c, csl=csl: e.matmul(po[:], Qg[b][:, hc, csl], Sb[:, hc, :], start=False, stop=(c == NC - 1)),
                w=[po], r=[Qg[b], ("Sb", hc)])
        if not second:
            pt = P1t[n % 2]
            P.I("act", lambda e, po=po, pt=pt, h=h: e.activation(out=pt[:, h * DV:(h + 1) * DV], in_=po[:], func=AF.Copy), w=[pt], r=[po])
        else:
            tail = cfg["epi"](g, b, tt, h, po)
            for t_ in pending:
                t_()
            pending = [tail]
        for c in range(NC):
            hc = h * NC + c
            pu = psU[(i * NC + c) % 4]
            P.I("pe", lambda e, pu=pu, b=b, tt=tt, hc=hc, vsl=vsl: e.matmul(pu[:], KEg[b][:, tt, hc * 128:(hc + 1) * 128], vsl, start=True, stop=True),
                w=[pu], r=[KEg[b], Vg[b]])
            dcol = cfg["dec"](h, c, n)
            P.I("dve", lambda e, pu=pu, hc=hc, dcol=dcol: e.scalar_tensor_tensor(out=S[:, hc, :], in0=S[:, hc, :], scalar=dcol, in1=pu[:],
                                                                             op0=ALU.mult, op1=ALU.add),
                w=[("S", hc)], r=[("S", hc), pu, cfg["dec_res"]])
            P.I("pool", lambda e, hc=hc: e.tensor_copy(out=Sb[:, hc, :], in_=S[:, hc, :]), w=[("Sb", hc)], r=[("S", hc)])
        if not second and h == H - 1:
            P.D("sp", cfg["P1"][n * 128:(n + 1) * 128, :], P1t[n % 2][:], r=[P1t[n % 2]], w=[("P1", n)])
        if second and last_of_group:
            for t_ in pending:
                t_()
            pending = []
            cfg["epi_store"](g, b)
    return S


def phase_ret_sweep1(P, C, j):
    P.begin_phase()
    cs = load_consts(P, C, ["maskLF", "maskLB", "A1", "A2", "A3", "c128"])
    lg = ret_lg(P, C, j)
    M = {}
    for h in range(4):
        m = P.sb("M%d" % h, [128, 128], F32)
        t2 = P.sb("Mt%d" % h, [128, 128], F32)
        lf = lg[:, h:h + 1]
        lb = lg[:, 4 + h:5 + h]
        P.I("act", lambda e, m=m, lf=lf: e.activation(out=m[:], in_=cs["A1"][:], func=AF.Exp, scale=lf), w=[m], r=[cs["A1"], lg])
        P.I("dve", lambda e, m=m: e.tensor_tensor(out=m[:], in0=m[:], in1=cs["maskLF"][:], op=ALU.mult), w=[m], r=[m, cs["maskLF"]])
        P.I("dve", lambda e, t2=t2, lb=lb: e.tensor_scalar(out=t2[:], in0=cs["A2"][:], scalar1=lb, scalar2=None, op0=ALU.mult), w=[t2], r=[cs["A2"], lg])
        P.I("dve", lambda e, t2=t2, lf=lf: e.scalar_tensor_tensor(out=t2[:], in0=cs["A3"][:], scalar=lf, in1=t2[:], op0=ALU.mult, op1=ALU.add),
            w=[t2], r=[t2, cs["A3"], lg])
        P.I("act", lambda e, t2=t2: e.activation(out=t2[:], in_=t2[:], func=AF.Exp), w=[t2], r=[t2])
        P.I("dve", lambda e, t2=t2: e.tensor_tensor(out=t2[:], in0=t2[:], in1=cs["maskLB"][:], op=ALU.mult), w=[t2], r=[t2, cs["maskLB"]])
        P.I("dve", lambda e, m=m, t2=t2: e.tensor_tensor(out=m[:], in0=m[:], in1=t2[:], op=ALU.add), w=[m], r=[m, t2])
        M[h] = m
    decc = P.sb("decc", [128, 4], F32)
    for h in range(4):
        P.I("act", lambda e, h=h: e.activation(out=decc[:, h:h + 1], in_=cs["c128"][:], func=AF.Exp, scale=lg[:, h:h + 1]), w=[decc], r=[cs["c128"], lg])
    cfg = dict(H=4, NC=2, DV=512, pairs=[(C.kT, C.qF, None)], Q=C.qF, KE=C.kEF, V=C.v, P1=C.P1,
               dec=lambda h, c, n: decc[:, h:h + 1], dec_res=decc, M=M, idb=None)
    S = sweep(P, C, cfg, False)
    P.end_phase()


def make_epilogue(P, C, H, DV, sg_src, y_dst, gain_bc=None):
    idb = make_ident(P)
    NF = DV // 128
    SGg = P.sb("SGg", [128, 4, H * DV], BF16)
    yTg = P.sb("yTg", [128, H * NF, 512], BF16)
    yt = _alt(P, "yt", [128, DV], BF16)
    yf = _alt(P, "yf", [128, DV], F32)
    ssc = P.sb("ssc", [128, 8], F32)
    junk = P.sb("junk", [128, DV], F32)
    psT = _palt(P, "psT", [128, DV], BF16, 2)
    st = {"k": 0}

    def epi_load(g, b):
        P.D("sp", SGg[:], sg_src[g * 512:(g + 1) * 512, :].rearrange("(tt p) n -> p tt n", p=128), w=[SGg])

    def epi(g, b, tt, h, po):
        k = st["k"]
        st["k"] += 1
        s = ssc[:, k % 8:k % 8 + 1]
        key = ("ssc", k % 8)
        y = yt[k % 2]
        pT = psT[k % 2]
        P.I("dve", lambda e: e.memset(s, 0.0), w=[key])
        P.I("act", lambda e: e.activation(out=junk[:], in_=po[:], func=AF.Square, accum_out=s), w=[junk, key], r=[po])
        P.I("act", lambda e: e.activation(out=s, in_=s, func=AF.Sqrt, scale=1.0 / DV, bias=EPS), w=[key], r=[key])
        P.I("dve", lambda e: e.reciprocal(out=s, in_=s), w=[key], r=[key])
        if gain_bc is None:
            P.I("dve", lambda e: e.scalar_tensor_tensor(out=y[:], in0=po[:], scalar=s, in1=SGg[:, tt, h * DV:(h + 1) * DV], op0=ALU.mult, op1=ALU.mult),
                w=[y], r=[po, key, SGg])
        else:
            f = yf[k % 2]
            P.I("dve", lambda e: e.scalar_tensor_tensor(out=f[:], in0=po[:], scalar=s, in1=gain_bc[:], op0=ALU.mult, op1=ALU.mult),
                w=[f], r=[po, key, gain_bc])
            P.I("pool", lambda e: e.tensor_tensor(out=y[:], in0=f[:], in1=SGg[:, tt, h * DV:(h + 1) * DV], op=ALU.mult), w=[y], r=[f, SGg])
        def tail():
            for q4 in range(NF):
                P.I("pe", lambda e, q4=q4: e.transpose(out=pT[:, q4 * 128:(q4 + 1) * 128], in_=y[:, q4 * 128:(q4 + 1) * 128], identity=idb[:]),
                    w=[pT], r=[y, idb])
            P.I("act", lambda e: e.activation(out=yTg[:, h * NF:(h + 1) * NF, tt * 128:(tt + 1) * 128],
                                              in_=pT[:].rearrange("p (q t) -> p q t", q=NF), func=AF.Copy), w=[yTg], r=[pT])
        return tail

    def epi_store(g, b):
        P.D("sp", y_dst[:, :, g * 512:(g + 1) * 512].rearrange("k p t -> p k t"), yTg[:], r=[yTg], w=[("ysrc", g)])

    return idb, epi_load, epi, epi_store


def phase_ret_sweep2(P, C, j):
    P.begin_phase()
    cs = load_consts(P, C, ["c128", "sel"])
    lg = ret_lg(P, C, j)
    decc = P.sb("decc", [128, 4], F32)
    for h in range(4):
        P.I("act", lambda e, h=h: e.activation(out=decc[:, h:h + 1], in_=cs["c128"][:], func=AF.Exp, scale=lg[:, 4 + h:5 + h]), w=[decc], r=[cs["c128"], lg])
    idb, epi_load, epi, epi_store = make_epilogue(P, C, 4, 512, C.sg, C.yT)
    cfg = dict(H=4, NC=2, DV=512, pairs=[], Q=C.qB, KE=C.kEB, V=C.v, P1=C.P1,
               dec=lambda h, c, n: decc[:, h:h + 1], dec_res=decc, idb=idb, sel=cs["sel"],
               epi_load=epi_load, epi=epi, epi_store=epi_store)
    sweep(P, C, cfg, True)
    P.end_phase()


def layer_ret(P, C, li, j, h_src, h_dst):
    phase_ret_qk(P, C, li, j, h_src)
    phase_vg(P, C, li, h_src, C.ret_w_in[j][:, 2048:6144], 2048, C.v, C.sg)
    phase_ret_sweep1(P, C, j)
    phase_ret_sweep2(P, C, j)
    phase_outproj(P, C, li, 1, C.ret_w_out[j], 16, C.yT, h_src, h_dst)


def layer_mlp(P, C, li, h_src, h_dst):
    phase_mlp_up(P, C, li, h_src)
    phase_mlp_down(P, C, li, h_src, h_dst)


def phase_gla_qk(P, C, li, h_src):
    P.begin_phase()
    NCH = C.NCH
    W = P.sb("W", [128, 8, 1024], BF16)
    load_w(P, W, C.gla_w_in[0][:, 0:1024], 8, 1024)
    w1 = P.sb("w1", [128, 2, 8, 16], BF16)
    for d in range(2):
        P.D("pool", w1[:, d, :, :], C.gla_w1[d].rearrange("(kc p) r -> p kc r", p=128), w=[w1])
    w2a = P.sb("w2a", [17, 2, 512], F32)
    for d in range(2):
        P.D("sp", w2a[0:16, d, :], C.gla_w2[d], w=[w2a])
        P.D("sp", w2a[16:17, d, :], C.gla_b[d:d + 1, :], w=[w2a])
    gcol = load_gcol(P, C, li, 0)
    nrm = Norm(P)
    cs = load_consts(P, C, ["triU", "triL", "sL", "sU"])
    tri = [cs["triU"], cs["triL"]]
    sX = [cs["sL"], cs["sU"]]
    DEC = [P.sb("DEC%d" % d, [128, 4, NCH], F32) for d in range(2)]
    hT = _alt(P, "hT", [128, 8, 512], F32)
    uT = _alt(P, "uT", [128, 8, 512], BF16)
    qs = P.sb("qs", [128, 4, 512], F32)
    ks = P.sb("ks", [128, 4, 512], F32)
    zTa = [P.sb("zTa%d" % d, [17, 512], F32) for d in range(2)]
    for d in range(2):
        P.I("dve", lambda e, d=d: e.memset(zTa[d][:], 1.0), w=[zTa[d]])
    lgt = [P.sb("lgt%d" % d, [128, 4, 512], F32) for d in range(2)]
    QX = [P.sb("QX%d" % d, [128, 4, 512], BF16) for d in range(2)]
    KX = [P.sb("KX%d" % d, [128, 4, 512], BF16) for d in range(2)]
    kE = [P.sb("kE%d" % d, [128, 4, 512], BF16) for d in range(2)]
    EQ = _alt(P, "EQ", [128, 4, 128], F32)
    EK = _alt(P, "EK", [128, 4, 128], F32)
    EE = _alt(P, "EE", [128, 512], F32)
    psq = _palt(P, "psq", [128, 512], F32, 2)
    psz = P.ps("psz", [16, 512], F32)
    psl = P.ps("psl", [128, 512], F32)
    psc = _palt(P, "psc", [128, 4, 128], F32, 2)
    pse = P.ps("pse", [128, 512], F32)
    QXd = [C.gQF, C.gQB]
    KXd = [C.gKF, C.gKB]
    kEd = [C.gkEF, C.gkEB]
    it = 0
    def prep_l(g):
        P.D("sp", hT[g % 2][:], hview(h_src, g), w=[hT[g % 2]], r=[("h", g)])

    def prep_a(g):
        nrm.stats_a(hT[g % 2])

    def prep_b(g):
        nrm.pre(hT[g % 2], gcol, uT[g % 2], skip_a=True)

    prep_l(0)
    prep_a(0)
    prep_b(0)
    for g in range(C.NG):
        u = uT[g % 2]
        tsl = slice(g * 512, (g + 1) * 512)
        for qk in range(2):
            dst = qs if qk == 0 else ks
            for h in range(4):
                ps = psq[h % 2]
                col = qk * 512 + h * 128
                for kc in range(8):
                    P.I("pe", lambda e, ps=ps, kc=kc, col=col, u=u: e.matmul(ps[:], W[:, kc, col:col + 128], u[:, kc, :], start=(kc == 0), stop=(kc == 7)),
                        w=[ps], r=[u, ("W", col // 512)])
                sc = float(128 ** -0.5) if qk == 0 else 1.0
                P.I("act", lambda e, ps=ps, dst=dst, h=h, sc=sc: e.activation(out=dst[:, h, :], in_=ps[:], func=AF.Copy, scale=sc), w=[dst], r=[ps])
        for d in range(2):
            for kc in range(8):
                P.I("pe", lambda e, d=d, kc=kc, u=u: e.matmul(psz[:], w1[:, d, kc, :], u[:, kc, :], start=(kc == 0), stop=(kc == 7)), w=[psz], r=[u, w1])
            P.I("act", lambda e, d=d: e.activation(out=zTa[d][0:16, :], in_=psz[:], func=AF.Copy), w=[zTa[d]], r=[psz])
            for tt in range(4):
                P.I("pe", lambda e, d=d, tt=tt: e.matmul(psl[:], zTa[d][:, tt * 128:(tt + 1) * 128], w2a[:, d, :], start=True, stop=True), w=[psl], r=[zTa[d], w2a])
                P.I("act", lambda e, d=d, tt=tt: e.activation(out=lgt[d][:, tt, :], in_=psl[:], func=AF.Exp, scale=-1.0), w=[lgt[d]], r=[psl])
            P.I("act", lambda e, d=d: e.activation(out=lgt[d][:], in_=lgt[d][:], func=AF.Ln, bias=1.0), w=[lgt[d]], r=[lgt[d]])
            P.I("dve", lambda e, d=d: e.tensor_scalar(out=lgt[d][:], in0=lgt[d][:], scalar1=-1.0 / 16.0, scalar2=None, op0=ALU.mult), w=[lgt[d]], r=[lgt[d]])
        for tt in range(4):
            if tt == 0 and g + 1 < C.NG:
                prep_l(g + 1)
            if tt == 2 and g + 1 < C.NG:
                prep_a(g + 1)
            if tt == 3 and g + 1 < C.NG:
                prep_b(g + 1)
            n = g * 4 + tt
            csl = slice(tt * 128, (tt + 1) * 128)
            for kc in range(8):
                P.I("pe", lambda e, kc=kc, csl=csl, u=u: e.matmul(pse[:], u[:, kc, csl], W[:, kc, 512:1024], start=(kc == 0), stop=(kc == 7)), w=[pse], r=[u, ("W", 1)])
            ktm = EE[0]
            P.I("act", lambda e, ktm=ktm: e.activation(out=ktm[:], in_=pse[:], func=AF.Copy), w=[ktm], r=[pse])
            for d in range(2):
                pc = psc[it % 2]
                eq = EQ[it % 2]
                ek = EK[it % 2]
                it += 1
                for h in range(4):
                    P.I("pe", lambda e, pc=pc, d=d, h=h, tt=tt: e.matmul(pc[:, h, :], lgt[d][:, tt, h * 128:(h + 1) * 128], tri[d][:, 0:128], start=True, stop=True),
                        w=[pc], r=[lgt[d], tri[d]])
                P.I("act", lambda e, pc=pc, eq=eq: e.activation(out=eq[:], in_=pc[:], func=AF.Exp), w=[eq], r=[pc])
                P.I("act", lambda e, pc=pc, ek=ek: e.activation(out=ek[:], in_=pc[:], func=AF.Exp, scale=-1.0), w=[ek], r=[pc])
                tcol = 127 if d == 0 else 0
                P.I("pool", lambda e, eq=eq, d=d, n=n, tcol=tcol: e.tensor_copy(out=DEC[d][:, :, n:n + 1], in_=eq[:, :, tcol:tcol + 1]), w=[DEC[d]], r=[eq])
                P.I("dve", lambda e, eq=eq, d=d, csl=csl: e.tensor_tensor(out=QX[d][:, :, csl], in0=qs[:, :, csl], in1=eq[:], op=ALU.mult), w=[QX[d]], r=[qs, eq])
                P.I("pool", lambda e, ek=ek, d=d, csl=csl: e.tensor_tensor(out=KX[d][:, :, csl], in0=ks[:, :, csl], in1=ek[:], op=ALU.mult), w=[KX[d]], r=[ks, ek])
                P.I("pe", lambda e, d=d, tt=tt: e.matmul(psl[:], sX[d][:], lgt[d][:, tt, :], start=True, stop=True), w=[psl], r=[sX[d], lgt[d]])
                ee = EE[1]
                P.I("act", lambda e, ee=ee: e.activation(out=ee[:], in_=psl[:], func=AF.Exp), w=[ee], r=[psl])
                P.I("dve", lambda e, ee=ee, d=d, tt=tt, ktm=ktm: e.tensor_tensor(out=kE[d][:, tt, :], in0=ktm[:], in1=ee[:], op=ALU.mult), w=[kE[d]], r=[ktm, ee])
        for d in range(2):
            P.D("sp", QXd[d][:, :, tsl].rearrange("k p t -> p k t"), QX[d][:], r=[QX[d]], w=[("gq", d, g)])
            P.D("sp", KXd[d][:, :, tsl].rearrange("k p t -> p k t"), KX[d][:], r=[KX[d]], w=[("gk", d, g)])
            P.D("sp", kEd[d][tsl, :].rearrange("(tt p) n -> p tt n", p=128), kE[d][:], r=[kE[d]], w=[("gke", d, g)])
    for d in range(2):
        P.D("sp", C.gDEC[d], DEC[d][:].rearrange("p h n -> p (h n)"), r=[DEC[d]], w=[("gdec", d)])
    P.end_phase()


def phase_gla_sweep1(P, C):
    P.begin_phase()
    cs = load_consts(P, C, ["maskLF", "maskLB"])
    DECt = P.sb("DECt", [128, 4 * C.NCH], F32)
    P.D("sp", DECt[:], C.gDEC[0], w=[DECt])
    NCH = C.NCH
    cfg = dict(H=4, NC=1, DV=256, pairs=[(C.gKF, C.gQF, "maskLF"), (C.gKB, C.gQB, "maskLB")], Q=C.gQF, KE=C.gkEF, V=C.gv, P1=C.gP1,
               dec=lambda h, c, n: DECt[:, h * NCH + n:h * NCH + n + 1], dec_res=DECt, masks=cs, idb=None, xin=C.xin2, xout=C.xout2)
    S = sweep(P, C, cfg, False)
    P.end_phase()


def phase_gla_sweep2(P, C):
    P.begin_phase()
    cs = load_consts(P, C, ["sel"])
    NCH = C.NCH
    DECt = P.sb("DECt", [128, 4 * NCH], F32)
    P.D("sp", DECt[:], C.gDEC[1], w=[DECt])
    gbc = P.sb("gbc", [128, 256], F32)
    P.D("sp", gbc[:], C.gla_ng, w=[gbc])
    idb, epi_load, epi, epi_store = make_epilogue(P, C, 4, 256, C.gsg, C.gyT, gain_bc=gbc)
    cfg = dict(H=4, NC=1, DV=256, pairs=[], Q=C.gQB, KE=C.gkEB, V=C.gv, P1=C.gP1,
               dec=lambda h, c, n: DECt[:, h * NCH + n:h * NCH + n + 1], dec_res=DECt, idb=idb, sel=cs["sel"],
               epi_load=epi_load, epi=epi, epi_store=epi_store, xin=C.xin2, xout=C.xout2)
    sweep(P, C, cfg, True)
    P.end_phase()


def layer_gla(P, C, li, h_src, h_dst):
    phase_gla_qk(P, C, li, h_src)
    phase_vg(P, C, li, h_src, C.gla_w_in[0][:, 1024:3072], 1024, C.gv, C.gsg)
    phase_gla_sweep1(P, C)
    phase_gla_sweep2(P, C)
    phase_outproj(P, C, li, 1, C.gla_w_out[0], 8, C.gyT, h_src, h_dst)


def phase_conv_glu(P, C, li, h_src):
    P.begin_phase()
    T = C.T
    W = P.sb("W", [128, 8, 2048], BF16)
    load_w(P, W, C.conv_w_in[0], 8, 2048)
    gcol = load_gcol(P, C, li, 0)
    nrm = Norm(P)
    bin_ = P.sb("bin", [128, 16], F32)
    P.D("sp", bin_[:], C.conv_bin, w=[bin_])
    hT = _alt(P, "hT", [128, 8, 512], F32)
    uT = _alt(P, "uT", [128, 8, 512], BF16)
    hg = _alt(P, "hg", [128, 8, 512], BF16)
    sig = _alt(P, "sig", [128, 512], F32)
    psa = _palt(P, "psa", [128, 512], F32, 2)
    psg = _palt(P, "psg", [128, 512], F32, 2)
    zt = P.sb("zt", [128, 8, 16], BF16)
    P.I("dve", lambda e: e.memset(zt[:], 0.0), w=[zt])
    P.D("sp", C.cHG[:, :, 0:16].rearrange("k p t -> p k t"), zt[:], r=[zt], w=[("hgpad",)])
    def prep_l(g):
        P.D("sp", hT[g % 2][:], hview(h_src, g), w=[hT[g % 2]], r=[("h", g)])

    def prep_a(g):
        nrm.stats_a(hT[g % 2])

    def prep_b(g):
        nrm.pre(hT[g % 2], gcol, uT[g % 2], skip_a=True)

    prep_l(0)
    prep_a(0)
    prep_b(0)
    for g in range(C.NG):
        u = uT[g % 2]
        o = hg[g % 2]
        for fc in range(8):
            if fc == 0 and g + 1 < C.NG:
                prep_l(g + 1)
            if fc == 4 and g + 1 < C.NG:
                prep_a(g + 1)
            if fc == 6 and g + 1 < C.NG:
                prep_b(g + 1)
            pa = psa[fc % 2]
            pg = psg[fc % 2]
            sg_ = sig[fc % 2]
            for kc in range(8):
                P.I("pe", lambda e, pa=pa, kc=kc, fc=fc, u=u: e.matmul(pa[:], W[:, kc, fc * 128:(fc + 1) * 128], u[:, kc, :], start=(kc == 0), stop=(kc == 7)),
                    w=[pa], r=[u, ("W", fc // 4)])
            for kc in range(8):
                P.I("pe", lambda e, pg=pg, kc=kc, fc=fc, u=u: e.matmul(pg[:], W[:, kc, 1024 + fc * 128:1024 + (fc + 1) * 128], u[:, kc, :], start=(kc == 0), stop=(kc == 7)),
                    w=[pg], r=[u, ("W", 2 + fc // 4)])
            P.I("act", lambda e, pg=pg, sg_=sg_, fc=fc: e.activation(out=sg_[:], in_=pg[:], func=AF.Sigmoid, bias=bin_[:, 8 + fc:9 + fc]), w=[sg_], r=[pg, bin_])
            P.I("dve", lambda e, pa=pa, sg_=sg_, fc=fc, o=o: e.scalar_tensor_tensor(out=o[:, fc, :], in0=pa[:], scalar=bin_[:, fc:fc + 1], in1=sg_[:],
                                                                                 op0=ALU.add, op1=ALU.mult), w=[o], r=[pa, sg_, bin_])
        P.D("sp", C.cHG[:, :, 16 + g * 512:16 + (g + 1) * 512].rearrange("k p t -> p k t"), o[:], r=[o], w=[("hg", g)])
    P.D("sp", C.cHG[:, :, 16 + T:32 + T].rearrange("k p t -> p k t"), zt[:], r=[zt], w=[("hghalo",)])
    P.end_phase()


def phase_conv_dw(P, C):
    P.begin_phase()
    T = C.T
    idb = make_ident(P)
    wT = P.sb("wT", [128, 8, 31], F32)
    P.D("sp", wT[:], C.conv_wdwT, w=[wT])
    D = P.sb("D", [128, 248, 128], BF16)
    for j in range(31):
        for fc in range(8):
            ve = "dve" if (j + fc) % 2 == 0 else "pool"
            P.I(ve, lambda e, j=j, fc=fc: e.tensor_scalar(out=D[:, j * 8 + fc, :], in0=idb[:], scalar1=wT[:, fc, j:j + 1], scalar2=None, op0=ALU.mult),
                w=[("D", j * 8 + fc)], r=[idb, wT])
    Dkeys = [("D", i) for i in range(248)]
    rows = P.sb("rows", [128, 3, 1024], F32)
    P.D("sp", rows[:], C.conv_rows, w=[rows])
    hw = _alt(P, "hw", [128, 8, 542], BF16)
    psc = _palt(P, "psc", [128, 1024], F32, 2)
    xs = _alt(P, "xs", [128, 1024], F32)
    junk = P.sb("junk", [128, 1024], F32)
    st = P.sb("st", [128, 8], F32)
    yb = _alt(P, "yb", [128, 1024], BF16)
    psT = _palt(P, "psT", [128, 1024], BF16, 2)
    yTg = _alt(P, "yTg", [128, 8, 512], BF16)
    k = 0
    for g in range(C.NG):
        w_ = hw[g % 2]
        yg = yTg[g % 2]
        P.D("sp", w_[:], C.cHG[:, :, 1 + g * 512:1 + g * 512 + 542].rearrange("k p t -> p k t"), w=[w_])
        for tt in range(4):
            pc = psc[k % 2]
            x = xs[k % 2]
            y = yb[k % 2]
            pT = psT[k % 2]
            k += 1
            for fc in range(8):
                for j in range(31):
                    P.I("pe", lambda e, pc=pc, fc=fc, j=j, tt=tt, w_=w_: e.matmul(pc[:, fc * 128:(fc + 1) * 128], w_[:, fc, tt * 128 + j:tt * 128 + j + 128],
                                                                              D[:, j * 8 + fc, :], start=(j == 0), stop=(j == 30)),
                        w=[pc], r=[w_] + (Dkeys if (g == 0 and tt == 0) else []))
            P.I("dve", lambda e, pc=pc, x=x: e.tensor_tensor(out=x[:], in0=pc[:], in1=rows[:, 0, :], op=ALU.add), w=[x], r=[pc, rows])
            P.I("dve", lambda e: e.memset(st[:, 0:2], 0.0), w=[st])
            P.I("act", lambda e, x=x: e.activation(out=junk[:], in_=x[:], func=AF.Copy, accum_out=st[:, 0:1]), w=[junk, st], r=[x, st])
            P.I("act", lambda e, x=x: e.activation(out=junk[:], in_=x[:], func=AF.Square, accum_out=st[:, 1:2]), w=[junk, st], r=[x, st])
            P.I("dve", lambda e: e.tensor_scalar(out=st[:, 2:4], in0=st[:, 0:2], scalar1=1.0 / 1024.0, scalar2=None, op0=ALU.mult), w=[st], r=[st])
            P.I("dve", lambda e: e.tensor_tensor(out=st[:, 4:5], in0=st[:, 2:3], in1=st[:, 2:3], op=ALU.mult), w=[st], r=[st])
            P.I("dve", lambda e: e.tensor_tensor(out=st[:, 5:6], in0=st[:, 3:4], in1=st[:, 4:5], op=ALU.subtract), w=[st], r=[st])
            P.I("act", lambda e: e.activation(out=st[:, 6:7], in_=st[:, 5:6], func=AF.Sqrt, bias=EPS), w=[st], r=[st])
            P.I("dve", lambda e: e.reciprocal(out=st[:, 7:8], in_=st[:, 6:7]), w=[st], r=[st])
            P.I("dve", lambda e, x=x: e.tensor_scalar(out=x[:], in0=x[:], scalar1=st[:, 2:3], scalar2=st[:, 7:8], op0=ALU.subtract, op1=ALU.mult), w=[x], r=[x, st])
            P.I("pool", lambda e, x=x: e.tensor_tensor(out=x[:], in0=x[:], in1=rows[:, 1, :], op=ALU.mult), w=[x], r=[x, rows])
            P.I("pool", lambda e, x=x: e.tensor_tensor(out=x[:], in0=x[:], in1=rows[:, 2, :], op=ALU.add), w=[x], r=[x, rows])
            P.I("act", lambda e, x=x, y=y: e.activation(out=y[:], in_=x[:], func=AF.Silu), w=[y], r=[x])
            for fc in range(8):
                P.I("pe", lambda e, pT=pT, y=y, fc=fc: e.transpose(out=pT[:, fc * 128:(fc + 1) * 128], in_=y[:, fc * 128:(fc + 1) * 128], identity=idb[:]),
                    w=[pT], r=[y, idb])
            P.I("act", lambda e, pT=pT, yg=yg, tt=tt: e.activation(out=yg[:, :, tt * 128:(tt + 1) * 128], in_=pT[:].rearrange("p (q t) -> p q t", q=8), func=AF.Copy),
                w=[yg], r=[pT])
        P.D("sp", C.cyT[:, :, g * 512:(g + 1) * 512].rearrange("k p t -> p k t"), yg[:], r=[yg], w=[("ysrc", g)])
    P.end_phase()


def layer_conv(P, C, li, h_src, h_dst):
    phase_conv_glu(P, C, li, h_src)
    phase_conv_dw(P, C)
    phase_outproj(P, C, li, 1, C.conv_w_out[0], 8, C.cyT, h_src, h_dst, bias_src=C.conv_bout)


def declare_extra(nc, C, kinds, din, dint):
    T = C.T
    if "gla" in kinds:
        C.gla_w_in = din("gla_w_in", [1, 1024, 3072])
        C.gla_w1 = din("gla_w1", [2, 1024, 16])
        C.gla_w2 = din("gla_w2", [2, 16, 512])
        C.gla_b = din("gla_b", [2, 512])
        C.gla_ng = din("gla_ng", [128, 256])
        C.gla_w_out = din("gla_w_out", [1, 1024, 1024])
        C.gQF = dint("gQF", [4, 128, T]); C.gQB = dint("gQB", [4, 128, T])
        C.gKF = dint("gKF", [4, 128, T]); C.gKB = dint("gKB", [4, 128, T])
        C.gkEF = dint("gkEF", [T, 512]); C.gkEB = dint("gkEB", [T, 512])
        C.gv = dint("gv", [T, 1024]); C.gsg = dint("gsg", [T, 1024]); C.gP1 = dint("gP1", [T, 1024])
        C.gyT = dint("gyT", [8, 128, T])
        C.gDEC = dint("gDEC", [2, 128, 4 * C.NCH], F32)
        C.xin2 = dint("xin2", [512, 256], F32)
        C.xout2 = dint("xout2", [1024, 256], F32)
    if "conv" in kinds:
        C.conv_w_in = din("conv_w_in", [1, 1024, 2048])
        C.conv_bin = din("conv_bin", [128, 16])
        C.conv_wdwT = din("conv_wdwT", [128, 8, 31])
        C.conv_rows = din("conv_rows", [128, 3, 1024])
        C.conv_w_out = din("conv_w_out", [1, 1024, 1024])
        C.conv_bout = din("conv_bout", [128, 8])
        C.cHG = dint("cHG", [8, 128, T + 32])
        C.cyT = dint("cyT", [8, 128, T])
        C.xin3 = dint("xin3", [1024, 16])
        C.xout3 = dint("xout3", [2048, 16])


def extra_in_maps(m, inputs, T, b, half, kinds):
    f = lambda k: np.asarray(inputs[k], np.float32)
    if "gla" in kinds:
        m["gla_w_in"] = f("gla_w_in")
        sw = (lambda a: a) if half == 0 else (lambda a: a[::-1])
        m["gla_w1"] = np.ascontiguousarray(sw(f("gla_gate_w1")[0]))
        m["gla_w2"] = np.ascontiguousarray(sw(f("gla_gate_w2")[0]))
        m["gla_b"] = np.ascontiguousarray(sw(f("gla_gate_b")[0]))
        m["gla_ng"] = np.ascontiguousarray(np.broadcast_to(f("gla_norm_gain")[0][None, :], (128, 256)))
        m["gla_w_out"] = f("gla_w_out")
    if "conv" in kinds:
        m["conv_w_in"] = f("conv_w_in")
        m["conv_bin"] = np.ascontiguousarray(f("conv_b_in")[0].reshape(16, 128).T)
        wdw = f("conv_w_dw")[0]
        if half == 1:
            wdw = wdw[::-1]
        m["conv_wdwT"] = np.ascontiguousarray(wdw.reshape(31, 8, 128).transpose(2, 1, 0))
        rows = np.stack([f("conv_b_dw")[0], f("conv_ln_gain")[0], f("conv_ln_bias")[0]], axis=0)
        m["conv_rows"] = np.ascontiguousarray(np.broadcast_to(rows[None], (128, 3, 1024)))
        m["conv_w_out"] = f("conv_w_out")
        m["conv_bout"] = np.ascontiguousarray(f("conv_b_out")[0].reshape(8, 128).T)


CST_LAYOUT = [("maskLF", 128), ("maskLB", 128), ("A1", 128), ("A2", 128), ("A3", 128), ("EF", 512), ("EB", 512),
              ("c127", 1), ("cs", 1), ("c128", 1), ("invf", 1), ("sel", 2),
              ("triU", 129), ("triL", 129), ("sL", 128), ("sU", 128)]


def cst_offsets():
    off = {}
    o = 0
    for nm, w in CST_LAYOUT:
        off[nm] = (o, w)
        o += w
    return off, o


def make_cst(half):
    off, n = cst_offsets()
    c = np.zeros((128, n), np.float32)
    s = np.arange(128)[:, None].astype(np.float64)
    t = np.arange(128)[None, :].astype(np.float64)

    def put(nm, a):
        o, w = off[nm]
        c[:, o:o + w] = np.broadcast_to(a, (128, w))
    put("maskLF", (s <= t) if half == 0 else (s < t))
    put("maskLB", (s > t) if half == 0 else (s >= t))
    put("A1", -(s + 1) + 0 * t)
    put("A2", s - t)
    put("A3", -(t + 1) + 0 * s)
    tt = (np.arange(512) % 128)[None, :]
    put("EF", tt + 1.0)
    put("EB", 128.0 - tt)
    put("c127", 127.0 - s)
    put("cs", s)
    put("c128", 128.0)
    put("invf", (10000.0 ** (-(np.arange(128, dtype=np.float32) / np.float32(128)))).astype(np.float32)[:, None])
    put("sel", np.array([[0.0, 1.0]]) if half == 0 else np.array([[1.0, 0.0]]))
    put("triU", np.concatenate([(s <= t), np.ones((128, 1))], axis=1))
    put("triL", np.concatenate([(s >= t), np.ones((128, 1))], axis=1))
    put("sL", (s > t))
    put("sU", (s < t))
    return c


def build(T, plan):
    nc = bass.Bass("TRN2", target_bir_lowering=False)
    C = Ctx()
    C.T, C.NG, C.NCH = T, T // 512, T // 128
    C.cst_off, ncst = cst_offsets()

    def din(name, shape, dt=F32):
        return nc.dram_tensor(name, list(shape), dt, kind="ExternalInput").ap()

    def dint(name, shape, dt=BF16):
        return nc.dram_tensor(name, list(shape), dt).ap()
    C.xT = din("xT", [8, 128, T])
    C.posr = din("posr", [128, T], I32)
    C.gcol = din("gcol", [128, 128])
    C.cst = din("cst", [128, ncst])
    kinds = set(k for k, _, _ in plan)
    C.ret_dl = din("ret_dl", [2, 128, 8])
    if any(k.startswith("p_") for k in kinds):
        kinds = kinds | {"ret"}
    if "ret" in kinds:
        C.ret_w_in = din("ret_w_in", [2, 1024, 6144])
        C.ret_w_out = din("ret_w_out", [2, 2048, 1024])
    if "mlp" in kinds:
        C.mlp_w_up = din("mlp_w_up", [4, 1024, 4096])
        C.mlp_w_down = din("mlp_w_down", [4, 4096, 1024])
    declare_extra(nc, C, kinds, din, dint)
    C.outT = nc.dram_tensor("outT", [8, 128, T], F32, kind="ExternalOutput").ap()
    C.hA = dint("hA", [8, 128, T], F32)
    C.qF = dint("qF", [8, 128, T])
    C.qB = dint("qB", [8, 128, T])
    C.kT = dint("kT", [8, 128, T])
    C.kB = dint("kB", [8, 128, T])
    C.kEF = dint("kEF", [T, 1024])
    C.kEB = dint("kEB", [T, 1024])
    C.v = dint("v", [T, 2048])
    C.sg = dint("sg", [T, 2048])
    C.P1 = dint("P1", [T, 2048])
    C.yT = dint("yT", [16, 128, T])
    C.hid = dint("hid", [32, 128, T])
    C.xin = dint("xin", [1024, 512], F32)
    C.xout = dint("xout", [2048, 512], F32)
    P = Prog(nc)
    nsteps = len(plan)
    src = C.xT
    for i, (kind, li, j) in enumerate(plan):
        dst = C.outT if i == nsteps - 1 else C.hA
        if kind == "ret":
            layer_ret(P, C, li, j, src, dst)
        elif kind == "mlp":
            layer_mlp(P, C, li, src, dst)
        elif kind == "p_qk":
            phase_ret_qk(P, C, li, j, src)
        elif kind == "p_vg":
            phase_vg(P, C, li, src, C.ret_w_in[j][:, 2048:6144], 2048, C.v, C.sg)
        elif kind == "p_s1":
            phase_ret_sweep1(P, C, j)
        elif kind == "p_s2":
            phase_ret_sweep2(P, C, j)
        elif kind == "p_out":
            phase_outproj(P, C, li, 1, C.ret_w_out[j], 16, C.yT, src, dst)
        elif kind == "gla":
            layer_gla(P, C, li, src, dst)
        elif kind == "conv":
            layer_conv(P, C, li, src, dst)
        src = dst
    P.begin_phase()
    P.end_phase(final=True)
    P.close()
    return nc, P


FULL_PLAN = [("ret", 0, 0), ("mlp", 0, 0), ("conv", 1, 0), ("mlp", 1, 0), ("gla", 2, 0), ("mlp", 2, 0), ("ret", 3, 1), ("mlp", 3, 0)]


def make_in_maps(inputs, T, ncores=4, kinds=("ret", "mlp", "conv", "gla")):
    x = np.asarray(inputs["x"], np.float32)
    pos = np.asarray(inputs["positions"], np.int32)
    ng = np.asarray(inputs["norm_gains"], np.float32)
    L = ng.shape[0]
    gcol = np.zeros((128, 128), np.float32)
    gcol[:, :L * 32] = ng.reshape(L, 4, 8, 128).transpose(3, 0, 1, 2).reshape(128, L * 32)
    rdl = np.asarray(inputs["ret_decay_logit"], np.float32)
    cst = make_cst(0)
    maps = []
    for b in range(ncores):
        m = {}
        m["xT"] = np.ascontiguousarray(x[b].reshape(T, 8, 128).transpose(1, 2, 0))
        m["posr"] = np.ascontiguousarray(np.broadcast_to(pos[b][None, :], (128, T)))
        m["gcol"] = gcol
        m["cst"] = cst
        m["ret_dl"] = np.ascontiguousarray(np.broadcast_to(rdl.reshape(2, 1, 8), (2, 128, 8)))
        big = []
        if "ret" in kinds:
            big += ["ret_w_in", "ret_w_out"]
        if "mlp" in kinds:
            big += ["mlp_w_up", "mlp_w_down"]
        for k in big:
            m[k] = np.asarray(inputs[k], np.float32)
        extra_in_maps(m, inputs, T, b, 0, kinds)
        maps.append(m)
    return maps


def gather_out(res, T, ncores=4):
    out = np.zeros((ncores, T, 1024), np.float32)
    for b in range(ncores):
        out[b] = res.results[b]["outT"].reshape(1024, T).T
    return out


_CACHE = {}


def kernel(**inputs):
    T = 8192
    if "nc" not in _CACHE:
        _CACHE["nc"] = build(T, FULL_PLAN)[0]
    nc = _CACHE["nc"]
    maps = make_in_maps(inputs, T, ncores=4)
    res = run_bass_kernel_spmd(nc, maps, core_ids=list(range(4)))
    return gather_out(res, T, 4)
```

```python
import numpy as np
from contextlib import ExitStack
import concourse.bass as bass
import concourse.mybir as mybir
from concourse.bass_utils import run_bass_kernel_spmd

F32 = mybir.dt.float32
BF16 = mybir.dt.bfloat16
I32 = mybir.dt.int32
AF = mybir.ActivationFunctionType
ALU = mybir.AluOpType

ENGS = ("pe", "act", "dve", "pool", "sp")


def _key(x):
    if isinstance(x, (str, tuple)):
        return x
    t = getattr(x, "tensor", None)
    return t.name if t is not None else x.name


class _Op:
    __slots__ = ("eng", "fn", "deps", "sig", "dma", "waits")


class Prog:
    N_DMA_SEMS = {"sp": 16, "pool": 12, "act": 4}

    def __init__(self, nc):
        self.nc = nc
        self.es = ExitStack()
        self.sem = {e: self.es.enter_context(nc.semaphore("s_" + e)) for e in ENGS if e != "sp"}
        self.sem["sp"] = self.es.enter_context(nc.semaphore("s_sp"))
        self.cnt = {e: 0 for e in ENGS}
        self.dsem = {q: [self.es.enter_context(nc.semaphore("d_%s%d" % (q, i))) for i in range(n)]
                     for q, n in self.N_DMA_SEMS.items()}
        self.dcnt = {q: [0] * n for q, n in self.N_DMA_SEMS.items()}
        self.dlast = {q: [None] * n for q, n in self.N_DMA_SEMS.items()}
        self.drr = {q: 0 for q in self.N_DMA_SEMS}
        self.known = {e: {} for e in ENGS}
        self.res_w = {}
        self.res_r = {}
        self.ops = []
        self.phase_es = None
        self.uid = 0
        self.last_sig = {e: None for e in ENGS}
        self.n_inst = 0

    def begin_phase(self):
        self.phase_es = ExitStack()
        self.nph = getattr(self, "nph", 0) + 1
        self.sem["pe"] = self.es.enter_context(self.nc.semaphore("s_pe%d" % self.nph))
        self.cnt["pe"] = 0

    def sb(self, name, shape, dtype):
        self.uid += 1
        return self.phase_es.enter_context(self.nc.sbuf_tensor("%s_%d" % (name, self.uid), list(shape), dtype))

    def ps(self, name, shape, dtype):
        self.uid += 1
        return self.phase_es.enter_context(self.nc.psum_tensor("%s_%d" % (name, self.uid), list(shape), dtype))

    def _deps(self, r, w):
        deps = []
        for x in r:
            k = _key(x)
            s = self.res_w.get(k)
            if s is not None:
                deps.append(s)
        for x in w:
            k = _key(x)
            s = self.res_w.get(k)
            if s is not None:
                deps.append(s)
            deps.extend(self.res_r.get(k, ()))
        return deps

    def _commit(self, sig, r, w):
        for x in r:
            self.res_r.setdefault(_key(x), []).append(sig)
        for x in w:
            k = _key(x)
            self.res_w[k] = sig
            self.res_r[k] = []

    def I(self, eng, fn, w=(), r=()):
        deps = self._deps(r, w)
        if eng == "pe":
            deps = [d for d in deps if d[0] is not self.sem["pe"]]
        self.cnt[eng] += 1
        sig = (self.sem[eng], self.cnt[eng])
        op = _Op()
        op.eng, op.fn, op.sig, op.dma = eng, fn, sig, False
        op.waits = self._waits(eng, deps)
        self.ops.append(op)
        self._commit(sig, r, w)
        self.last_sig[eng] = sig
        return sig

    def D(self, q, out, in_, w=(), r=(), **kw):
        return self.Dfn(q, lambda e: e.dma_start(out=out, in_=in_, **kw), w=w, r=r)

    def Dfn(self, q, fn, w=(), r=()):
        deps = self._deps(r, w)
        i = self.drr[q]
        self.drr[q] = (i + 1) % len(self.dsem[q])
        if self.dlast[q][i] is not None:
            deps.append(self.dlast[q][i])
        self.dcnt[q][i] += 1
        sig = (self.dsem[q][i], 16 * self.dcnt[q][i])
        self.dlast[q][i] = sig
        op = _Op()
        op.eng, op.sig, op.dma = q, sig, True
        op.fn = fn
        op.waits = self._waits(q, deps)
        self.ops.append(op)
        self._commit(sig, r, w)
        return sig

    def _waits(self, eng, deps):
        best = {}
        for (s, v) in deps:
            if best.get(id(s), (None, 0))[1] < v:
                best[id(s)] = (s, v)
        out = []
        kn = self.known[eng]
        for sid, (s, v) in best.items():
            if kn.get(sid, 0) >= v:
                continue
            kn[sid] = v
            out.append((s, v))
        return out

    def barrier(self):
        sigs = [s for s in self.last_sig.values() if s is not None]
        for q in self.dlast:
            sigs.extend(s for s in self.dlast[q] if s is not None)
        for e in ENGS:
            ws = self._waits(e, sigs)
            if ws:
                op = _Op()
                op.eng, op.fn, op.sig, op.dma, op.waits = e, None, None, False, ws
                self.ops.append(op)

    def wait_all(self, eng):
        sigs = [s for s in self.last_sig.values() if s is not None]
        for q in self.dlast:
            sigs.extend(s for s in self.dlast[q] if s is not None)
        ws = self._waits(eng, sigs)
        if ws:
            op = _Op()
            op.eng, op.fn, op.sig, op.dma, op.waits = eng, None, None, False, ws
            self.ops.append(op)

    def end_phase(self, final=False):
        self.barrier()
        if final:
            self.wait_all("sp")
        ops = self.ops
        self.ops = []
        per = {e: [o for o in ops if o.eng == e] for e in ENGS}
        self.n_inst += len(ops)
        if not hasattr(self, "remap"):
            self.remap = {}
            self.newcnt = {}
        eng_ids = set(id(x) for x in self.sem.values())
        targets = set()
        for o in ops:
            for (sm, v) in o.waits:
                if id(sm) in eng_ids:
                    targets.add((id(sm), v))
        for o in ops:
            if o.fn is None or o.dma:
                continue
            sm, v = o.sig
            if (id(sm), v) in targets:
                self.newcnt[id(sm)] = self.newcnt.get(id(sm), 0) + 1
                self.remap[(id(sm), v)] = self.newcnt[id(sm)]
        remap = self.remap

        def replay(e, lst):
            for o in lst:
                for (sm, v) in o.waits:
                    e.wait_ge(sm, remap.get((id(sm), v), v) if id(sm) in eng_ids else v)
                if o.fn is not None:
                    ins = o.fn(e)
                    if o.dma:
                        ins.then_inc(o.sig[0], 16)
                    elif (id(o.sig[0]), o.sig[1]) in remap:
                        ins.then_inc(o.sig[0], 1)

        with self.nc.Block() as block:
            if per["pe"]:
                block.tensor(lambda e: replay(e, per["pe"]))
            if per["act"]:
                block.scalar(lambda e: replay(e, per["act"]))
            if per["dve"]:
                block.vector(lambda e: replay(e, per["dve"]))
            if per["pool"]:
                block.gpsimd(lambda e: replay(e, per["pool"]))
            if per["sp"]:
                block.sync(lambda e: replay(e, per["sp"]))
        self.phase_es.close()
        self.phase_es = None
        self.res_w = {k: v for k, v in self.res_w.items() if isinstance(k, tuple)}
        self.res_r = {k: v for k, v in self.res_r.items() if isinstance(k, tuple)}

    def close(self):
        self.es.close()

PI = float(np.pi)
EPS = 1e-6


class Ctx:
    pass


def _alt(P, name, shape, dtype, n=2):
    return [P.sb("%s%d" % (name, i), shape, dtype) for i in range(n)]


def _palt(P, name, shape, dtype, n=2):
    return [P.ps("%s%d" % (name, i), shape, dtype) for i in range(n)]


def load_consts(P, C, names):
    out = {}
    for nm in names:
        off, w = C.cst_off[nm]
        t = P.sb("c_" + nm, [128, w], F32)
        P.D("sp", t[:], C.cst[:, off:off + w], w=[t], allow_slow_non_contiguous=True)
        out[nm] = t
    return out


def make_ident(P, dtype=BF16):
    idb = P.sb("ident", [128, 128], dtype)
    P.I("dve", lambda e: e.memset(idb[:], 0.0), w=[idb])
    P.I("pool", lambda e: e.affine_select(out=idb[:], in_=idb[:], pattern=[[-1, 128]], compare_op=ALU.not_equal,
                                          fill=1.0, base=0, channel_multiplier=1), w=[idb], r=[idb])
    return idb


def make_ones(P):
    o = P.sb("ones", [128, 128], BF16)
    P.I("dve", lambda e: e.memset(o[:], 1.0), w=[o])
    return o


def load_gcol(P, C, li, i):
    g = P.sb("gcol", [128, 8], F32)
    o = (li * 4 + i) * 8
    P.D("sp", g[:], C.gcol[:, o:o + 8], w=[g])
    return g


class Norm:
    def __init__(self, P, nk=8, G=512):
        self.P = P
        self.ones = make_ones(P)
        self.sq = P.sb("nsq", [128, nk, G], BF16)
        self.ps = P.ps("nps", [128, G], F32)
        self.rstd = P.sb("nrstd", [128, G], F32)
        self.nk = nk

    def stats_a(self, x):
        self.P.I("act", lambda e: e.activation(out=self.sq[:], in_=x[:], func=AF.Square), w=[self.sq], r=[x])

    def stats(self, x, d_model, skip_a=False):
        P = self.P
        nk = self.nk
        if not skip_a:
            self.stats_a(x)
        for kc in range(nk):
            P.I("pe", lambda e, kc=kc: e.matmul(self.ps[:], self.ones[:], self.sq[:, kc, :], start=(kc == 0), stop=(kc == nk - 1)),
                w=[self.ps], r=[self.sq, self.ones])
        P.I("act", lambda e: e.activation(out=self.rstd[:], in_=self.ps[:], func=AF.Sqrt, scale=1.0 / d_model, bias=EPS),
            w=[self.rstd], r=[self.ps])
        P.I("dve", lambda e: e.reciprocal(out=self.rstd[:], in_=self.rstd[:]), w=[self.rstd], r=[self.rstd])

    def pre(self, hT, gcol, uT, skip_a=False):
        P = self.P
        self.stats(hT, 1024.0, skip_a)
        for kc in range(8):
            P.I("dve", lambda e, kc=kc: e.scalar_tensor_tensor(out=uT[:, kc, :], in0=hT[:, kc, :], scalar=gcol[:, kc:kc + 1],
                                                              in1=self.rstd[:], op0=ALU.mult, op1=ALU.mult),
                w=[uT], r=[hT, gcol, self.rstd])

    def post(self, yo, gcol, hT, skip_a=False):
        P = self.P
        self.stats(yo, 1024.0, skip_a)
        for kc in range(8):
            P.I("dve", lambda e, kc=kc: e.scalar_tensor_tensor(out=yo[:, kc, :], in0=yo[:, kc, :], scalar=gcol[:, kc:kc + 1],
                                                              in1=self.rstd[:], op0=ALU.mult, op1=ALU.mult),
                w=[yo], r=[yo, gcol, self.rstd])
        P.I("pool", lambda e: e.tensor_tensor(out=hT[:], in0=hT[:], in1=yo[:], op=ALU.add), w=[hT], r=[hT, yo])


def hview(ap, g, G=512):
    return ap[:, :, g * G:(g + 1) * G].rearrange("k p t -> p k t")


def load_w(P, W, src, nk, ncols, blk=512, key="W"):
    for cb in range(ncols // blk):
        for k0 in range(0, nk, 8):
            k1 = min(nk, k0 + 8)
            P.D("pool", W[:, k0:k1, cb * blk:(cb + 1) * blk],
                src[k0 * 128:k1 * 128, cb * blk:(cb + 1) * blk].rearrange("(kc p) n -> p kc n", p=128),
                w=[(key, cb)])


def phase_outproj(P, C, li, gi, w_src, nk, y_src, h_src, h_dst, bias_src=None):
    P.begin_phase()
    W = P.sb("W", [128, nk, 1024], BF16)
    load_w(P, W, w_src, nk, 1024)
    gcol = load_gcol(P, C, li, gi)
    nrm = Norm(P)
    bcol = None
    if bias_src is not None:
        bcol = P.sb("bcol", [128, 8], F32)
        P.D("sp", bcol[:], bias_src, w=[bcol])
    yT = _alt(P, "yT", [128, nk, 512], BF16)
    hT = _alt(P, "hT", [128, 8, 512], F32)
    yos = _alt(P, "yo", [128, 8, 512], F32)
    pso = _palt(P, "pso", [128, 512], F32, 3)

    def finish_a(g):
        nrm.stats_a(yos[g % 2])

    def finish(g):
        nrm.post(yos[g % 2], gcol, hT[g % 2], skip_a=True)
        P.D("sp", hview(h_dst, g), hT[g % 2][:], r=[hT[g % 2]], w=[("h", g)])

    def load_y(g):
        P.D("sp", yT[g % 2][:], y_src[:, :, g * 512:(g + 1) * 512].rearrange("k p t -> p k t"), w=[yT[g % 2]], r=[("ysrc", g)])

    load_y(0)
    for g in range(C.NG + 1):
        if g < C.NG:
            y = yT[g % 2]
            h = hT[g % 2]
            yo = yos[g % 2]
            if g + 1 < C.NG:
                load_y(g + 1)
            P.D("sp", h[:], hview(h_src, g), w=[h], r=[("h", g)])
            for oc in range(8):
                ps = pso[oc % 3]
                for kc in range(nk):
                    P.I("pe", lambda e, ps=ps, kc=kc, oc=oc, y=y: e.matmul(ps[:], W[:, kc, oc * 128:(oc + 1) * 128], y[:, kc, :],
                                                                         start=(kc == 0), stop=(kc == nk - 1)),
                        w=[ps], r=[y, ("W", oc // 4)])
                if bcol is None:
                    P.I("act", lambda e, ps=ps, oc=oc, yo=yo: e.activation(out=yo[:, oc, :], in_=ps[:], func=AF.Copy), w=[yo], r=[ps])
                else:
                    P.I("act", lambda e, ps=ps, oc=oc, yo=yo: e.activation(out=yo[:, oc, :], in_=ps[:], func=AF.Identity,
                                                                        bias=bcol[:, oc:oc + 1]), w=[yo], r=[ps, bcol])
                if oc == 0 and g > 0:
                    finish_a(g - 1)
                if oc == 3 and g > 0:
                    finish(g - 1)
        else:
            finish_a(g - 1)
            finish(g - 1)
    P.end_phase()


def phase_mlp_up(P, C, li, h_src):
    P.begin_phase()
    W = P.sb("W", [128, 8, 4096], BF16)
    load_w(P, W, C.mlp_w_up[li], 8, 4096)
    gcol = load_gcol(P, C, li, 2)
    nrm = Norm(P)
    hT = _alt(P, "hT", [128, 8, 512], F32)
    uT = _alt(P, "uT", [128, 8, 512], BF16)
    hid = _alt(P, "hid", [128, 32, 512], BF16)
    psu = _palt(P, "psu", [128, 512], F32, 4)
    rtmp = _alt(P, "rtmp", [128, 512], F32)
    def prep_l(g):
        P.D("sp", hT[g % 2][:], hview(h_src, g), w=[hT[g % 2]], r=[("h", g)])

    def prep_a(g):
        nrm.stats_a(hT[g % 2])

    def prep_b(g):
        nrm.pre(hT[g % 2], gcol, uT[g % 2], skip_a=True)

    prep_l(0)
    prep_a(0)
    prep_b(0)
    for g in range(C.NG):
        u = uT[g % 2]
        hd = hid[g % 2]
        for hc in range(32):
            if hc == 0 and g + 1 < C.NG:
                prep_l(g + 1)
            if hc == 16 and g + 1 < C.NG:
                prep_a(g + 1)
            if hc == 22 and g + 1 < C.NG:
                prep_b(g + 1)
            ps = psu[hc % 4]
            for kc in range(8):
                P.I("pe", lambda e, ps=ps, kc=kc, hc=hc, u=u: e.matmul(ps[:], W[:, kc, hc * 128:(hc + 1) * 128], u[:, kc, :],
                                                                     start=(kc == 0), stop=(kc == 7)),
                    w=[ps], r=[u, ("W", hc // 4)])
            rt = rtmp[hc % 2]
            P.I("act", lambda e, ps=ps, rt=rt: e.activation(out=rt[:], in_=ps[:], func=AF.Relu), w=[rt], r=[ps])
            ve = "dve" if hc % 2 == 0 else "pool"
            P.I(ve, lambda e, rt=rt, hc=hc, hd=hd: e.tensor_tensor(out=hd[:, hc, :], in0=rt[:], in1=rt[:], op=ALU.mult),
                w=[(hd.name, hc)], r=[rt])
        P.D("sp", C.hid[:, :, g * 512:(g + 1) * 512].rearrange("k p t -> p k t"), hd[:],
            r=[(hd.name, hc) for hc in range(32)], w=[("ysrc", g)])
    P.end_phase()


def phase_mlp_down(P, C, li, h_src, h_dst):
    phase_outproj(P, C, li, 3, C.mlp_w_down[li], 32, C.hid, h_src, h_dst)


def ret_lg(P, C, j):
    dl = P.sb("dl", [128, 8], F32)
    lg = P.sb("lg", [128, 8], F32)
    P.D("sp", dl[:], C.ret_dl[j], w=[dl])
    P.I("act", lambda e: e.activation(out=lg[:], in_=dl[:], func=AF.Exp, scale=-1.0), w=[lg], r=[dl])
    P.I("act", lambda e: e.activation(out=lg[:], in_=lg[:], func=AF.Ln, bias=1.0), w=[lg], r=[lg])
    P.I("dve", lambda e: e.tensor_scalar(out=lg[:], in0=lg[:], scalar1=-1.0, scalar2=None, op0=ALU.mult), w=[lg], r=[lg])
    return lg


def rope_tables(P, C, g, K, cos, sin):
    pi_, pf, ang, ki, kf = K["pi"], K["pf"], K["ang"], K["ki"], K["kf"]
    P.D("sp", pi_[:], C.posr[:, g * 512:(g + 1) * 512], w=[pi_])
    P.I("dve", lambda e: e.tensor_copy(out=pf[:], in_=pi_[:]), w=[pf], r=[pi_])
    P.I("dve", lambda e: e.tensor_scalar(out=ang[:], in0=pf[:], scalar1=K["invf"][:, 0:1], scalar2=None, op0=ALU.mult),
        w=[ang], r=[pf, K["invf"]])
    P.I("dve", lambda e: e.tensor_scalar(out=ki[:], in0=ang[:], scalar1=float(1 / (2 * np.pi)), scalar2=None, op0=ALU.mult),
        w=[ki], r=[ang])
    P.I("dve", lambda e: e.tensor_copy(out=kf[:], in_=ki[:]), w=[kf], r=[ki])
    P.I("dve", lambda e: e.scalar_tensor_tensor(out=ang[:], in0=kf[:], scalar=float(-2 * np.pi), in1=ang[:], op0=ALU.mult, op1=ALU.add),
        w=[ang], r=[ang, kf])
    for (dst, shift) in ((sin, 0.0), (cos, PI / 2)):
        if shift != 0.0:
            P.I("dve", lambda e, dst=dst, shift=shift: e.tensor_scalar(out=dst[:], in0=ang[:], scalar1=shift, scalar2=None, op0=ALU.add),
                w=[dst], r=[ang])
            src = dst
        else:
            src = ang
        P.I("dve", lambda e, src=src: e.tensor_scalar(out=kf[:], in0=src[:], scalar1=PI, scalar2=float(2 * np.pi), op0=ALU.is_gt, op1=ALU.mult),
            w=[kf], r=[src])
        P.I("dve", lambda e, src=src, dst=dst: e.tensor_tensor(out=dst[:], in0=src[:], in1=kf[:], op=ALU.subtract), w=[dst], r=[src, kf])
        P.I("act", lambda e, dst=dst: e.activation(out=dst[:], in_=dst[:], func=AF.Sin), w=[dst], r=[dst])


def phase_ret_qk(P, C, li, j, h_src):
    P.begin_phase()
    W = P.sb("W", [128, 8, 2048], BF16)
    load_w(P, W, C.ret_w_in[j][:, 0:2048], 8, 2048)
    gcol = load_gcol(P, C, li, 0)
    nrm = Norm(P)
    idb = make_ident(P)
    cs = load_consts(P, C, ["EF", "EB", "c127", "cs", "invf"])
    lg = ret_lg(P, C, j)
    aF = P.sb("aF", [128, 4, 512], F32)
    aB = P.sb("aB", [128, 4, 512], F32)
    colE = P.sb("colE", [128, 8], F32)
    for h in range(4):
        P.I("act", lambda e, h=h: e.activation(out=aF[:, h, :], in_=cs["EF"][:], func=AF.Exp, scale=lg[:, h:h + 1]), w=[aF], r=[cs["EF"], lg])
        P.I("act", lambda e, h=h: e.activation(out=aB[:, h, :], in_=cs["EB"][:], func=AF.Exp, scale=lg[:, 4 + h:5 + h]), w=[aB], r=[cs["EB"], lg])
        P.I("act", lambda e, h=h: e.activation(out=colE[:, h:h + 1], in_=cs["c127"][:], func=AF.Exp, scale=lg[:, h:h + 1]), w=[colE], r=[cs["c127"], lg])
        P.I("act", lambda e, h=h: e.activation(out=colE[:, 4 + h:5 + h], in_=cs["cs"][:], func=AF.Exp, scale=lg[:, 4 + h:5 + h]), w=[colE], r=[cs["cs"], lg])
    P.I("dve", lambda e: e.tensor_scalar(out=aF[:], in0=aF[:], scalar1=0.0625, scalar2=None, op0=ALU.mult), w=[aF], r=[aF])
    P.I("dve", lambda e: e.tensor_scalar(out=aB[:], in0=aB[:], scalar1=0.0625, scalar2=None, op0=ALU.mult), w=[aB], r=[aB])
    K = {"pi": P.sb("pi", [128, 512], I32), "pf": P.sb("pf", [128, 512], F32), "ang": P.sb("ang", [128, 512], F32),
         "ki": P.sb("ki", [128, 512], I32), "kf": P.sb("kf", [128, 512], F32), "invf": cs["invf"]}
    cos = P.sb("cos", [128, 512], F32)
    sin = P.sb("sin", [128, 512], F32)
    hT = _alt(P, "hT", [128, 8, 512], F32)
    uT = _alt(P, "uT", [128, 8, 512], BF16)
    psq = _palt(P, "psq", [128, 2, 512], F32, 2)
    pst = _palt(P, "pst", [128, 256], BF16, 2)
    t12 = _alt(P, "t12", [128, 2, 512], F32)
    tmp = _alt(P, "tmp", [128, 4, 512], F32)
    rr = _alt(P, "rr", [128, 2, 512], F32)
    kb = _alt(P, "kb", [128, 2, 512], BF16)
    qo = _alt(P, "qo", [128, 4, 512], BF16)
    kE = _alt(P, "kE", [128, 2, 4, 256], BF16)
    it = 0
    def prep_l(g):
        P.D("sp", hT[g % 2][:], hview(h_src, g), w=[hT[g % 2]], r=[("h", g)])

    def prep_a(g):
        nrm.stats_a(hT[g % 2])

    def prep_b(g):
        nrm.pre(hT[g % 2], gcol, uT[g % 2], skip_a=True)

    prep_l(0)
    prep_a(0)
    prep_b(0)
    for g in range(C.NG):
        u = uT[g % 2]
        rope_tables(P, C, g, K, cos, sin)
        for h in range(4):
            for qk in range(2):
                if h == 0 and qk == 0 and g + 1 < C.NG:
                    prep_l(g + 1)
                if h == 2 and qk == 0 and g + 1 < C.NG:
                    prep_a(g + 1)
                if h == 3 and qk == 0 and g + 1 < C.NG:
                    prep_b(g + 1)
                base = qk * 1024 + h * 256
                ps = psq[it % 2]
                t = t12[it % 2]
                tm = tmp[it % 2]
                r_ = rr[it % 2]
                ve = "pool" if it % 3 == 2 else "dve"
                it += 1
                for c in range(2):
                    for kc in range(8):
                        P.I("pe", lambda e, ps=ps, c=c, kc=kc, base=base, u=u: e.matmul(ps[:, c, :], W[:, kc, base + c * 128:base + (c + 1) * 128],
                                                                                     u[:, kc, :], start=(kc == 0), stop=(kc == 7)),
                            w=[ps], r=[u, ("W", base // 512)])
                P.I("act", lambda e, ps=ps, t=t: e.activation(out=t[:], in_=ps[:], func=AF.Copy), w=[t], r=[ps])
                P.I(ve, lambda e, t=t, tm=tm: e.tensor_tensor(out=tm[:, 0, :], in0=t[:, 0, :], in1=cos[:], op=ALU.mult), w=[tm], r=[t, cos])
                P.I(ve, lambda e, t=t, tm=tm: e.tensor_tensor(out=tm[:, 1, :], in0=t[:, 1, :], in1=sin[:], op=ALU.mult), w=[tm], r=[t, sin])
                P.I(ve, lambda e, t=t, tm=tm: e.tensor_tensor(out=tm[:, 2, :], in0=t[:, 1, :], in1=cos[:], op=ALU.mult), w=[tm], r=[t, cos])
                P.I(ve, lambda e, t=t, tm=tm: e.tensor_tensor(out=tm[:, 3, :], in0=t[:, 0, :], in1=sin[:], op=ALU.mult), w=[tm], r=[t, sin])
                if qk == 0:
                    q = qo[h % 2]
                    P.I(ve, lambda e, tm=tm, r_=r_: e.tensor_tensor(out=r_[:, 0, :], in0=tm[:, 0, :], in1=tm[:, 1, :], op=ALU.subtract), w=[r_], r=[tm])
                    P.I(ve, lambda e, tm=tm, r_=r_: e.tensor_tensor(out=r_[:, 1, :], in0=tm[:, 2, :], in1=tm[:, 3, :], op=ALU.add), w=[r_], r=[tm])
                    for c in range(2):
                        P.I(ve, lambda e, r_=r_, q=q, c=c, h=h: e.tensor_tensor(out=q[:, c, :], in0=r_[:, c, :], in1=aF[:, h, :], op=ALU.mult), w=[q], r=[r_, aF])
                        P.I(ve, lambda e, r_=r_, q=q, c=c, h=h: e.tensor_tensor(out=q[:, 2 + c, :], in0=r_[:, c, :], in1=aB[:, h, :], op=ALU.mult), w=[q], r=[r_, aB])
                    P.D("sp", C.qF[h * 2:h * 2 + 2, :, g * 512:(g + 1) * 512].rearrange("c p t -> p c t"), q[:, 0:2, :], r=[q], w=[("qF", g, h)])
                    P.D("sp", C.qB[h * 2:h * 2 + 2, :, g * 512:(g + 1) * 512].rearrange("c p t -> p c t"), q[:, 2:4, :], r=[q], w=[("qB", g, h)])
                else:
                    k = kb[h % 2]
                    ke = kE[h % 2]
                    P.I(ve, lambda e, tm=tm, k=k: e.tensor_tensor(out=k[:, 0, :], in0=tm[:, 0, :], in1=tm[:, 1, :], op=ALU.subtract), w=[k], r=[tm])
                    P.I(ve, lambda e, tm=tm, k=k: e.tensor_tensor(out=k[:, 1, :], in0=tm[:, 2, :], in1=tm[:, 3, :], op=ALU.add), w=[k], r=[tm])
                    P.D("sp", C.kT[h * 2:h * 2 + 2, :, g * 512:(g + 1) * 512].rearrange("c p t -> p c t"), k[:], r=[k], w=[("kT", g, h)])
                    for tt in range(4):
                        pt = pst[tt % 2]
                        for c in range(2):
                            P.I("pe", lambda e, pt=pt, k=k, c=c, tt=tt: e.transpose(out=pt[:, c * 128:(c + 1) * 128], in_=k[:, c, tt * 128:(tt + 1) * 128],
                                                                                 identity=idb[:]), w=[pt], r=[k, idb])
                        P.I("act", lambda e, pt=pt, ke=ke, tt=tt, h=h: e.activation(out=ke[:, 0, tt, :], in_=pt[:], func=AF.Copy, scale=colE[:, h:h + 1]),
                            w=[ke], r=[pt, colE])
                        P.I("act", lambda e, pt=pt, ke=ke, tt=tt, h=h: e.activation(out=ke[:, 1, tt, :], in_=pt[:], func=AF.Copy, scale=colE[:, 4 + h:5 + h]),
                            w=[ke], r=[pt, colE])
                    rows = slice(g * 512, (g + 1) * 512)
                    P.D("sp", C.kEF[rows, h * 256:(h + 1) * 256].rearrange("(tt p) n -> p tt n", p=128), ke[:, 0, :, :], r=[ke], w=[("kEF", g, h)])
                    P.D("sp", C.kEB[rows, h * 256:(h + 1) * 256].rearrange("(tt p) n -> p tt n", p=128), ke[:, 1, :, :], r=[ke], w=[("kEB", g, h)])
    P.end_phase()


def phase_vg(P, C, li, h_src, w_src, ncols_v, v_dst, g_dst, G=None):
    P.begin_phase()
    nc_ = 2 * ncols_v
    W = P.sb("W", [128, 8, nc_], BF16)
    load_w(P, W, w_src, 8, nc_)
    gcol = load_gcol(P, C, li, 0)
    nrm = Norm(P)
    hT = _alt(P, "hT", [128, 8, 512], F32)
    uT = _alt(P, "uT", [128, 8, 512], BF16)
    vo = _alt(P, "vo", [128, nc_], BF16)
    psv = _palt(P, "psv", [128, 512], F32, 4)
    it = 0
    def prep_l(g):
        P.D("sp", hT[g % 2][:], hview(h_src, g), w=[hT[g % 2]], r=[("h", g)])

    def prep_a(g):
        nrm.stats_a(hT[g % 2])

    def prep_b(g):
        nrm.pre(hT[g % 2], gcol, uT[g % 2], skip_a=True)

    prep_l(0)
    prep_a(0)
    prep_b(0)
    for g in range(C.NG):
        u = uT[g % 2]
        for tt in range(4):
            if tt == 0 and g + 1 < C.NG:
                prep_l(g + 1)
            if tt == 2 and g + 1 < C.NG:
                prep_a(g + 1)
            if tt == 3 and g + 1 < C.NG:
                prep_b(g + 1)
            o = vo[tt % 2]
            for cb in range(nc_ // 512):
                ps = psv[it % 4]
                it += 1
                for kc in range(8):
                    P.I("pe", lambda e, ps=ps, kc=kc, tt=tt, cb=cb, u=u: e.matmul(ps[:], u[:, kc, tt * 128:(tt + 1) * 128], W[:, kc, cb * 512:(cb + 1) * 512],
                                                                               start=(kc == 0), stop=(kc == 7)), w=[ps], r=[u, ("W", cb)])
                fn = AF.Copy if cb * 512 < ncols_v else AF.Silu
                P.I("act", lambda e, ps=ps, o=o, cb=cb, fn=fn: e.activation(out=o[:, cb * 512:(cb + 1) * 512], in_=ps[:], func=fn), w=[o], r=[ps])
            rows = slice(g * 512 + tt * 128, g * 512 + (tt + 1) * 128)
            P.D("sp", v_dst[rows, :], o[:, 0:ncols_v], r=[o], w=[("v", g, tt)])
            P.D("sp", g_dst[rows, :], o[:, ncols_v:nc_], r=[o], w=[("sg", g, tt)])
    P.end_phase()


def exchange_states(P, C, S, nhc, DV, xin_t=None, xout_t=None):
    rows = nhc * 128
    xin_t = C.xin if xin_t is None else xin_t
    xout_t = C.xout if xout_t is None else xout_t
    P.D("sp", xin_t.rearrange("(k p) n -> p k n", p=128), S[:, 0:nhc, 0:DV],
        r=[("S", i) for i in range(nhc)], w=[("xin",)])
    xin = xin_t
    xout = xout_t
    P.Dfn('pool', lambda e: e.collective_compute("AllGather", ALU.bypass,
                                          replica_groups=[[0, 1], [2, 3], [4, 5], [6, 7]],
                                          ins=[xin], outs=[xout]), r=[("xin",)], w=[("xout",)])


def sweep(P, C, cfg, second):
    H, NC, DV = cfg["H"], cfg["NC"], cfg["DV"]
    HC = H * NC
    NG = C.NG
    pairs = cfg["pairs"] if not second else []
    npair = len(pairs)
    S = P.sb("S", [128, HC, DV], F32)
    Sb = P.sb("Sb", [128, HC, DV], BF16)
    idb = cfg["idb"]
    P.I("dve", lambda e: e.memset(S[:], 0.0), w=[("S", i) for i in range(HC)])
    P.I("pool", lambda e: e.memset(Sb[:], 0.0), w=[("Sb", i) for i in range(HC)])
    Qg = _alt(P, "Qg", [128, HC, 512], BF16)
    Kg = [_alt(P, "Kg%d" % i, [128, HC, 512], BF16) for i in range(npair)]
    Q2g = [_alt(P, "Q2g%d" % i, [128, HC, 512], BF16) for i in range(1, npair)]
    KEg = _alt(P, "KEg", [128, 4, HC * 128], BF16)
    Vg = _alt(P, "Vg", [128, 4, H * DV], BF16)
    if second:
        P1g = _alt(P, "P1g", [128, 4, H * DV], BF16)
    else:
        P1t = _alt(P, "P1t", [128, H * DV], BF16)
    psS = _palt(P, "psS", [128, 128], F32, 2) if not second else None
    psO = _palt(P, "psO", [128, DV], F32, 2)
    psU = _palt(P, "psU", [128, DV], F32, 4)
    scm = _alt(P, "scm", [128, 128], BF16)
    sct = _alt(P, "sct", [128, 128], F32)
    sct2 = _alt(P, "sct2", [128, 128], F32)
    order = list(range(NG)) if not second else list(reversed(range(NG)))

    def loads(gi):
        g = order[gi]
        b = gi % 2
        tsl = slice(g * 512, (g + 1) * 512)
        P.D("sp", Qg[b][:], cfg["Q"][:, :, tsl].rearrange("k p t -> p k t"), w=[Qg[b]], r=[("q",)])
        for i, (K_ap, Q_ap, mk) in enumerate(pairs):
            P.D("sp", Kg[i][b][:], K_ap[:, :, tsl].rearrange("k p t -> p k t"), w=[Kg[i][b]])
            if i > 0:
                P.D("sp", Q2g[i - 1][b][:], Q_ap[:, :, tsl].rearrange("k p t -> p k t"), w=[Q2g[i - 1][b]])
        P.D("sp", KEg[b][:], cfg["KE"][tsl, :].rearrange("(tt p) n -> p tt n", p=128), w=[KEg[b]])
        P.D("sp", Vg[b][:], cfg["V"][tsl, :].rearrange("(tt p) n -> p tt n", p=128), w=[Vg[b]])
        if second:
            P.D("sp", P1g[b][:], cfg["P1"][tsl, :].rearrange("(tt p) n -> p tt n", p=128), w=[P1g[b]])

    steps = []
    for gi, g in enumerate(order):
        tts = list(range(4)) if not second else [3, 2, 1, 0]
        for tt in tts:
            for h in range(H):
                steps.append((gi, g, tt, h))

    def stageA(i):
        gi, g, tt, h = steps[i]
        b = gi % 2
        csl = slice(tt * 128, (tt + 1) * 128)
        sm = scm[i % 2]
        for pi_, (K_ap, Q_ap, mk) in enumerate(pairs):
            pS = psS[pi_ % 2] if npair > 1 else psS[i % 2]
            qq = Qg[b] if pi_ == 0 else Q2g[pi_ - 1][b]
            for c in range(NC):
                P.I("pe", lambda e, pS=pS, kk=Kg[pi_][b], qq=qq, hc=h * NC + c, c=c, csl=csl:
                    e.matmul(pS[:], kk[:, hc, csl], qq[:, hc, csl], start=(c == 0), stop=(c == NC - 1)), w=[pS], r=[Kg[pi_][b], qq])
            mt = cfg["M"][h] if mk is None else cfg["masks"][mk]
            if pi_ == 0 and npair == 1:
                P.I("dve", lambda e, pS=pS, sm=sm, mt=mt: e.tensor_tensor(out=sm[:], in0=pS[:], in1=mt[:], op=ALU.mult), w=[sm], r=[pS, mt])
            elif pi_ == 0:
                st_ = sct[i % 2]
                P.I("dve", lambda e, pS=pS, st_=st_, mt=mt: e.tensor_tensor(out=st_[:], in0=pS[:], in1=mt[:], op=ALU.mult), w=[st_], r=[pS, mt])
            else:
                st_ = sct[i % 2]
                st2 = sct2[i % 2]
                P.I("dve", lambda e, pS=pS, st2=st2, mt=mt: e.tensor_tensor(out=st2[:], in0=pS[:], in1=mt[:], op=ALU.mult), w=[st2], r=[pS, mt])
                P.I("pool", lambda e, st_=st_, st2=st2, sm=sm: e.tensor_tensor(out=sm[:], in0=st_[:], in1=st2[:], op=ALU.add), w=[sm], r=[st_, st2])

    pending = []
    loads(0)
    if not second:
        stageA(0)
    for i, (gi, g, tt, h) in enumerate(steps):
        b = gi % 2
        n = g * 4 + tt
        csl = slice(tt * 128, (tt + 1) * 128)
        first_of_group = (i % (4 * H) == 0)
        last_of_group = (i % (4 * H) == 4 * H - 1)
        if first_of_group:
            if gi + 1 < NG:
                loads(gi + 1)
            if second:
                cfg["epi_load"](g, b)
        if not second and i + 1 < len(steps):
            stageA(i + 1)
        po = psO[i % 2]
        vsl = Vg[b][:, tt, h * DV:(h + 1) * DV]
        if not second:
            sm = scm[i % 2]
            P.I("pe", lambda e, po=po, sm=sm, vsl=vsl: e.matmul(po[:], sm[:], vsl, start=True, stop=False), w=[po], r=[sm, Vg[b]])
        else:
            P.I("pe", lambda e, po=po, b=b, tt=tt, h=h: e.matmul(po[:], idb[:], P1g[b][:, tt, h * DV:(h + 1) * DV], start=True, stop=False),
                w=[po], r=[idb, P1g[b]])
        for c in range(NC):
            hc = h * NC + c
            P.I("pe", lambda e, po=po, b=b, hc=hc, c=c, csl=csl: e.matmul(po[:], Qg[b][:, hc, csl], Sb[:, hc, :], start=False, stop=(c == NC - 1)),
                w=[po], r=[Qg[b], ("Sb", hc)])
        if not second:
            pt = P1t[n % 2]
            P.I("act", lambda e, po=po, pt=pt, h=h: e.activation(out=pt[:, h * DV:(h + 1) * DV], in_=po[:], func=AF.Copy), w=[pt], r=[po])
        else:
            tail = cfg["epi"](g, b, tt, h, po)
            for t_ in pending:
                t_()
            pending = [tail]
        for c in range(NC):
            hc = h * NC + c
            pu = psU[(i * NC + c) % 4]
            P.I("pe", lambda e, pu=pu, b=b, tt=tt, hc=hc, vsl=vsl: e.matmul(pu[:], KEg[b][:, tt, hc * 128:(hc + 1) * 128], vsl, start=True, stop=True),
                w=[pu], r=[KEg[b], Vg[b]])
            dcol = cfg["dec"](h, c, n)
            P.I("dve", lambda e, pu=pu, hc=hc, dcol=dcol: e.scalar_tensor_tensor(out=S[:, hc, :], in0=S[:, hc, :], scalar=dcol, in1=pu[:],
                                                                             op0=ALU.mult, op1=ALU.add),
                w=[("S", hc)], r=[("S", hc), pu, cfg["dec_res"]])
            P.I("pool", lambda e, hc=hc: e.tensor_copy(out=Sb[:, hc, :], in_=S[:, hc, :]), w=[("Sb", hc)], r=[("S", hc)])
        if not second and h == H - 1:
            P.D("sp", cfg["P1"][n * 128:(n + 1) * 128, :], P1t[n % 2][:], r=[P1t[n % 2]], w=[("P1", n)])
        if second and last_of_group:
            for t_ in pending:
                t_()
            pending = []
            cfg["epi_store"](g, b)
    return S


def phase_ret_sweep1(P, C, j):
    P.begin_phase()
    cs = load_consts(P, C, ["maskLF", "maskLB", "A1", "A2", "A3", "c128"])
    lg = ret_lg(P, C, j)
    M = {}
    for h in range(4):
        m = P.sb("M%d" % h, [128, 128], F32)
        t2 = P.sb("Mt%d" % h, [128, 128], F32)
        lf = lg[:, h:h + 1]
        lb = lg[:, 4 + h:5 + h]
        P.I("act", lambda e, m=m, lf=lf: e.activation(out=m[:], in_=cs["A1"][:], func=AF.Exp, scale=lf), w=[m], r=[cs["A1"], lg])
        P.I("dve", lambda e, m=m: e.tensor_tensor(out=m[:], in0=m[:], in1=cs["maskLF"][:], op=ALU.mult), w=[m], r=[m, cs["maskLF"]])
        P.I("dve", lambda e, t2=t2, lb=lb: e.tensor_scalar(out=t2[:], in0=cs["A2"][:], scalar1=lb, scalar2=None, op0=ALU.mult), w=[t2], r=[cs["A2"], lg])
        P.I("dve", lambda e, t2=t2, lf=lf: e.scalar_tensor_tensor(out=t2[:], in0=cs["A3"][:], scalar=lf, in1=t2[:], op0=ALU.mult, op1=ALU.add),
            w=[t2], r=[t2, cs["A3"], lg])
        P.I("act", lambda e, t2=t2: e.activation(out=t2[:], in_=t2[:], func=AF.Exp), w=[t2], r=[t2])
        P.I("dve", lambda e, t2=t2: e.tensor_tensor(out=t2[:], in0=t2[:], in1=cs["maskLB"][:], op=ALU.mult), w=[t2], r=[t2, cs["maskLB"]])
        P.I("dve", lambda e, m=m, t2=t2: e.tensor_tensor(out=m[:], in0=m[:], in1=t2[:], op=ALU.add), w=[m], r=[m, t2])
        M[h] = m
    decc = P.sb("decc", [128, 4], F32)
    for h in range(4):
        P.I("act", lambda e, h=h: e.activation(out=decc[:, h:h + 1], in_=cs["c128"][:], func=AF.Exp, scale=lg[:, h:h + 1]), w=[decc], r=[cs["c128"], lg])
    cfg = dict(H=4, NC=2, DV=512, pairs=[(C.kT, C.qF, None)], Q=C.qF, KE=C.kEF, V=C.v, P1=C.P1,
               dec=lambda h, c, n: decc[:, h:h + 1], dec_res=decc, M=M, idb=None)
    S = sweep(P, C, cfg, False)
    P.end_phase()


def make_epilogue(P, C, H, DV, sg_src, y_dst, gain_bc=None):
    idb = make_ident(P)
    NF = DV // 128
    SGg = P.sb("SGg", [128, 4, H * DV], BF16)
    yTg = P.sb("yTg", [128, H * NF, 512], BF16)
    yt = _alt(P, "yt", [128, DV], BF16)
    yf = _alt(P, "yf", [128, DV], F32)
    ssc = P.sb("ssc", [128, 8], F32)
    junk = P.sb("junk", [128, DV], F32)
    psT = _palt(P, "psT", [128, DV], BF16, 2)
    st = {"k": 0}

    def epi_load(g, b):
        P.D("sp", SGg[:], sg_src[g * 512:(g + 1) * 512, :].rearrange("(tt p) n -> p tt n", p=128), w=[SGg])

    def epi(g, b, tt, h, po):
        k = st["k"]
        st["k"] += 1
        s = ssc[:, k % 8:k % 8 + 1]
        key = ("ssc", k % 8)
        y = yt[k % 2]
        pT = psT[k % 2]
        P.I("dve", lambda e: e.memset(s, 0.0), w=[key])
        P.I("act", lambda e: e.activation(out=junk[:], in_=po[:], func=AF.Square, accum_out=s), w=[junk, key], r=[po])
        P.I("act", lambda e: e.activation(out=s, in_=s, func=AF.Sqrt, scale=1.0 / DV, bias=EPS), w=[key], r=[key])
        P.I("dve", lambda e: e.reciprocal(out=s, in_=s), w=[key], r=[key])
        if gain_bc is None:
            P.I("dve", lambda e: e.scalar_tensor_tensor(out=y[:], in0=po[:], scalar=s, in1=SGg[:, tt, h * DV:(h + 1) * DV], op0=ALU.mult, op1=ALU.mult),
                w=[y], r=[po, key, SGg])
        else:
            f = yf[k % 2]
            P.I("dve", lambda e: e.scalar_tensor_tensor(out=f[:], in0=po[:], scalar=s, in1=gain_bc[:], op0=ALU.mult, op1=ALU.mult),
                w=[f], r=[po, key, gain_bc])
            P.I("pool", lambda e: e.tensor_tensor(out=y[:], in0=f[:], in1=SGg[:, tt, h * DV:(h + 1) * DV], op=ALU.mult), w=[y], r=[f, SGg])
        def tail():
            for q4 in range(NF):
                P.I("pe", lambda e, q4=q4: e.transpose(out=pT[:, q4 * 128:(q4 + 1) * 128], in_=y[:, q4 * 128:(q4 + 1) * 128], identity=idb[:]),
                    w=[pT], r=[y, idb])
            P.I("act", lambda e: e.activation(out=yTg[:, h * NF:(h + 1) * NF, tt * 128:(tt + 1) * 128],
                                              in_=pT[:].rearrange("p (q t) -> p q t", q=NF), func=AF.Copy), w=[yTg], r=[pT])
        return tail

    def epi_store(g, b):
        P.D("sp", y_dst[:, :, g * 512:(g + 1) * 512].rearrange("k p t -> p k t"), yTg[:], r=[yTg], w=[("ysrc", g)])

    return idb, epi_load, epi, epi_store


def phase_ret_sweep2(P, C, j):
    P.begin_phase()
    cs = load_consts(P, C, ["c128", "sel"])
    lg = ret_lg(P, C, j)
    decc = P.sb("decc", [128, 4], F32)
    for h in range(4):
        P.I("act", lambda e, h=h: e.activation(out=decc[:, h:h + 1], in_=cs["c128"][:], func=AF.Exp, scale=lg[:, 4 + h:5 + h]), w=[decc], r=[cs["c128"], lg])
    idb, epi_load, epi, epi_store = make_epilogue(P, C, 4, 512, C.sg, C.yT)
    cfg = dict(H=4, NC=2, DV=512, pairs=[], Q=C.qB, KE=C.kEB, V=C.v, P1=C.P1,
               dec=lambda h, c, n: decc[:, h:h + 1], dec_res=decc, idb=idb, sel=cs["sel"],
               epi_load=epi_load, epi=epi, epi_store=epi_store)
    sweep(P, C, cfg, True)
    P.end_phase()


def layer_ret(P, C, li, j, h_src, h_dst):
    phase_ret_qk(P, C, li, j, h_src)
    phase_vg(P, C, li, h_src, C.ret_w_in[j][:, 2048:6144], 2048, C.v, C.sg)
    phase_ret_sweep1(P, C, j)
    phase_ret_sweep2(P, C, j)
    phase_outproj(P, C, li, 1, C.ret_w_out[j], 16, C.yT, h_src, h_dst)


def layer_mlp(P, C, li, h_src, h_dst):
    phase_mlp_up(P, C, li, h_src)
    phase_mlp_down(P, C, li, h_src, h_dst)


def phase_gla_qk(P, C, li, h_src):
    P.begin_phase()
    NCH = C.NCH
    W = P.sb("W", [128, 8, 1024], BF16)
    load_w(P, W, C.gla_w_in[0][:, 0:1024], 8, 1024)
    w1 = P.sb("w1", [128, 2, 8, 16], BF16)
    for d in range(2):
        P.D("pool", w1[:, d, :, :], C.gla_w1[d].rearrange("(kc p) r -> p kc r", p=128), w=[w1])
    w2a = P.sb("w2a", [17, 2, 512], F32)
    for d in range(2):
        P.D("sp", w2a[0:16, d, :], C.gla_w2[d], w=[w2a])
        P.D("sp", w2a[16:17, d, :], C.gla_b[d:d + 1, :], w=[w2a])
    gcol = load_gcol(P, C, li, 0)
    nrm = Norm(P)
    cs = load_consts(P, C, ["triU", "triL", "sL", "sU"])
    tri = [cs["triU"], cs["triL"]]
    sX = [cs["sL"], cs["sU"]]
    DEC = [P.sb("DEC%d" % d, [128, 4, NCH], F32) for d in range(2)]
    hT = _alt(P, "hT", [128, 8, 512], F32)
    uT = _alt(P, "uT", [128, 8, 512], BF16)
    qs = P.sb("qs", [128, 4, 512], F32)
    ks = P.sb("ks", [128, 4, 512], F32)
    zTa = [P.sb("zTa%d" % d, [17, 512], F32) for d in range(2)]
    for d in range(2):
        P.I("dve", lambda e, d=d: e.memset(zTa[d][:], 1.0), w=[zTa[d]])
    lgt = [P.sb("lgt%d" % d, [128, 4, 512], F32) for d in range(2)]
    QX = [P.sb("QX%d" % d, [128, 4, 512], BF16) for d in range(2)]
    KX = [P.sb("KX%d" % d, [128, 4, 512], BF16) for d in range(2)]
    kE = [P.sb("kE%d" % d, [128, 4, 512], BF16) for d in range(2)]
    EQ = _alt(P, "EQ", [128, 4, 128], F32)
    EK = _alt(P, "EK", [128, 4, 128], F32)
    EE = _alt(P, "EE", [128, 512], F32)
    psq = _palt(P, "psq", [128, 512], F32, 2)
    psz = P.ps("psz", [16, 512], F32)
    psl = P.ps("psl", [128, 512], F32)
    psc = _palt(P, "psc", [128, 4, 128], F32, 2)
    pse = P.ps("pse", [128, 512], F32)
    QXd = [C.gQF, C.gQB]
    KXd = [C.gKF, C.gKB]
    kEd = [C.gkEF, C.gkEB]
    it = 0
    def prep_l(g):
        P.D("sp", hT[g % 2][:], hview(h_src, g), w=[hT[g % 2]], r=[("h", g)])

    def prep_a(g):
        nrm.stats_a(hT[g % 2])

    def prep_b(g):
        nrm.pre(hT[g % 2], gcol, uT[g % 2], skip_a=True)

    prep_l(0)
    prep_a(0)
    prep_b(0)
    for g in range(C.NG):
        u = uT[g % 2]
        tsl = slice(g * 512, (g + 1) * 512)
        for qk in range(2):
            dst = qs if qk == 0 else ks
            for h in range(4):
                ps = psq[h % 2]
                col = qk * 512 + h * 128
                for kc in range(8):
                    P.I("pe", lambda e, ps=ps, kc=kc, col=col, u=u: e.matmul(ps[:], W[:, kc, col:col + 128], u[:, kc, :], start=(kc == 0), stop=(kc == 7)),
                        w=[ps], r=[u, ("W", col // 512)])
                sc = float(128 ** -0.5) if qk == 0 else 1.0
                P.I("act", lambda e, ps=ps, dst=dst, h=h, sc=sc: e.activation(out=dst[:, h, :], in_=ps[:], func=AF.Copy, scale=sc), w=[dst], r=[ps])
        for d in range(2):
            for kc in range(8):
                P.I("pe", lambda e, d=d, kc=kc, u=u: e.matmul(psz[:], w1[:, d, kc, :], u[:, kc, :], start=(kc == 0), stop=(kc == 7)), w=[psz], r=[u, w1])
            P.I("act", lambda e, d=d: e.activation(out=zTa[d][0:16, :], in_=psz[:], func=AF.Copy), w=[zTa[d]], r=[psz])
            for tt in range(4):
                P.I("pe", lambda e, d=d, tt=tt: e.matmul(psl[:], zTa[d][:, tt * 128:(tt + 1) * 128], w2a[:, d, :], start=True, stop=True), w=[psl], r=[zTa[d], w2a])
                P.I("act", lambda e, d=d, tt=tt: e.activation(out=lgt[d][:, tt, :], in_=psl[:], func=AF.Exp, scale=-1.0), w=[lgt[d]], r=[psl])
            P.I("act", lambda e, d=d: e.activation(out=lgt[d][:], in_=lgt[d][:], func=AF.Ln, bias=1.0), w=[lgt[d]], r=[lgt[d]])
            P.I("dve", lambda e, d=d: e.tensor_scalar(out=lgt[d][:], in0=lgt[d][:], scalar1=-1.0 / 16.0, scalar2=None, op0=ALU.mult), w=[lgt[d]], r=[lgt[d]])
        for tt in range(4):
            if tt == 0 and g + 1 < C.NG:
                prep_l(g + 1)
            if tt == 2 and g + 1 < C.NG:
                prep_a(g + 1)
            if tt == 3 and g + 1 < C.NG:
                prep_b(g + 1)
            n = g * 4 + tt
            csl = slice(tt * 128, (tt + 1) * 128)
            for kc in range(8):
                P.I("pe", lambda e, kc=kc, csl=csl, u=u: e.matmul(pse[:], u[:, kc, csl], W[:, kc, 512:1024], start=(kc == 0), stop=(kc == 7)), w=[pse], r=[u, ("W", 1)])
            ktm = EE[0]
            P.I("act", lambda e, ktm=ktm: e.activation(out=ktm[:], in_=pse[:], func=AF.Copy), w=[ktm], r=[pse])
            for d in range(2):
                pc = psc[it % 2]
                eq = EQ[it % 2]
                ek = EK[it % 2]
                it += 1
                for h in range(4):
                    P.I("pe", lambda e, pc=pc, d=d, h=h, tt=tt: e.matmul(pc[:, h, :], lgt[d][:, tt, h * 128:(h + 1) * 128], tri[d][:, 0:128], start=True, stop=True),
                        w=[pc], r=[lgt[d], tri[d]])
                P.I("act", lambda e, pc=pc, eq=eq: e.activation(out=eq[:], in_=pc[:], func=AF.Exp), w=[eq], r=[pc])
                P.I("act", lambda e, pc=pc, ek=ek: e.activation(out=ek[:], in_=pc[:], func=AF.Exp, scale=-1.0), w=[ek], r=[pc])
                tcol = 127 if d == 0 else 0
                P.I("pool", lambda e, eq=eq, d=d, n=n, tcol=tcol: e.tensor_copy(out=DEC[d][:, :, n:n + 1], in_=eq[:, :, tcol:tcol + 1]), w=[DEC[d]], r=[eq])
                P.I("dve", lambda e, eq=eq, d=d, csl=csl: e.tensor_tensor(out=QX[d][:, :, csl], in0=qs[:, :, csl], in1=eq[:], op=ALU.mult), w=[QX[d]], r=[qs, eq])
                P.I("pool", lambda e, ek=ek, d=d, csl=csl: e.tensor_tensor(out=KX[d][:, :, csl], in0=ks[:, :, csl], in1=ek[:], op=ALU.mult), w=[KX[d]], r=[ks, ek])
                P.I("pe", lambda e, d=d, tt=tt: e.matmul(psl[:], sX[d][:], lgt[d][:, tt, :], start=True, stop=True), w=[psl], r=[sX[d], lgt[d]])
                ee = EE[1]
                P.I("act", lambda e, ee=ee: e.activation(out=ee[:], in_=psl[:], func=AF.Exp), w=[ee], r=[psl])
                P.I("dve", lambda e, ee=ee, d=d, tt=tt, ktm=ktm: e.tensor_tensor(out=kE[d][:, tt, :], in0=ktm[:], in1=ee[:], op=ALU.mult), w=[kE[d]], r=[ktm, ee])
        for d in range(2):
            P.D("sp", QXd[d][:, :, tsl].rearrange("k p t -> p k t"), QX[d][:], r=[QX[d]], w=[("gq", d, g)])
            P.D("sp", KXd[d][:, :, tsl].rearrange("k p t -> p k t"), KX[d][:], r=[KX[d]], w=[("gk", d, g)])
            P.D("sp", kEd[d][tsl, :].rearrange("(tt p) n -> p tt n", p=128), kE[d][:], r=[kE[d]], w=[("gke", d, g)])
    for d in range(2):
        P.D("sp", C.gDEC[d], DEC[d][:].rearrange("p h n -> p (h n)"), r=[DEC[d]], w=[("gdec", d)])
    P.end_phase()


def phase_gla_sweep1(P, C):
    P.begin_phase()
    cs = load_consts(P, C, ["maskLF", "maskLB"])
    DECt = P.sb("DECt", [128, 4 * C.NCH], F32)
    P.D("sp", DECt[:], C.gDEC[0], w=[DECt])
    NCH = C.NCH
    cfg = dict(H=4, NC=1, DV=256, pairs=[(C.gKF, C.gQF, "maskLF"), (C.gKB, C.gQB, "maskLB")], Q=C.gQF, KE=C.gkEF, V=C.gv, P1=C.gP1,
               dec=lambda h, c, n: DECt[:, h * NCH + n:h * NCH + n + 1], dec_res=DECt, masks=cs, idb=None, xin=C.xin2, xout=C.xout2)
    S = sweep(P, C, cfg, False)
    P.end_phase()


def phase_gla_sweep2(P, C):
    P.begin_phase()
    cs = load_consts(P, C, ["sel"])
    NCH = C.NCH
    DECt = P.sb("DECt", [128, 4 * NCH], F32)
    P.D("sp", DECt[:], C.gDEC[1], w=[DECt])
    gbc = P.sb("gbc", [128, 256], F32)
    P.D("sp", gbc[:], C.gla_ng, w=[gbc])
    idb, epi_load, epi, epi_store = make_epilogue(P, C, 4, 256, C.gsg, C.gyT, gain_bc=gbc)
    cfg = dict(H=4, NC=1, DV=256, pairs=[], Q=C.gQB, KE=C.gkEB, V=C.gv, P1=C.gP1,
               dec=lambda h, c, n: DECt[:, h * NCH + n:h * NCH + n + 1], dec_res=DECt, idb=idb, sel=cs["sel"],
               epi_load=epi_load, epi=epi, epi_store=epi_store, xin=C.xin2, xout=C.xout2)
    sweep(P, C, cfg, True)
    P.end_phase()


def layer_gla(P, C, li, h_src, h_dst):
    phase_gla_qk(P, C, li, h_src)
    phase_vg(P, C, li, h_src, C.gla_w_in[0][:, 1024:3072], 1024, C.gv, C.gsg)
    phase_gla_sweep1(P, C)
    phase_gla_sweep2(P, C)
    phase_outproj(P, C, li, 1, C.gla_w_out[0], 8, C.gyT, h_src, h_dst)


def phase_conv_glu(P, C, li, h_src):
    P.begin_phase()
    T = C.T
    W = P.sb("W", [128, 8, 2048], BF16)
    load_w(P, W, C.conv_w_in[0], 8, 2048)
    gcol = load_gcol(P, C, li, 0)
    nrm = Norm(P)
    bin_ = P.sb("bin", [128, 16], F32)
    P.D("sp", bin_[:], C.conv_bin, w=[bin_])
    hT = _alt(P, "hT", [128, 8, 512], F32)
    uT = _alt(P, "uT", [128, 8, 512], BF16)
    hg = _alt(P, "hg", [128, 8, 512], BF16)
    sig = _alt(P, "sig", [128, 512], F32)
    psa = _palt(P, "psa", [128, 512], F32, 2)
    psg = _palt(P, "psg", [128, 512], F32, 2)
    zt = P.sb("zt", [128, 8, 16], BF16)
    P.I("dve", lambda e: e.memset(zt[:], 0.0), w=[zt])
    P.D("sp", C.cHG[:, :, 0:16].rearrange("k p t -> p k t"), zt[:], r=[zt], w=[("hgpad",)])
    def prep_l(g):
        P.D("sp", hT[g % 2][:], hview(h_src, g), w=[hT[g % 2]], r=[("h", g)])

    def prep_a(g):
        nrm.stats_a(hT[g % 2])

    def prep_b(g):
        nrm.pre(hT[g % 2], gcol, uT[g % 2], skip_a=True)

    prep_l(0)
    prep_a(0)
    prep_b(0)
    for g in range(C.NG):
        u = uT[g % 2]
        o = hg[g % 2]
        for fc in range(8):
            if fc == 0 and g + 1 < C.NG:
                prep_l(g + 1)
            if fc == 4 and g + 1 < C.NG:
                prep_a(g + 1)
            if fc == 6 and g + 1 < C.NG:
                prep_b(g + 1)
            pa = psa[fc % 2]
            pg = psg[fc % 2]
            sg_ = sig[fc % 2]
            for kc in range(8):
                P.I("pe", lambda e, pa=pa, kc=kc, fc=fc, u=u: e.matmul(pa[:], W[:, kc, fc * 128:(fc + 1) * 128], u[:, kc, :], start=(kc == 0), stop=(kc == 7)),
                    w=[pa], r=[u, ("W", fc // 4)])
            for kc in range(8):
                P.I("pe", lambda e, pg=pg, kc=kc, fc=fc, u=u: e.matmul(pg[:], W[:, kc, 1024 + fc * 128:1024 + (fc + 1) * 128], u[:, kc, :], start=(kc == 0), stop=(kc == 7)),
                    w=[pg], r=[u, ("W", 2 + fc // 4)])
            P.I("act", lambda e, pg=pg, sg_=sg_, fc=fc: e.activation(out=sg_[:], in_=pg[:], func=AF.Sigmoid, bias=bin_[:, 8 + fc:9 + fc]), w=[sg_], r=[pg, bin_])
            P.I("dve", lambda e, pa=pa, sg_=sg_, fc=fc, o=o: e.scalar_tensor_tensor(out=o[:, fc, :], in0=pa[:], scalar=bin_[:, fc:fc + 1], in1=sg_[:],
                                                                                 op0=ALU.add, op1=ALU.mult), w=[o], r=[pa, sg_, bin_])
        P.D("sp", C.cHG[:, :, 16 + g * 512:16 + (g + 1) * 512].rearrange("k p t -> p k t"), o[:], r=[o], w=[("hg", g)])
    P.D("sp", C.cHG[:, :, 16 + T:32 + T].rearrange("k p t -> p k t"), zt[:], r=[zt], w=[("hghalo",)])
    P.end_phase()


def phase_conv_dw(P, C):
    P.begin_phase()
    T = C.T
    idb = make_ident(P)
    ones = make_ones(P)
    wT = P.sb("wT", [128, 8, 31], F32)
    P.D("sp", wT[:], C.conv_wdwT, w=[wT])
    cols = P.sb("cols", [128, 24], F32)
    P.D("sp", cols[:], C.conv_cols, w=[cols])
    D = P.sb("D", [128, 248, 128], BF16)
    for j in range(31):
        for fc in range(8):
            ve = "dve" if (j + fc) % 2 == 0 else "pool"
            P.I(ve, lambda e, j=j, fc=fc: e.tensor_scalar(out=D[:, j * 8 + fc, :], in0=idb[:], scalar1=wT[:, fc, j:j + 1], scalar2=None, op0=ALU.mult),
                w=[("D", j * 8 + fc)], r=[idb, wT])
    hw = _alt(P, "hw", [128, 8, 542], BF16)
    psc = _palt(P, "psc", [128, 512], F32, 3)
    psm = P.ps("psm", [128, 512], F32)
    pss = P.ps("pss", [128, 512], F32)
    xs = P.sb("xs", [128, 8, 512], F32)
    xb = P.sb("xb", [128, 8, 512], BF16)
    sq = P.sb("sq", [128, 8, 512], BF16)
    mt = P.sb("mt", [128, 512], F32)
    m2 = P.sb("m2", [128, 512], F32)
    rs = P.sb("rs", [128, 512], F32)
    tt_ = _alt(P, "tt", [128, 512], F32)
    yTg = _alt(P, "yTg", [128, 8, 512], BF16)
    for g in range(C.NG):
        w_ = hw[g % 2]
        yg = yTg[g % 2]
        P.D("sp", w_[:], C.cHG[:, :, 1 + g * 512:1 + g * 512 + 542].rearrange("k p t -> p k t"), w=[w_])
        for fc in range(8):
            pc = psc[fc % 3]
            for j in range(31):
                P.I("pe", lambda e, pc=pc, fc=fc, j=j, w_=w_: e.matmul(pc[:], D[:, j * 8 + fc, :], w_[:, fc, j:j + 512], start=(j == 0), stop=(j == 30)),
                    w=[pc], r=[w_] + ([("D", j * 8 + fc)] if g == 0 else []))
            P.I("act", lambda e, pc=pc, fc=fc: e.activation(out=xs[:, fc, :], in_=pc[:], func=AF.Identity, bias=cols[:, fc:fc + 1]),
                w=[("xs", fc)], r=[pc, cols])
        xkeys = [("xs", fc) for fc in range(8)]
        P.I("act", lambda e: e.activation(out=sq[:], in_=xs[:], func=AF.Square), w=[sq], r=xkeys)
        P.I("dve", lambda e: e.tensor_copy(out=xb[:], in_=xs[:]), w=[xb], r=xkeys)
        for kc in range(8):
            P.I("pe", lambda e, kc=kc: e.matmul(psm[:], ones[:], xb[:, kc, :], start=(kc == 0), stop=(kc == 7)), w=[psm], r=[xb, ones])
        for kc in range(8):
            P.I("pe", lambda e, kc=kc: e.matmul(pss[:], ones[:], sq[:, kc, :], start=(kc == 0), stop=(kc == 7)), w=[pss], r=[sq, ones])
        P.I("act", lambda e: e.activation(out=mt[:], in_=psm[:], func=AF.Copy, scale=1.0 / 1024.0), w=[mt], r=[psm])
        P.I("dve", lambda e: e.tensor_tensor(out=m2[:], in0=mt[:], in1=mt[:], op=ALU.mult), w=[m2], r=[mt])
        P.I("dve", lambda e: e.scalar_tensor_tensor(out=rs[:], in0=pss[:], scalar=1.0 / 1024.0, in1=m2[:], op0=ALU.mult, op1=ALU.subtract),
            w=[rs], r=[pss, m2])
        P.I("act", lambda e: e.activation(out=rs[:], in_=rs[:], func=AF.Sqrt, bias=EPS), w=[rs], r=[rs])
        P.I("dve", lambda e: e.reciprocal(out=rs[:], in_=rs[:]), w=[rs], r=[rs])
        for fc in range(8):
            t = tt_[fc % 2]
            ve = "dve" if fc % 2 == 0 else "pool"
            P.I(ve, lambda e, t=t, fc=fc: e.tensor_tensor(out=t[:], in0=xs[:, fc, :], in1=mt[:], op=ALU.subtract), w=[t], r=[("xs", fc), mt])
            P.I(ve, lambda e, t=t: e.tensor_tensor(out=t[:], in0=t[:], in1=rs[:], op=ALU.mult), w=[t], r=[t, rs])
            P.I("act", lambda e, t=t, fc=fc, yg=yg: e.activation(out=yg[:, fc, :], in_=t[:], func=AF.Silu, scale=cols[:, 8 + fc:9 + fc], bias=cols[:, 16 + fc:17 + fc]),
                w=[yg], r=[t, cols])
        P.D("sp", C.cyT[:, :, g * 512:(g + 1) * 512].rearrange("k p t -> p k t"), yg[:], r=[yg], w=[("ysrc", g)])
    P.end_phase()


def layer_conv(P, C, li, h_src, h_dst):
    phase_conv_glu(P, C, li, h_src)
    phase_conv_dw(P, C)
    phase_outproj(P, C, li, 1, C.conv_w_out[0], 8, C.cyT, h_src, h_dst, bias_src=C.conv_bout)


def declare_extra(nc, C, kinds, din, dint):
    T = C.T
    if "gla" in kinds:
        C.gla_w_in = din("gla_w_in", [1, 1024, 3072])
        C.gla_w1 = din("gla_w1", [2, 1024, 16])
        C.gla_w2 = din("gla_w2", [2, 16, 512])
        C.gla_b = din("gla_b", [2, 512])
        C.gla_ng = din("gla_ng", [128, 256])
        C.gla_w_out = din("gla_w_out", [1, 1024, 1024])
        C.gQF = dint("gQF", [4, 128, T]); C.gQB = dint("gQB", [4, 128, T])
        C.gKF = dint("gKF", [4, 128, T]); C.gKB = dint("gKB", [4, 128, T])
        C.gkEF = dint("gkEF", [T, 512]); C.gkEB = dint("gkEB", [T, 512])
        C.gv = dint("gv", [T, 1024]); C.gsg = dint("gsg", [T, 1024]); C.gP1 = dint("gP1", [T, 1024])
        C.gyT = dint("gyT", [8, 128, T])
        C.gDEC = dint("gDEC", [2, 128, 4 * C.NCH], F32)
        C.xin2 = dint("xin2", [512, 256], F32)
        C.xout2 = dint("xout2", [1024, 256], F32)
    if "conv" in kinds:
        C.conv_w_in = din("conv_w_in", [1, 1024, 2048])
        C.conv_bin = din("conv_bin", [128, 16])
        C.conv_wdwT = din("conv_wdwT", [128, 8, 31])
        C.conv_cols = din("conv_cols", [128, 24])
        C.conv_w_out = din("conv_w_out", [1, 1024, 1024])
        C.conv_bout = din("conv_bout", [128, 8])
        C.cHG = dint("cHG", [8, 128, T + 32])
        C.cyT = dint("cyT", [8, 128, T])
        C.xin3 = dint("xin3", [1024, 16])
        C.xout3 = dint("xout3", [2048, 16])


def extra_in_maps(m, inputs, T, b, half, kinds):
    f = lambda k: np.asarray(inputs[k], np.float32)
    if "gla" in kinds:
        m["gla_w_in"] = f("gla_w_in")
        sw = (lambda a: a) if half == 0 else (lambda a: a[::-1])
        m["gla_w1"] = np.ascontiguousarray(sw(f("gla_gate_w1")[0]))
        m["gla_w2"] = np.ascontiguousarray(sw(f("gla_gate_w2")[0]))
        m["gla_b"] = np.ascontiguousarray(sw(f("gla_gate_b")[0]))
        m["gla_ng"] = np.ascontiguousarray(np.broadcast_to(f("gla_norm_gain")[0][None, :], (128, 256)))
        m["gla_w_out"] = f("gla_w_out")
    if "conv" in kinds:
        m["conv_w_in"] = f("conv_w_in")
        m["conv_bin"] = np.ascontiguousarray(f("conv_b_in")[0].reshape(16, 128).T)
        wdw = f("conv_w_dw")[0]
        if half == 1:
            wdw = wdw[::-1]
        m["conv_wdwT"] = np.ascontiguousarray(wdw.reshape(31, 8, 128).transpose(2, 1, 0))
        rows = np.stack([f("conv_b_dw")[0], f("conv_ln_gain")[0], f("conv_ln_bias")[0]], axis=0)
        m["conv_cols"] = np.ascontiguousarray(rows.reshape(3, 8, 128).transpose(2, 0, 1).reshape(128, 24))
        m["conv_w_out"] = f("conv_w_out")
        m["conv_bout"] = np.ascontiguousarray(f("conv_b_out")[0].reshape(8, 128).T)


CST_LAYOUT = [("maskLF", 128), ("maskLB", 128), ("A1", 128), ("A2", 128), ("A3", 128), ("EF", 512), ("EB", 512),
              ("c127", 1), ("cs", 1), ("c128", 1), ("invf", 1), ("sel", 2),
              ("triU", 129), ("triL", 129), ("sL", 128), ("sU", 128)]


def cst_offsets():
    off = {}
    o = 0
    for nm, w in CST_LAYOUT:
        off[nm] = (o, w)
        o += w
    return off, o


def make_cst(half):
    off, n = cst_offsets()
    c = np.zeros((128, n), np.float32)
    s = np.arange(128)[:, None].astype(np.float64)
    t = np.arange(128)[None, :].astype(np.float64)

    def put(nm, a):
        o, w = off[nm]
        c[:, o:o + w] = np.broadcast_to(a, (128, w))
    put("maskLF", (s <= t) if half == 0 else (s < t))
    put("maskLB", (s > t) if half == 0 else (s >= t))
    put("A1", -(s + 1) + 0 * t)
    put("A2", s - t)
    put("A3", -(t + 1) + 0 * s)
    tt = (np.arange(512) % 128)[None, :]
    put("EF", tt + 1.0)
    put("EB", 128.0 - tt)
    put("c127", 127.0 - s)
    put("cs", s)
    put("c128", 128.0)
    put("invf", (10000.0 ** (-(np.arange(128, dtype=np.float32) / np.float32(128)))).astype(np.float32)[:, None])
    put("sel", np.array([[0.0, 1.0]]) if half == 0 else np.array([[1.0, 0.0]]))
    put("triU", np.concatenate([(s <= t), np.ones((128, 1))], axis=1))
    put("triL", np.concatenate([(s >= t), np.ones((128, 1))], axis=1))
    put("sL", (s > t))
    put("sU", (s < t))
    return c


def build(T, plan):
    nc = bass.Bass("TRN2", target_bir_lowering=False)
    C = Ctx()
    C.T, C.NG, C.NCH = T, T // 512, T // 128
    C.cst_off, ncst = cst_offsets()

    def din(name, shape, dt=F32):
        return nc.dram_tensor(name, list(shape), dt, kind="ExternalInput").ap()

    def dint(name, shape, dt=BF16):
        return nc.dram_tensor(name, list(shape), dt).ap()
    C.xT = din("xT", [8, 128, T])
    C.posr = din("posr", [128, T], I32)
    C.gcol = din("gcol", [128, 128])
    C.cst = din("cst", [128, ncst])
    kinds = set(k for k, _, _ in plan)
    C.ret_dl = din("ret_dl", [2, 128, 8])
    if any(k.startswith("p_") for k in kinds):
        kinds = kinds | {"ret"}
    if "ret" in kinds:
        C.ret_w_in = din("ret_w_in", [2, 1024, 6144])
        C.ret_w_out = din("ret_w_out", [2, 2048, 1024])
    if "mlp" in kinds:
        C.mlp_w_up = din("mlp_w_up", [4, 1024, 4096])
        C.mlp_w_down = din("mlp_w_down", [4, 4096, 1024])
    declare_extra(nc, C, kinds, din, dint)
    C.outT = nc.dram_tensor("outT", [8, 128, T], F32, kind="ExternalOutput").ap()
    C.hA = dint("hA", [8, 128, T], F32)
    C.qF = dint("qF", [8, 128, T])
    C.qB = dint("qB", [8, 128, T])
    C.kT = dint("kT", [8, 128, T])
    C.kB = dint("kB", [8, 128, T])
    C.kEF = dint("kEF", [T, 1024])
    C.kEB = dint("kEB", [T, 1024])
    C.v = dint("v", [T, 2048])
    C.sg = dint("sg", [T, 2048])
    C.P1 = dint("P1", [T, 2048])
    C.yT = dint("yT", [16, 128, T])
    C.hid = dint("hid", [32, 128, T])
    C.xin = dint("xin", [1024, 512], F32)
    C.xout = dint("xout", [2048, 512], F32)
    P = Prog(nc)
    nsteps = len(plan)
    src = C.xT
    for i, (kind, li, j) in enumerate(plan):
        dst = C.outT if i == nsteps - 1 else C.hA
        if kind == "ret":
            layer_ret(P, C, li, j, src, dst)
        elif kind == "mlp":
            layer_mlp(P, C, li, src, dst)
        elif kind == "p_qk":
            phase_ret_qk(P, C, li, j, src)
        elif kind == "p_vg":
            phase_vg(P, C, li, src, C.ret_w_in[j][:, 2048:6144], 2048, C.v, C.sg)
        elif kind == "p_s1":
            phase_ret_sweep1(P, C, j)
        elif kind == "p_s2":
            phase_ret_sweep2(P, C, j)
        elif kind == "p_out":
            phase_outproj(P, C, li, 1, C.ret_w_out[j], 16, C.yT, src, dst)
        elif kind == "gla":
            layer_gla(P, C, li, src, dst)
        elif kind == "conv":
            layer_conv(P, C, li, src, dst)
        src = dst
    P.begin_phase()
    P.end_phase(final=True)
    P.close()
    return nc, P


FULL_PLAN = [("ret", 0, 0), ("mlp", 0, 0), ("conv", 1, 0), ("mlp", 1, 0), ("gla", 2, 0), ("mlp", 2, 0), ("ret", 3, 1), ("mlp", 3, 0)]


def make_in_maps(inputs, T, ncores=4, kinds=("ret", "mlp", "conv", "gla")):
    x = np.asarray(inputs["x"], np.float32)
    pos = np.asarray(inputs["positions"], np.int32)
    ng = np.asarray(inputs["norm_gains"], np.float32)
    L = ng.shape[0]
    gcol = np.zeros((128, 128), np.float32)
    gcol[:, :L * 32] = ng.reshape(L, 4, 8, 128).transpose(3, 0, 1, 2).reshape(128, L * 32)
    rdl = np.asarray(inputs["ret_decay_logit"], np.float32)
    cst = make_cst(0)
    maps = []
    for b in range(ncores):
        m = {}
        m["xT"] = np.ascontiguousarray(x[b].reshape(T, 8, 128).transpose(1, 2, 0))
        m["posr"] = np.ascontiguousarray(np.broadcast_to(pos[b][None, :], (128, T)))
        m["gcol"] = gcol
        m["cst"] = cst
        m["ret_dl"] = np.ascontiguousarray(np.broadcast_to(rdl.reshape(2, 1, 8), (2, 128, 8)))
        big = []
        if "ret" in kinds:
            big += ["ret_w_in", "ret_w_out"]
        if "mlp" in kinds:
            big += ["mlp_w_up", "mlp_w_down"]
        for k in big:
            m[k] = np.asarray(inputs[k], np.float32)
        extra_in_maps(m, inputs, T, b, 0, kinds)
        maps.append(m)
    return maps


def gather_out(res, T, ncores=4):
    out = np.zeros((ncores, T, 1024), np.float32)
    for b in range(ncores):
        out[b] = res.results[b]["outT"].reshape(1024, T).T
    return out


_CACHE = {}


def kernel(**inputs):
    T = 8192
    if "nc" not in _CACHE:
        _CACHE["nc"] = build(T, FULL_PLAN)[0]
    nc = _CACHE["nc"]
    maps = make_in_maps(inputs, T, ncores=4)
    res = run_bass_kernel_spmd(nc, maps, core_ids=list(range(4)))
    return gather_out(res, T, 4)
```

```python
import numpy as np
from contextlib import ExitStack
import concourse.bass as bass
import concourse.mybir as mybir
from concourse.bass_utils import run_bass_kernel_spmd

F32 = mybir.dt.float32
BF16 = mybir.dt.bfloat16
I32 = mybir.dt.int32
AF = mybir.ActivationFunctionType
ALU = mybir.AluOpType

ENGS = ("pe", "act", "dve", "pool", "sp")


def _key(x):
    if isinstance(x, (str, tuple)):
        return x
    t = getattr(x, "tensor", None)
    return t.name if t is not None else x.name


class _Op:
    __slots__ = ("eng", "fn", "deps", "sig", "dma", "waits")


class Prog:
    N_DMA_SEMS = {"sp": 16, "pool": 12, "act": 4}

    def __init__(self, nc):
        self.nc = nc
        self.es = ExitStack()
        self.sem = {e: self.es.enter_context(nc.semaphore("s_" + e)) for e in ENGS if e != "sp"}
        self.sem["sp"] = self.es.enter_context(nc.semaphore("s_sp"))
        self.cnt = {e: 0 for e in ENGS}
        self.dsem = {q: [self.es.enter_context(nc.semaphore("d_%s%d" % (q, i))) for i in range(n)]
                     for q, n in self.N_DMA_SEMS.items()}
        self.dcnt = {q: [0] * n for q, n in self.N_DMA_SEMS.items()}
        self.dlast = {q: [None] * n for q, n in self.N_DMA_SEMS.items()}
        self.drr = {q: 0 for q in self.N_DMA_SEMS}
        self.known = {e: {} for e in ENGS}
        self.res_w = {}
        self.res_r = {}
        self.ops = []
        self.phase_es = None
        self.uid = 0
        self.last_sig = {e: None for e in ENGS}
        self.n_inst = 0

    def begin_phase(self):
        self.phase_es = ExitStack()
        self.nph = getattr(self, "nph", 0) + 1
        self.sem["pe"] = self.es.enter_context(self.nc.semaphore("s_pe%d" % self.nph))
        self.cnt["pe"] = 0

    def sb(self, name, shape, dtype):
        self.uid += 1
        return self.phase_es.enter_context(self.nc.sbuf_tensor("%s_%d" % (name, self.uid), list(shape), dtype))

    def ps(self, name, shape, dtype):
        self.uid += 1
        return self.phase_es.enter_context(self.nc.psum_tensor("%s_%d" % (name, self.uid), list(shape), dtype))

    def _deps(self, r, w):
        deps = []
        for x in r:
            k = _key(x)
            s = self.res_w.get(k)
            if s is not None:
                deps.append(s)
        for x in w:
            k = _key(x)
            s = self.res_w.get(k)
            if s is not None:
                deps.append(s)
            deps.extend(self.res_r.get(k, ()))
        return deps

    def _commit(self, sig, r, w):
        for x in r:
            self.res_r.setdefault(_key(x), []).append(sig)
        for x in w:
            k = _key(x)
            self.res_w[k] = sig
            self.res_r[k] = []

    def I(self, eng, fn, w=(), r=()):
        deps = self._deps(r, w)
        if eng == "pe":
            deps = [d for d in deps if d[0] is not self.sem["pe"]]
        self.cnt[eng] += 1
        sig = (self.sem[eng], self.cnt[eng])
        op = _Op()
        op.eng, op.fn, op.sig, op.dma = eng, fn, sig, False
        op.waits = self._waits(eng, deps)
        self.ops.append(op)
        self._commit(sig, r, w)
        self.last_sig[eng] = sig
        return sig

    def D(self, q, out, in_, w=(), r=(), **kw):
        return self.Dfn(q, lambda e: e.dma_start(out=out, in_=in_, **kw), w=w, r=r)

    def Dfn(self, q, fn, w=(), r=()):
        deps = self._deps(r, w)
        i = self.drr[q]
        self.drr[q] = (i + 1) % len(self.dsem[q])
        if self.dlast[q][i] is not None:
            deps.append(self.dlast[q][i])
        self.dcnt[q][i] += 1
        sig = (self.dsem[q][i], 16 * self.dcnt[q][i])
        self.dlast[q][i] = sig
        op = _Op()
        op.eng, op.sig, op.dma = q, sig, True
        op.fn = fn
        op.waits = self._waits(q, deps)
        self.ops.append(op)
        self._commit(sig, r, w)
        return sig

    def _waits(self, eng, deps):
        best = {}
        for (s, v) in deps:
            if best.get(id(s), (None, 0))[1] < v:
                best[id(s)] = (s, v)
        out = []
        kn = self.known[eng]
        for sid, (s, v) in best.items():
            if kn.get(sid, 0) >= v:
                continue
            kn[sid] = v
            out.append((s, v))
        return out

    def barrier(self):
        sigs = [s for s in self.last_sig.values() if s is not None]
        for q in self.dlast:
            sigs.extend(s for s in self.dlast[q] if s is not None)
        for e in ENGS:
            ws = self._waits(e, sigs)
            if ws:
                op = _Op()
                op.eng, op.fn, op.sig, op.dma, op.waits = e, None, None, False, ws
                self.ops.append(op)

    def wait_all(self, eng):
        sigs = [s for s in self.last_sig.values() if s is not None]
        for q in self.dlast:
            sigs.extend(s for s in self.dlast[q] if s is not None)
        ws = self._waits(eng, sigs)
        if ws:
            op = _Op()
            op.eng, op.fn, op.sig, op.dma, op.waits = eng, None, None, False, ws
            self.ops.append(op)

    def end_phase(self, final=False):
        self.barrier()
        if final:
            self.wait_all("sp")
        ops = self.ops
        self.ops = []
        per = {e: [o for o in ops if o.eng == e] for e in ENGS}
        self.n_inst += len(ops)
        if not hasattr(self, "remap"):
            self.remap = {}
            self.newcnt = {}
        eng_ids = set(id(x) for x in self.sem.values())
        targets = set()
        for o in ops:
            for (sm, v) in o.waits:
                if id(sm) in eng_ids:
                    targets.add((id(sm), v))
        for o in ops:
            if o.fn is None or o.dma:
                continue
            sm, v = o.sig
            if (id(sm), v) in targets:
                self.newcnt[id(sm)] = self.newcnt.get(id(sm), 0) + 1
                self.remap[(id(sm), v)] = self.newcnt[id(sm)]
        remap = self.remap

        def replay(e, lst):
            for o in lst:
                for (sm, v) in o.waits:
                    e.wait_ge(sm, remap.get((id(sm), v), v) if id(sm) in eng_ids else v)
                if o.fn is not None:
                    ins = o.fn(e)
                    if o.dma:
                        ins.then_inc(o.sig[0], 16)
                    elif (id(o.sig[0]), o.sig[1]) in remap:
                        ins.then_inc(o.sig[0], 1)

        with self.nc.Block() as block:
            if per["pe"]:
                block.tensor(lambda e: replay(e, per["pe"]))
            if per["act"]:
                block.scalar(lambda e: replay(e, per["act"]))
            if per["dve"]:
                block.vector(lambda e: replay(e, per["dve"]))
            if per["pool"]:
                block.gpsimd(lambda e: replay(e, per["pool"]))
            if per["sp"]:
                block.sync(lambda e: replay(e, per["sp"]))
        self.phase_es.close()
        self.phase_es = None
        self.res_w = {k: v for k, v in self.res_w.items() if isinstance(k, tuple)}
        self.res_r = {k: v for k, v in self.res_r.items() if isinstance(k, tuple)}

    def close(self):
        self.es.close()

PI = float(np.pi)
EPS = 1e-6


class Ctx:
    pass


def _alt(P, name, shape, dtype, n=2):
    return [P.sb("%s%d" % (name, i), shape, dtype) for i in range(n)]


def _palt(P, name, shape, dtype, n=2):
    return [P.ps("%s%d" % (name, i), shape, dtype) for i in range(n)]


def load_consts(P, C, names):
    out = {}
    for nm in names:
        off, w = C.cst_off[nm]
        t = P.sb("c_" + nm, [128, w], F32)
        P.D("sp", t[:], C.cst[:, off:off + w], w=[t], allow_slow_non_contiguous=True)
        out[nm] = t
    return out


def make_ident(P, dtype=BF16):
    idb = P.sb("ident", [128, 128], dtype)
    P.I("dve", lambda e: e.memset(idb[:], 0.0), w=[idb])
    P.I("pool", lambda e: e.affine_select(out=idb[:], in_=idb[:], pattern=[[-1, 128]], compare_op=ALU.not_equal,
                                          fill=1.0, base=0, channel_multiplier=1), w=[idb], r=[idb])
    return idb


def make_ones(P):
    o = P.sb("ones", [128, 128], BF16)
    P.I("dve", lambda e: e.memset(o[:], 1.0), w=[o])
    return o


def load_gcol(P, C, li, i):
    g = P.sb("gcol", [128, 8], F32)
    o = (li * 4 + i) * 8
    P.D("sp", g[:], C.gcol[:, o:o + 8], w=[g])
    return g


class Norm:
    def __init__(self, P, nk=8, G=512):
        self.P = P
        self.ones = make_ones(P)
        self.sq = P.sb("nsq", [128, nk, G], BF16)
        self.ps = P.ps("nps", [128, G], F32)
        self.rstd = P.sb("nrstd", [128, G], F32)
        self.nk = nk

    def stats_a(self, x):
        self.P.I("act", lambda e: e.activation(out=self.sq[:], in_=x[:], func=AF.Square), w=[self.sq], r=[x])

    def stats(self, x, d_model, skip_a=False):
        P = self.P
        nk = self.nk
        if not skip_a:
            self.stats_a(x)
        for kc in range(nk):
            P.I("pe", lambda e, kc=kc: e.matmul(self.ps[:], self.ones[:], self.sq[:, kc, :], start=(kc == 0), stop=(kc == nk - 1)),
                w=[self.ps], r=[self.sq, self.ones])
        P.I("act", lambda e: e.activation(out=self.rstd[:], in_=self.ps[:], func=AF.Sqrt, scale=1.0 / d_model, bias=EPS),
            w=[self.rstd], r=[self.ps])
        P.I("dve", lambda e: e.reciprocal(out=self.rstd[:], in_=self.rstd[:]), w=[self.rstd], r=[self.rstd])

    def pre(self, hT, gcol, uT, skip_a=False):
        P = self.P
        self.stats(hT, 1024.0, skip_a)
        for kc in range(8):
            P.I("dve", lambda e, kc=kc: e.scalar_tensor_tensor(out=uT[:, kc, :], in0=hT[:, kc, :], scalar=gcol[:, kc:kc + 1],
                                                              in1=self.rstd[:], op0=ALU.mult, op1=ALU.mult),
                w=[uT], r=[hT, gcol, self.rstd])

    def post(self, yo, gcol, hT, skip_a=False):
        P = self.P
        self.stats(yo, 1024.0, skip_a)
        for kc in range(8):
            P.I("dve", lambda e, kc=kc: e.scalar_tensor_tensor(out=yo[:, kc, :], in0=yo[:, kc, :], scalar=gcol[:, kc:kc + 1],
                                                              in1=self.rstd[:], op0=ALU.mult, op1=ALU.mult),
                w=[yo], r=[yo, gcol, self.rstd])
        P.I("pool", lambda e: e.tensor_tensor(out=hT[:], in0=hT[:], in1=yo[:], op=ALU.add), w=[hT], r=[hT, yo])


def hview(ap, g, G=512):
    return ap[:, :, g * G:(g + 1) * G].rearrange("k p t -> p k t")


def load_w(P, W, src, nk, ncols, blk=512, key="W"):
    for cb in range(ncols // blk):
        for k0 in range(0, nk, 8):
            k1 = min(nk, k0 + 8)
            P.D("pool", W[:, k0:k1, cb * blk:(cb + 1) * blk],
                src[k0 * 128:k1 * 128, cb * blk:(cb + 1) * blk].rearrange("(kc p) n -> p kc n", p=128),
                w=[(key, cb)])


def phase_outproj(P, C, li, gi, w_src, nk, y_src, h_src, h_dst, bias_src=None):
    P.begin_phase()
    W = P.sb("W", [128, nk, 1024], BF16)
    load_w(P, W, w_src, nk, 1024)
    gcol = load_gcol(P, C, li, gi)
    nrm = Norm(P)
    bcol = None
    if bias_src is not None:
        bcol = P.sb("bcol", [128, 8], F32)
        P.D("sp", bcol[:], bias_src, w=[bcol])
    yT = _alt(P, "yT", [128, nk, 512], BF16)
    hT = _alt(P, "hT", [128, 8, 512], F32)
    yos = _alt(P, "yo", [128, 8, 512], F32)
    pso = _palt(P, "pso", [128, 512], F32, 3)

    def finish_a(g):
        nrm.stats_a(yos[g % 2])

    def finish(g):
        nrm.post(yos[g % 2], gcol, hT[g % 2], skip_a=True)
        P.D("sp", hview(h_dst, g), hT[g % 2][:], r=[hT[g % 2]], w=[("h", g)])

    def load_y(g):
        P.D("sp", yT[g % 2][:], y_src[:, :, g * 512:(g + 1) * 512].rearrange("k p t -> p k t"), w=[yT[g % 2]], r=[("ysrc", g)])

    load_y(0)
    for g in range(C.NG + 1):
        if g < C.NG:
            y = yT[g % 2]
            h = hT[g % 2]
            yo = yos[g % 2]
            if g + 1 < C.NG:
                load_y(g + 1)
            P.D("sp", h[:], hview(h_src, g), w=[h], r=[("h", g)])
            for oc in range(8):
                ps = pso[oc % 3]
                for kc in range(nk):
                    P.I("pe", lambda e, ps=ps, kc=kc, oc=oc, y=y: e.matmul(ps[:], W[:, kc, oc * 128:(oc + 1) * 128], y[:, kc, :],
                                                                         start=(kc == 0), stop=(kc == nk - 1)),
                        w=[ps], r=[y, ("W", oc // 4)])
                if bcol is None:
                    P.I("act", lambda e, ps=ps, oc=oc, yo=yo: e.activation(out=yo[:, oc, :], in_=ps[:], func=AF.Copy), w=[yo], r=[ps])
                else:
                    P.I("act", lambda e, ps=ps, oc=oc, yo=yo: e.activation(out=yo[:, oc, :], in_=ps[:], func=AF.Identity,
                                                                        bias=bcol[:, oc:oc + 1]), w=[yo], r=[ps, bcol])
                if oc == 0 and g > 0:
                    finish_a(g - 1)
                if oc == 3 and g > 0:
                    finish(g - 1)
        else:
            finish_a(g - 1)
            finish(g - 1)
    P.end_phase()


def phase_mlp_up(P, C, li, h_src):
    P.begin_phase()
    W = P.sb("W", [128, 8, 4096], BF16)
    load_w(P, W, C.mlp_w_up[li], 8, 4096)
    gcol = load_gcol(P, C, li, 2)
    nrm = Norm(P)
    hT = _alt(P, "hT", [128, 8, 512], F32)
    uT = _alt(P, "uT", [128, 8, 512], BF16)
    hid = _alt(P, "hid", [128, 32, 512], BF16)
    psu = _palt(P, "psu", [128, 512], F32, 4)
    rtmp = _alt(P, "rtmp", [128, 512], F32)
    def prep_l(g):
        P.D("sp", hT[g % 2][:], hview(h_src, g), w=[hT[g % 2]], r=[("h", g)])

    def prep_a(g):
        nrm.stats_a(hT[g % 2])

    def prep_b(g):
        nrm.pre(hT[g % 2], gcol, uT[g % 2], skip_a=True)

    prep_l(0)
    prep_a(0)
    prep_b(0)
    for g in range(C.NG):
        u = uT[g % 2]
        hd = hid[g % 2]
        for hc in range(32):
            if hc == 0 and g + 1 < C.NG:
                prep_l(g + 1)
            if hc == 16 and g + 1 < C.NG:
                prep_a(g + 1)
            if hc == 22 and g + 1 < C.NG:
                prep_b(g + 1)
            ps = psu[hc % 4]
            for kc in range(8):
                P.I("pe", lambda e, ps=ps, kc=kc, hc=hc, u=u: e.matmul(ps[:], W[:, kc, hc * 128:(hc + 1) * 128], u[:, kc, :],
                                                                     start=(kc == 0), stop=(kc == 7)),
                    w=[ps], r=[u, ("W", hc // 4)])
            rt = rtmp[hc % 2]
            P.I("act", lambda e, ps=ps, rt=rt: e.activation(out=rt[:], in_=ps[:], func=AF.Relu), w=[rt], r=[ps])
            ve = "dve" if hc % 2 == 0 else "pool"
            P.I(ve, lambda e, rt=rt, hc=hc, hd=hd: e.tensor_tensor(out=hd[:, hc, :], in0=rt[:], in1=rt[:], op=ALU.mult),
                w=[(hd.name, hc)], r=[rt])
        P.D("sp", C.hid[:, :, g * 512:(g + 1) * 512].rearrange("k p t -> p k t"), hd[:],
            r=[(hd.name, hc) for hc in range(32)], w=[("ysrc", g)])
    P.end_phase()


def phase_mlp_down(P, C, li, h_src, h_dst):
    phase_outproj(P, C, li, 3, C.mlp_w_down[li], 32, C.hid, h_src, h_dst)


def ret_lg(P, C, j):
    dl = P.sb("dl", [128, 8], F32)
    lg = P.sb("lg", [128, 8], F32)
    P.D("sp", dl[:], C.ret_dl[j], w=[dl])
    P.I("act", lambda e: e.activation(out=lg[:], in_=dl[:], func=AF.Exp, scale=-1.0), w=[lg], r=[dl])
    P.I("act", lambda e: e.activation(out=lg[:], in_=lg[:], func=AF.Ln, bias=1.0), w=[lg], r=[lg])
    P.I("dve", lambda e: e.tensor_scalar(out=lg[:], in0=lg[:], scalar1=-1.0, scalar2=None, op0=ALU.mult), w=[lg], r=[lg])
    return lg


def rope_tables(P, C, g, K, cos, sin):
    pi_, pf, ang, ki, kf = K["pi"], K["pf"], K["ang"], K["ki"], K["kf"]
    P.D("sp", pi_[:], C.posr[:, g * 512:(g + 1) * 512], w=[pi_])
    P.I("dve", lambda e: e.tensor_copy(out=pf[:], in_=pi_[:]), w=[pf], r=[pi_])
    P.I("dve", lambda e: e.tensor_scalar(out=ang[:], in0=pf[:], scalar1=K["invf"][:, 0:1], scalar2=None, op0=ALU.mult),
        w=[ang], r=[pf, K["invf"]])
    P.I("dve", lambda e: e.tensor_scalar(out=ki[:], in0=ang[:], scalar1=float(1 / (2 * np.pi)), scalar2=None, op0=ALU.mult),
        w=[ki], r=[ang])
    P.I("dve", lambda e: e.tensor_copy(out=kf[:], in_=ki[:]), w=[kf], r=[ki])
    P.I("dve", lambda e: e.scalar_tensor_tensor(out=ang[:], in0=kf[:], scalar=float(-2 * np.pi), in1=ang[:], op0=ALU.mult, op1=ALU.add),
        w=[ang], r=[ang, kf])
    for (dst, shift) in ((sin, 0.0), (cos, PI / 2)):
        if shift != 0.0:
            P.I("dve", lambda e, dst=dst, shift=shift: e.tensor_scalar(out=dst[:], in0=ang[:], scalar1=shift, scalar2=None, op0=ALU.add),
                w=[dst], r=[ang])
            src = dst
        else:
            src = ang
        P.I("dve", lambda e, src=src: e.tensor_scalar(out=kf[:], in0=src[:], scalar1=PI, scalar2=float(2 * np.pi), op0=ALU.is_gt, op1=ALU.mult),
            w=[kf], r=[src])
        P.I("dve", lambda e, src=src, dst=dst: e.tensor_tensor(out=dst[:], in0=src[:], in1=kf[:], op=ALU.subtract), w=[dst], r=[src, kf])
        P.I("act", lambda e, dst=dst: e.activation(out=dst[:], in_=dst[:], func=AF.Sin), w=[dst], r=[dst])


def phase_ret_qk(P, C, li, j, h_src):
    P.begin_phase()
    W = P.sb("W", [128, 8, 2048], BF16)
    load_w(P, W, C.ret_w_in[j][:, 0:2048], 8, 2048)
    gcol = load_gcol(P, C, li, 0)
    nrm = Norm(P)
    idb = make_ident(P)
    cs = load_consts(P, C, ["EF", "EB", "c127", "cs", "invf"])
    lg = ret_lg(P, C, j)
    aF = P.sb("aF", [128, 4, 512], F32)
    aB = P.sb("aB", [128, 4, 512], F32)
    colE = P.sb("colE", [128, 8], F32)
    for h in range(4):
        P.I("act", lambda e, h=h: e.activation(out=aF[:, h, :], in_=cs["EF"][:], func=AF.Exp, scale=lg[:, h:h + 1]), w=[aF], r=[cs["EF"], lg])
        P.I("act", lambda e, h=h: e.activation(out=aB[:, h, :], in_=cs["EB"][:], func=AF.Exp, scale=lg[:, 4 + h:5 + h]), w=[aB], r=[cs["EB"], lg])
        P.I("act", lambda e, h=h: e.activation(out=colE[:, h:h + 1], in_=cs["c127"][:], func=AF.Exp, scale=lg[:, h:h + 1]), w=[colE], r=[cs["c127"], lg])
        P.I("act", lambda e, h=h: e.activation(out=colE[:, 4 + h:5 + h], in_=cs["cs"][:], func=AF.Exp, scale=lg[:, 4 + h:5 + h]), w=[colE], r=[cs["cs"], lg])
    P.I("dve", lambda e: e.tensor_scalar(out=aF[:], in0=aF[:], scalar1=0.0625, scalar2=None, op0=ALU.mult), w=[aF], r=[aF])
    P.I("dve", lambda e: e.tensor_scalar(out=aB[:], in0=aB[:], scalar1=0.0625, scalar2=None, op0=ALU.mult), w=[aB], r=[aB])
    K = {"pi": P.sb("pi", [128, 512], I32), "pf": P.sb("pf", [128, 512], F32), "ang": P.sb("ang", [128, 512], F32),
         "ki": P.sb("ki", [128, 512], I32), "kf": P.sb("kf", [128, 512], F32), "invf": cs["invf"]}
    cos = P.sb("cos", [128, 512], F32)
    sin = P.sb("sin", [128, 512], F32)
    hT = _alt(P, "hT", [128, 8, 512], F32)
    uT = _alt(P, "uT", [128, 8, 512], BF16)
    psq = _palt(P, "psq", [128, 2, 512], F32, 2)
    pst = _palt(P, "pst", [128, 256], BF16, 2)
    t12 = _alt(P, "t12", [128, 2, 512], F32)
    tmp = _alt(P, "tmp", [128, 4, 512], F32)
    rr = _alt(P, "rr", [128, 2, 512], F32)
    kb = _alt(P, "kb", [128, 2, 512], BF16)
    qo = _alt(P, "qo", [128, 4, 512], BF16)
    kE = _alt(P, "kE", [128, 2, 4, 256], BF16)
    it = 0
    def prep_l(g):
        P.D("sp", hT[g % 2][:], hview(h_src, g), w=[hT[g % 2]], r=[("h", g)])

    def prep_a(g):
        nrm.stats_a(hT[g % 2])

    def prep_b(g):
        nrm.pre(hT[g % 2], gcol, uT[g % 2], skip_a=True)

    prep_l(0)
    prep_a(0)
    prep_b(0)
    for g in range(C.NG):
        u = uT[g % 2]
        rope_tables(P, C, g, K, cos, sin)
        for h in range(4):
            for qk in range(2):
                if h == 0 and qk == 0 and g + 1 < C.NG:
                    prep_l(g + 1)
                if h == 2 and qk == 0 and g + 1 < C.NG:
                    prep_a(g + 1)
                if h == 3 and qk == 0 and g + 1 < C.NG:
                    prep_b(g + 1)
                base = qk * 1024 + h * 256
                ps = psq[it % 2]
                t = t12[it % 2]
                tm = tmp[it % 2]
                r_ = rr[it % 2]
                ve = "pool" if it % 3 == 2 else "dve"
                it += 1
                for c in range(2):
                    for kc in range(8):
                        P.I("pe", lambda e, ps=ps, c=c, kc=kc, base=base, u=u: e.matmul(ps[:, c, :], W[:, kc, base + c * 128:base + (c + 1) * 128],
                                                                                     u[:, kc, :], start=(kc == 0), stop=(kc == 7)),
                            w=[ps], r=[u, ("W", base // 512)])
                P.I("act", lambda e, ps=ps, t=t: e.activation(out=t[:], in_=ps[:], func=AF.Copy), w=[t], r=[ps])
                P.I(ve, lambda e, t=t, tm=tm: e.tensor_tensor(out=tm[:, 0, :], in0=t[:, 0, :], in1=cos[:], op=ALU.mult), w=[tm], r=[t, cos])
                P.I(ve, lambda e, t=t, tm=tm: e.tensor_tensor(out=tm[:, 1, :], in0=t[:, 1, :], in1=sin[:], op=ALU.mult), w=[tm], r=[t, sin])
                P.I(ve, lambda e, t=t, tm=tm: e.tensor_tensor(out=tm[:, 2, :], in0=t[:, 1, :], in1=cos[:], op=ALU.mult), w=[tm], r=[t, cos])
                P.I(ve, lambda e, t=t, tm=tm: e.tensor_tensor(out=tm[:, 3, :], in0=t[:, 0, :], in1=sin[:], op=ALU.mult), w=[tm], r=[t, sin])
                if qk == 0:
                    q = qo[h % 2]
                    P.I(ve, lambda e, tm=tm, r_=r_: e.tensor_tensor(out=r_[:, 0, :], in0=tm[:, 0, :], in1=tm[:, 1, :], op=ALU.subtract), w=[r_], r=[tm])
                    P.I(ve, lambda e, tm=tm, r_=r_: e.tensor_tensor(out=r_[:, 1, :], in0=tm[:, 2, :], in1=tm[:, 3, :], op=ALU.add), w=[r_], r=[tm])
                    for c in range(2):
                        P.I(ve, lambda e, r_=r_, q=q, c=c, h=h: e.tensor_tensor(out=q[:, c, :], in0=r_[:, c, :], in1=aF[:, h, :], op=ALU.mult), w=[q], r=[r_, aF])
                        P.I(ve, lambda e, r_=r_, q=q, c=c, h=h: e.tensor_tensor(out=q[:, 2 + c, :], in0=r_[:, c, :], in1=aB[:, h, :], op=ALU.mult), w=[q], r=[r_, aB])
                    P.D("sp", C.qF[h * 2:h * 2 + 2, :, g * 512:(g + 1) * 512].rearrange("c p t -> p c t"), q[:, 0:2, :], r=[q], w=[("qF", g, h)])
                    P.D("sp", C.qB[h * 2:h * 2 + 2, :, g * 512:(g + 1) * 512].rearrange("c p t -> p c t"), q[:, 2:4, :], r=[q], w=[("qB", g, h)])
                else:
                    k = kb[h % 2]
                    ke = kE[h % 2]
                    P.I(ve, lambda e, tm=tm, k=k: e.tensor_tensor(out=k[:, 0, :], in0=tm[:, 0, :], in1=tm[:, 1, :], op=ALU.subtract), w=[k], r=[tm])
                    P.I(ve, lambda e, tm=tm, k=k: e.tensor_tensor(out=k[:, 1, :], in0=tm[:, 2, :], in1=tm[:, 3, :], op=ALU.add), w=[k], r=[tm])
                    P.D("sp", C.kT[h * 2:h * 2 + 2, :, g * 512:(g + 1) * 512].rearrange("c p t -> p c t"), k[:], r=[k], w=[("kT", g, h)])
                    for tt in range(4):
                        pt = pst[tt % 2]
                        for c in range(2):
                            P.I("pe", lambda e, pt=pt, k=k, c=c, tt=tt: e.transpose(out=pt[:, c * 128:(c + 1) * 128], in_=k[:, c, tt * 128:(tt + 1) * 128],
                                                                                 identity=idb[:]), w=[pt], r=[k, idb])
                        P.I("act", lambda e, pt=pt, ke=ke, tt=tt, h=h: e.activation(out=ke[:, 0, tt, :], in_=pt[:], func=AF.Copy, scale=colE[:, h:h + 1]),
                            w=[ke], r=[pt, colE])
                        P.I("act", lambda e, pt=pt, ke=ke, tt=tt, h=h: e.activation(out=ke[:, 1, tt, :], in_=pt[:], func=AF.Copy, scale=colE[:, 4 + h:5 + h]),
                            w=[ke], r=[pt, colE])
                    rows = slice(g * 512, (g + 1) * 512)
                    P.D("sp", C.kEF[rows, h * 256:(h + 1) * 256].rearrange("(tt p) n -> p tt n", p=128), ke[:, 0, :, :], r=[ke], w=[("kEF", g, h)])
                    P.D("sp", C.kEB[rows, h * 256:(h + 1) * 256].rearrange("(tt p) n -> p tt n", p=128), ke[:, 1, :, :], r=[ke], w=[("kEB", g, h)])
    P.end_phase()


def phase_vg(P, C, li, h_src, w_src, ncols_v, v_dst, g_dst, G=None):
    P.begin_phase()
    nc_ = 2 * ncols_v
    W = P.sb("W", [128, 8, nc_], BF16)
    load_w(P, W, w_src, 8, nc_)
    gcol = load_gcol(P, C, li, 0)
    nrm = Norm(P)
    hT = _alt(P, "hT", [128, 8, 512], F32)
    uT = _alt(P, "uT", [128, 8, 512], BF16)
    vo = _alt(P, "vo", [128, nc_], BF16)
    psv = _palt(P, "psv", [128, 512], F32, 4)
    it = 0
    def prep_l(g):
        P.D("sp", hT[g % 2][:], hview(h_src, g), w=[hT[g % 2]], r=[("h", g)])

    def prep_a(g):
        nrm.stats_a(hT[g % 2])

    def prep_b(g):
        nrm.pre(hT[g % 2], gcol, uT[g % 2], skip_a=True)

    prep_l(0)
    prep_a(0)
    prep_b(0)
    for g in range(C.NG):
        u = uT[g % 2]
        for tt in range(4):
            if tt == 0 and g + 1 < C.NG:
                prep_l(g + 1)
            if tt == 2 and g + 1 < C.NG:
                prep_a(g + 1)
            if tt == 3 and g + 1 < C.NG:
                prep_b(g + 1)
            o = vo[tt % 2]
            for cb in range(nc_ // 512):
                ps = psv[it % 4]
                it += 1
                for kc in range(8):
                    P.I("pe", lambda e, ps=ps, kc=kc, tt=tt, cb=cb, u=u: e.matmul(ps[:], u[:, kc, tt * 128:(tt + 1) * 128], W[:, kc, cb * 512:(cb + 1) * 512],
                                                                               start=(kc == 0), stop=(kc == 7)), w=[ps], r=[u, ("W", cb)])
                fn = AF.Copy if cb * 512 < ncols_v else AF.Silu
                P.I("act", lambda e, ps=ps, o=o, cb=cb, fn=fn: e.activation(out=o[:, cb * 512:(cb + 1) * 512], in_=ps[:], func=fn), w=[o], r=[ps])
            rows = slice(g * 512 + tt * 128, g * 512 + (tt + 1) * 128)
            P.D("sp", v_dst[rows, :], o[:, 0:ncols_v], r=[o], w=[("v", g, tt)])
            P.D("sp", g_dst[rows, :], o[:, ncols_v:nc_], r=[o], w=[("sg", g, tt)])
    P.end_phase()


def exchange_states(P, C, S, nhc, DV, xin_t=None, xout_t=None):
    rows = nhc * 128
    xin_t = C.xin if xin_t is None else xin_t
    xout_t = C.xout if xout_t is None else xout_t
    P.D("sp", xin_t.rearrange("(k p) n -> p k n", p=128), S[:, 0:nhc, 0:DV],
        r=[("S", i) for i in range(nhc)], w=[("xin",)])
    xin = xin_t
    xout = xout_t
    P.Dfn('pool', lambda e: e.collective_compute("AllGather", ALU.bypass,
                                          replica_groups=[[0, 1], [2, 3], [4, 5], [6, 7]],
                                          ins=[xin], outs=[xout]), r=[("xin",)], w=[("xout",)])


def sweep(P, C, cfg, second):
    H, NC, DV = cfg["H"], cfg["NC"], cfg["DV"]
    HC = H * NC
    NG = C.NG
    pairs = cfg["pairs"] if not second else []
    npair = len(pairs)
    S = P.sb("S", [128, HC, DV], F32)
    Sb = P.sb("Sb", [128, HC, DV], BF16)
    idb = cfg["idb"]
    P.I("dve", lambda e: e.memset(S[:], 0.0), w=[("S", i) for i in range(HC)])
    P.I("pool", lambda e: e.memset(Sb[:], 0.0), w=[("Sb", i) for i in range(HC)])
    Qg = _alt(P, "Qg", [128, HC, 512], BF16)
    Kg = [_alt(P, "Kg%d" % i, [128, HC, 512], BF16) for i in range(npair)]
    Q2g = [_alt(P, "Q2g%d" % i, [128, HC, 512], BF16) for i in range(1, npair)]
    KEg = _alt(P, "KEg", [128, 4, HC * 128], BF16)
    Vg = _alt(P, "Vg", [128, 4, H * DV], BF16)
    if second:
        P1g = _alt(P, "P1g", [128, 4, H * DV], BF16)
    else:
        P1t = _alt(P, "P1t", [128, H * DV], BF16)
    psS = _palt(P, "psS", [128, 128], F32, 2) if not second else None
    psO = _palt(P, "psO", [128, DV], F32, 2)
    psU = _palt(P, "psU", [128, DV], F32, 4)
    scm = _alt(P, "scm", [128, 128], BF16)
    sct = _alt(P, "sct", [128, 128], F32)
    sct2 = _alt(P, "sct2", [128, 128], F32)
    order = list(range(NG)) if not second else list(reversed(range(NG)))

    def loads(gi):
        g = order[gi]
        b = gi % 2
        tsl = slice(g * 512, (g + 1) * 512)
        P.D("sp", Qg[b][:], cfg["Q"][:, :, tsl].rearrange("k p t -> p k t"), w=[Qg[b]], r=[("q",)])
        for i, (K_ap, Q_ap, mk) in enumerate(pairs):
            P.D("sp", Kg[i][b][:], K_ap[:, :, tsl].rearrange("k p t -> p k t"), w=[Kg[i][b]])
            if i > 0:
                P.D("sp", Q2g[i - 1][b][:], Q_ap[:, :, tsl].rearrange("k p t -> p k t"), w=[Q2g[i - 1][b]])
        P.D("sp", KEg[b][:], cfg["KE"][tsl, :].rearrange("(tt p) n -> p tt n", p=128), w=[KEg[b]])
        P.D("sp", Vg[b][:], cfg["V"][tsl, :].rearrange("(tt p) n -> p tt n", p=128), w=[Vg[b]])
        if second:
            P.D("sp", P1g[b][:], cfg["P1"][tsl, :].rearrange("(tt p) n -> p tt n", p=128), w=[P1g[b]])

    steps = []
    for gi, g in enumerate(order):
        tts = list(range(4)) if not second else [3, 2, 1, 0]
        for tt in tts:
            for h in range(H):
                steps.append((gi, g, tt, h))

    def stageA(i):
        gi, g, tt, h = steps[i]
        b = gi % 2
        csl = slice(tt * 128, (tt + 1) * 128)
        sm = scm[i % 2]
        for pi_, (K_ap, Q_ap, mk) in enumerate(pairs):
            pS = psS[pi_ % 2] if npair > 1 else psS[i % 2]
            qq = Qg[b] if pi_ == 0 else Q2g[pi_ - 1][b]
            for c in range(NC):
                P.I("pe", lambda e, pS=pS, kk=Kg[pi_][b], qq=qq, hc=h * NC + c, c=c, csl=csl:
                    e.matmul(pS[:], kk[:, hc, csl], qq[:, hc, csl], start=(c == 0), stop=(c == NC - 1)), w=[pS], r=[Kg[pi_][b], qq])
            mt = cfg["M"][h] if mk is None else cfg["masks"][mk]
            if pi_ == 0 and npair == 1:
                P.I("dve", lambda e, pS=pS, sm=sm, mt=mt: e.tensor_tensor(out=sm[:], in0=pS[:], in1=mt[:], op=ALU.mult), w=[sm], r=[pS, mt])
            elif pi_ == 0:
                st_ = sct[i % 2]
                P.I("dve", lambda e, pS=pS, st_=st_, mt=mt: e.tensor_tensor(out=st_[:], in0=pS[:], in1=mt[:], op=ALU.mult), w=[st_], r=[pS, mt])
            else:
                st_ = sct[i % 2]
                st2 = sct2[i % 2]
                P.I("dve", lambda e, pS=pS, st2=st2, mt=mt: e.tensor_tensor(out=st2[:], in0=pS[:], in1=mt[:], op=ALU.mult), w=[st2], r=[pS, mt])
                P.I("pool", lambda e, st_=st_, st2=st2, sm=sm: e.tensor_tensor(out=sm[:], in0=st_[:], in1=st2[:], op=ALU.add), w=[sm], r=[st_, st2])

    pending = []
    loads(0)
    if second:
        cfg["epi_load"](order[0], 0)
    if not second:
        stageA(0)
    for i, (gi, g, tt, h) in enumerate(steps):
        b = gi % 2
        n = g * 4 + tt
        csl = slice(tt * 128, (tt + 1) * 128)
        first_of_group = (i % (4 * H) == 0)
        last_of_group = (i % (4 * H) == 4 * H - 1)
        if first_of_group:
            if gi + 1 < NG:
                loads(gi + 1)
                if second:
                    cfg["epi_load"](order[gi + 1], (gi + 1) % 2)
        if not second and i + 1 < len(steps):
            stageA(i + 1)
        po = psO[i % 2]
        vsl = Vg[b][:, tt, h * DV:(h + 1) * DV]
        if not second:
            sm = scm[i % 2]
            P.I("pe", lambda e, po=po, sm=sm, vsl=vsl: e.matmul(po[:], sm[:], vsl, start=True, stop=False), w=[po], r=[sm, Vg[b]])
        else:
            P.I("pe", lambda e, po=po, b=b, tt=tt, h=h: e.matmul(po[:], idb[:], P1g[b][:, tt, h * DV:(h + 1) * DV], start=True, stop=False),
                w=[po], r=[idb, P1g[b]])
        for c in range(NC):
            hc = h * NC + c
            P.I("pe", lambda e, po=po, b=b, hc=hc, c=c, csl=csl: e.matmul(po[:], Qg[b][:, hc, csl], Sb[:, hc, :], start=False, stop=(c == NC - 1)),
                w=[po], r=[Qg[b], ("Sb", hc)])
        if not second:
            pt = P1t[n % 2]
            P.I("act", lambda e, po=po, pt=pt, h=h: e.activation(out=pt[:, h * DV:(h + 1) * DV], in_=po[:], func=AF.Copy), w=[pt], r=[po])
        else:
            tail = cfg["epi"](g, b, tt, h, po)
            for t_ in pending:
                t_()
            pending = [tail]
        for c in range(NC):
            hc = h * NC + c
            pu = psU[(i * NC + c) % 4]
            P.I("pe", lambda e, pu=pu, b=b, tt=tt, hc=hc, vsl=vsl: e.matmul(pu[:], KEg[b][:, tt, hc * 128:(hc + 1) * 128], vsl, start=True, stop=True),
                w=[pu], r=[KEg[b], Vg[b]])
            dcol = cfg["dec"](h, c, n)
            P.I("dve", lambda e, pu=pu, hc=hc, dcol=dcol: e.scalar_tensor_tensor(out=S[:, hc, :], in0=S[:, hc, :], scalar=dcol, in1=pu[:],
                                                                             op0=ALU.mult, op1=ALU.add),
                w=[("S", hc)], r=[("S", hc), pu, cfg["dec_res"]])
            P.I("pool", lambda e, hc=hc: e.tensor_copy(out=Sb[:, hc, :], in_=S[:, hc, :]), w=[("Sb", hc)], r=[("S", hc)])
        if not second and h == H - 1:
            P.D("sp", cfg["P1"][n * 128:(n + 1) * 128, :], P1t[n % 2][:], r=[P1t[n % 2]], w=[("P1", n)])
        if second and last_of_group:
            for t_ in pending:
                t_()
            pending = []
            cfg["epi_store"](g, b)
    return S


def phase_ret_sweep1(P, C, j):
    P.begin_phase()
    cs = load_consts(P, C, ["maskLF", "maskLB", "A1", "A2", "A3", "c128"])
    lg = ret_lg(P, C, j)
    M = {}
    for h in range(4):
        m = P.sb("M%d" % h, [128, 128], F32)
        t2 = P.sb("Mt%d" % h, [128, 128], F32)
        lf = lg[:, h:h + 1]
        lb = lg[:, 4 + h:5 + h]
        P.I("act", lambda e, m=m, lf=lf: e.activation(out=m[:], in_=cs["A1"][:], func=AF.Exp, scale=lf), w=[m], r=[cs["A1"], lg])
        P.I("dve", lambda e, m=m: e.tensor_tensor(out=m[:], in0=m[:], in1=cs["maskLF"][:], op=ALU.mult), w=[m], r=[m, cs["maskLF"]])
        P.I("dve", lambda e, t2=t2, lb=lb: e.tensor_scalar(out=t2[:], in0=cs["A2"][:], scalar1=lb, scalar2=None, op0=ALU.mult), w=[t2], r=[cs["A2"], lg])
        P.I("dve", lambda e, t2=t2, lf=lf: e.scalar_tensor_tensor(out=t2[:], in0=cs["A3"][:], scalar=lf, in1=t2[:], op0=ALU.mult, op1=ALU.add),
            w=[t2], r=[t2, cs["A3"], lg])
        P.I("act", lambda e, t2=t2: e.activation(out=t2[:], in_=t2[:], func=AF.Exp), w=[t2], r=[t2])
        P.I("dve", lambda e, t2=t2: e.tensor_tensor(out=t2[:], in0=t2[:], in1=cs["maskLB"][:], op=ALU.mult), w=[t2], r=[t2, cs["maskLB"]])
        P.I("dve", lambda e, m=m, t2=t2: e.tensor_tensor(out=m[:], in0=m[:], in1=t2[:], op=ALU.add), w=[m], r=[m, t2])
        M[h] = m
    decc = P.sb("decc", [128, 4], F32)
    for h in range(4):
        P.I("act", lambda e, h=h: e.activation(out=decc[:, h:h + 1], in_=cs["c128"][:], func=AF.Exp, scale=lg[:, h:h + 1]), w=[decc], r=[cs["c128"], lg])
    cfg = dict(H=4, NC=2, DV=512, pairs=[(C.kT, C.qF, None)], Q=C.qF, KE=C.kEF, V=C.v, P1=C.P1,
               dec=lambda h, c, n: decc[:, h:h + 1], dec_res=decc, M=M, idb=None)
    S = sweep(P, C, cfg, False)
    P.end_phase()


def make_epilogue(P, C, H, DV, sg_src, y_dst, gain_bc=None):
    idb = make_ident(P)
    NF = DV // 128
    SGgs = _alt(P, "SGg", [128, 4, H * DV], BF16)
    yTg = P.sb("yTg", [128, H * NF, 512], BF16)
    yt = _alt(P, "yt", [128, DV], BF16)
    yf = _alt(P, "yf", [128, DV], F32)
    ssc = P.sb("ssc", [128, 8], F32)
    junk = P.sb("junk", [128, DV], F32)
    psT = _palt(P, "psT", [128, DV], BF16, 2)
    st = {"k": 0}

    def epi_load(g, b):
        P.D("sp", SGgs[b][:], sg_src[g * 512:(g + 1) * 512, :].rearrange("(tt p) n -> p tt n", p=128), w=[SGgs[b]])

    def epi(g, b, tt, h, po):
        SGg = SGgs[b]
        k = st["k"]
        st["k"] += 1
        s = ssc[:, k % 8:k % 8 + 1]
        key = ("ssc", k % 8)
        y = yt[k % 2]
        pT = psT[k % 2]
        P.I("dve", lambda e: e.memset(s, 0.0), w=[key])
        P.I("act", lambda e: e.activation(out=junk[:], in_=po[:], func=AF.Square, accum_out=s), w=[junk, key], r=[po])
        P.I("act", lambda e: e.activation(out=s, in_=s, func=AF.Sqrt, scale=1.0 / DV, bias=EPS), w=[key], r=[key])
        P.I("dve", lambda e: e.reciprocal(out=s, in_=s), w=[key], r=[key])
        if gain_bc is None:
            P.I("dve", lambda e: e.scalar_tensor_tensor(out=y[:], in0=po[:], scalar=s, in1=SGg[:, tt, h * DV:(h + 1) * DV], op0=ALU.mult, op1=ALU.mult),
                w=[y], r=[po, key, SGg])
        else:
            f = yf[k % 2]
            P.I("dve", lambda e: e.scalar_tensor_tensor(out=f[:], in0=po[:], scalar=s, in1=gain_bc[:], op0=ALU.mult, op1=ALU.mult),
                w=[f], r=[po, key, gain_bc])
            P.I("pool", lambda e: e.tensor_tensor(out=y[:], in0=f[:], in1=SGg[:, tt, h * DV:(h + 1) * DV], op=ALU.mult), w=[y], r=[f, SGg])
        def tail():
            for q4 in range(NF):
                P.I("pe", lambda e, q4=q4: e.transpose(out=pT[:, q4 * 128:(q4 + 1) * 128], in_=y[:, q4 * 128:(q4 + 1) * 128], identity=idb[:]),
                    w=[pT], r=[y, idb])
            P.I("act", lambda e: e.activation(out=yTg[:, h * NF:(h + 1) * NF, tt * 128:(tt + 1) * 128],
                                              in_=pT[:].rearrange("p (q t) -> p q t", q=NF), func=AF.Copy), w=[yTg], r=[pT])
        return tail

    def epi_store(g, b):
        P.D("sp", y_dst[:, :, g * 512:(g + 1) * 512].rearrange("k p t -> p k t"), yTg[:], r=[yTg], w=[("ysrc", g)])

    return idb, epi_load, epi, epi_store


def phase_ret_sweep2(P, C, j):
    P.begin_phase()
    cs = load_consts(P, C, ["c128", "sel"])
    lg = ret_lg(P, C, j)
    decc = P.sb("decc", [128, 4], F32)
    for h in range(4):
        P.I("act", lambda e, h=h: e.activation(out=decc[:, h:h + 1], in_=cs["c128"][:], func=AF.Exp, scale=lg[:, 4 + h:5 + h]), w=[decc], r=[cs["c128"], lg])
    idb, epi_load, epi, epi_store = make_epilogue(P, C, 4, 512, C.sg, C.yT)
    cfg = dict(H=4, NC=2, DV=512, pairs=[], Q=C.qB, KE=C.kEB, V=C.v, P1=C.P1,
               dec=lambda h, c, n: decc[:, h:h + 1], dec_res=decc, idb=idb, sel=cs["sel"],
               epi_load=epi_load, epi=epi, epi_store=epi_store)
    sweep(P, C, cfg, True)
    P.end_phase()


def layer_ret(P, C, li, j, h_src, h_dst):
    phase_ret_qk(P, C, li, j, h_src)
    phase_vg(P, C, li, h_src, C.ret_w_in[j][:, 2048:6144], 2048, C.v, C.sg)
    phase_ret_sweep1(P, C, j)
    phase_ret_sweep2(P, C, j)
    phase_outproj(P, C, li, 1, C.ret_w_out[j], 16, C.yT, h_src, h_dst)


def layer_mlp(P, C, li, h_src, h_dst):
    phase_mlp_up(P, C, li, h_src)
    phase_mlp_down(P, C, li, h_src, h_dst)


def phase_gla_qk(P, C, li, h_src):
    P.begin_phase()
    NCH = C.NCH
    W = P.sb("W", [128, 8, 1024], BF16)
    load_w(P, W, C.gla_w_in[0][:, 0:1024], 8, 1024)
    w1 = P.sb("w1", [128, 2, 8, 16], BF16)
    for d in range(2):
        P.D("pool", w1[:, d, :, :], C.gla_w1[d].rearrange("(kc p) r -> p kc r", p=128), w=[w1])
    w2a = P.sb("w2a", [17, 2, 512], F32)
    for d in range(2):
        P.D("sp", w2a[0:16, d, :], C.gla_w2[d], w=[w2a])
        P.D("sp", w2a[16:17, d, :], C.gla_b[d:d + 1, :], w=[w2a])
    gcol = load_gcol(P, C, li, 0)
    nrm = Norm(P)
    cs = load_consts(P, C, ["triU", "triL", "sL", "sU"])
    tri = [cs["triU"], cs["triL"]]
    sX = [cs["sL"], cs["sU"]]
    DEC = [P.sb("DEC%d" % d, [128, 4, NCH], F32) for d in range(2)]
    hT = _alt(P, "hT", [128, 8, 512], F32)
    uT = _alt(P, "uT", [128, 8, 512], BF16)
    qs = P.sb("qs", [128, 4, 512], F32)
    ks = P.sb("ks", [128, 4, 512], F32)
    zTa = [P.sb("zTa%d" % d, [17, 512], F32) for d in range(2)]
    for d in range(2):
        P.I("dve", lambda e, d=d: e.memset(zTa[d][:], 1.0), w=[zTa[d]])
    lgt = [P.sb("lgt%d" % d, [128, 4, 512], F32) for d in range(2)]
    QX = [P.sb("QX%d" % d, [128, 4, 512], BF16) for d in range(2)]
    KX = [P.sb("KX%d" % d, [128, 4, 512], BF16) for d in range(2)]
    kE = [P.sb("kE%d" % d, [128, 4, 512], BF16) for d in range(2)]
    EQ = _alt(P, "EQ", [128, 4, 128], F32)
    EK = _alt(P, "EK", [128, 4, 128], F32)
    EE = _alt(P, "EE", [128, 512], F32)
    psq = _palt(P, "psq", [128, 512], F32, 2)
    psz = P.ps("psz", [16, 512], F32)
    psl = P.ps("psl", [128, 512], F32)
    psc = _palt(P, "psc", [128, 4, 128], F32, 2)
    pse = P.ps("pse", [128, 512], F32)
    QXd = [C.gQF, C.gQB]
    KXd = [C.gKF, C.gKB]
    kEd = [C.gkEF, C.gkEB]
    it = 0
    def prep_l(g):
        P.D("sp", hT[g % 2][:], hview(h_src, g), w=[hT[g % 2]], r=[("h", g)])

    def prep_a(g):
        nrm.stats_a(hT[g % 2])

    def prep_b(g):
        nrm.pre(hT[g % 2], gcol, uT[g % 2], skip_a=True)

    prep_l(0)
    prep_a(0)
    prep_b(0)
    for g in range(C.NG):
        u = uT[g % 2]
        tsl = slice(g * 512, (g + 1) * 512)
        for qk in range(2):
            dst = qs if qk == 0 else ks
            for h in range(4):
                ps = psq[h % 2]
                col = qk * 512 + h * 128
                for kc in range(8):
                    P.I("pe", lambda e, ps=ps, kc=kc, col=col, u=u: e.matmul(ps[:], W[:, kc, col:col + 128], u[:, kc, :], start=(kc == 0), stop=(kc == 7)),
                        w=[ps], r=[u, ("W", col // 512)])
                sc = float(128 ** -0.5) if qk == 0 else 1.0
                P.I("act", lambda e, ps=ps, dst=dst, h=h, sc=sc: e.activation(out=dst[:, h, :], in_=ps[:], func=AF.Copy, scale=sc), w=[dst], r=[ps])
        for d in range(2):
            for kc in range(8):
                P.I("pe", lambda e, d=d, kc=kc, u=u: e.matmul(psz[:], w1[:, d, kc, :], u[:, kc, :], start=(kc == 0), stop=(kc == 7)), w=[psz], r=[u, w1])
            P.I("act", lambda e, d=d: e.activation(out=zTa[d][0:16, :], in_=psz[:], func=AF.Copy), w=[zTa[d]], r=[psz])
            for tt in range(4):
                P.I("pe", lambda e, d=d, tt=tt: e.matmul(psl[:], zTa[d][:, tt * 128:(tt + 1) * 128], w2a[:, d, :], start=True, stop=True), w=[psl], r=[zTa[d], w2a])
                P.I("act", lambda e, d=d, tt=tt: e.activation(out=lgt[d][:, tt, :], in_=psl[:], func=AF.Exp, scale=-1.0), w=[lgt[d]], r=[psl])
            P.I("act", lambda e, d=d: e.activation(out=lgt[d][:], in_=lgt[d][:], func=AF.Ln, bias=1.0), w=[lgt[d]], r=[lgt[d]])
            P.I("dve", lambda e, d=d: e.tensor_scalar(out=lgt[d][:], in0=lgt[d][:], scalar1=-1.0 / 16.0, scalar2=None, op0=ALU.mult), w=[lgt[d]], r=[lgt[d]])
        for tt in range(4):
            if tt == 0 and g + 1 < C.NG:
                prep_l(g + 1)
            if tt == 2 and g + 1 < C.NG:
                prep_a(g + 1)
            if tt == 3 and g + 1 < C.NG:
                prep_b(g + 1)
            n = g * 4 + tt
            csl = slice(tt * 128, (tt + 1) * 128)
            for kc in range(8):
                P.I("pe", lambda e, kc=kc, csl=csl, u=u: e.matmul(pse[:], u[:, kc, csl], W[:, kc, 512:1024], start=(kc == 0), stop=(kc == 7)), w=[pse], r=[u, ("W", 1)])
            ktm = EE[0]
            P.I("act", lambda e, ktm=ktm: e.activation(out=ktm[:], in_=pse[:], func=AF.Copy), w=[ktm], r=[pse])
            for d in range(2):
                pc = psc[it % 2]
                eq = EQ[it % 2]
                ek = EK[it % 2]
                it += 1
                for h in range(4):
                    P.I("pe", lambda e, pc=pc, d=d, h=h, tt=tt: e.matmul(pc[:, h, :], lgt[d][:, tt, h * 128:(h + 1) * 128], tri[d][:, 0:128], start=True, stop=True),
                        w=[pc], r=[lgt[d], tri[d]])
                P.I("act", lambda e, pc=pc, eq=eq: e.activation(out=eq[:], in_=pc[:], func=AF.Exp), w=[eq], r=[pc])
                P.I("act", lambda e, pc=pc, ek=ek: e.activation(out=ek[:], in_=pc[:], func=AF.Exp, scale=-1.0), w=[ek], r=[pc])
                tcol = 127 if d == 0 else 0
                P.I("pool", lambda e, eq=eq, d=d, n=n, tcol=tcol: e.tensor_copy(out=DEC[d][:, :, n:n + 1], in_=eq[:, :, tcol:tcol + 1]), w=[DEC[d]], r=[eq])
                P.I("dve", lambda e, eq=eq, d=d, csl=csl: e.tensor_tensor(out=QX[d][:, :, csl], in0=qs[:, :, csl], in1=eq[:], op=ALU.mult), w=[QX[d]], r=[qs, eq])
                P.I("pool", lambda e, ek=ek, d=d, csl=csl: e.tensor_tensor(out=KX[d][:, :, csl], in0=ks[:, :, csl], in1=ek[:], op=ALU.mult), w=[KX[d]], r=[ks, ek])
                P.I("pe", lambda e, d=d, tt=tt: e.matmul(psl[:], sX[d][:], lgt[d][:, tt, :], start=True, stop=True), w=[psl], r=[sX[d], lgt[d]])
                ee = EE[1]
                P.I("act", lambda e, ee=ee: e.activation(out=ee[:], in_=psl[:], func=AF.Exp), w=[ee], r=[psl])
                P.I("dve", lambda e, ee=ee, d=d, tt=tt, ktm=ktm: e.tensor_tensor(out=kE[d][:, tt, :], in0=ktm[:], in1=ee[:], op=ALU.mult), w=[kE[d]], r=[ktm, ee])
        for d in range(2):
            P.D("sp", QXd[d][:, :, tsl].rearrange("k p t -> p k t"), QX[d][:], r=[QX[d]], w=[("gq", d, g)])
            P.D("sp", KXd[d][:, :, tsl].rearrange("k p t -> p k t"), KX[d][:], r=[KX[d]], w=[("gk", d, g)])
            P.D("sp", kEd[d][tsl, :].rearrange("(tt p) n -> p tt n", p=128), kE[d][:], r=[kE[d]], w=[("gke", d, g)])
    for d in range(2):
        P.D("sp", C.gDEC[d], DEC[d][:].rearrange("p h n -> p (h n)"), r=[DEC[d]], w=[("gdec", d)])
    P.end_phase()


def phase_gla_sweep1(P, C):
    P.begin_phase()
    cs = load_consts(P, C, ["maskLF", "maskLB"])
    DECt = P.sb("DECt", [128, 4 * C.NCH], F32)
    P.D("sp", DECt[:], C.gDEC[0], w=[DECt])
    NCH = C.NCH
    cfg = dict(H=4, NC=1, DV=256, pairs=[(C.gKF, C.gQF, "maskLF"), (C.gKB, C.gQB, "maskLB")], Q=C.gQF, KE=C.gkEF, V=C.gv, P1=C.gP1,
               dec=lambda h, c, n: DECt[:, h * NCH + n:h * NCH + n + 1], dec_res=DECt, masks=cs, idb=None, xin=C.xin2, xout=C.xout2)
    S = sweep(P, C, cfg, False)
    P.end_phase()


def phase_gla_sweep2(P, C):
    P.begin_phase()
    cs = load_consts(P, C, ["sel"])
    NCH = C.NCH
    DECt = P.sb("DECt", [128, 4 * NCH], F32)
    P.D("sp", DECt[:], C.gDEC[1], w=[DECt])
    gbc = P.sb("gbc", [128, 256], F32)
    P.D("sp", gbc[:], C.gla_ng, w=[gbc])
    idb, epi_load, epi, epi_store = make_epilogue(P, C, 4, 256, C.gsg, C.gyT, gain_bc=gbc)
    cfg = dict(H=4, NC=1, DV=256, pairs=[], Q=C.gQB, KE=C.gkEB, V=C.gv, P1=C.gP1,
               dec=lambda h, c, n: DECt[:, h * NCH + n:h * NCH + n + 1], dec_res=DECt, idb=idb, sel=cs["sel"],
               epi_load=epi_load, epi=epi, epi_store=epi_store, xin=C.xin2, xout=C.xout2)
    sweep(P, C, cfg, True)
    P.end_phase()


def layer_gla(P, C, li, h_src, h_dst):
    phase_gla_qk(P, C, li, h_src)
    phase_vg(P, C, li, h_src, C.gla_w_in[0][:, 1024:3072], 1024, C.gv, C.gsg)
    phase_gla_sweep1(P, C)
    phase_gla_sweep2(P, C)
    phase_outproj(P, C, li, 1, C.gla_w_out[0], 8, C.gyT, h_src, h_dst)


def phase_conv_glu(P, C, li, h_src):
    P.begin_phase()
    T = C.T
    W = P.sb("W", [128, 8, 2048], BF16)
    load_w(P, W, C.conv_w_in[0], 8, 2048)
    gcol = load_gcol(P, C, li, 0)
    nrm = Norm(P)
    bin_ = P.sb("bin", [128, 16], F32)
    P.D("sp", bin_[:], C.conv_bin, w=[bin_])
    hT = _alt(P, "hT", [128, 8, 512], F32)
    uT = _alt(P, "uT", [128, 8, 512], BF16)
    hg = _alt(P, "hg", [128, 8, 512], BF16)
    sig = _alt(P, "sig", [128, 512], F32)
    psa = _palt(P, "psa", [128, 512], F32, 2)
    psg = _palt(P, "psg", [128, 512], F32, 2)
    zt = P.sb("zt", [128, 8, 16], BF16)
    P.I("dve", lambda e: e.memset(zt[:], 0.0), w=[zt])
    P.D("sp", C.cHG[:, :, 0:16].rearrange("k p t -> p k t"), zt[:], r=[zt], w=[("hgpad",)])
    def prep_l(g):
        P.D("sp", hT[g % 2][:], hview(h_src, g), w=[hT[g % 2]], r=[("h", g)])

    def prep_a(g):
        nrm.stats_a(hT[g % 2])

    def prep_b(g):
        nrm.pre(hT[g % 2], gcol, uT[g % 2], skip_a=True)

    prep_l(0)
    prep_a(0)
    prep_b(0)
    for g in range(C.NG):
        u = uT[g % 2]
        o = hg[g % 2]
        for fc in range(8):
            if fc == 0 and g + 1 < C.NG:
                prep_l(g + 1)
            if fc == 4 and g + 1 < C.NG:
                prep_a(g + 1)
            if fc == 6 and g + 1 < C.NG:
                prep_b(g + 1)
            pa = psa[fc % 2]
            pg = psg[fc % 2]
            sg_ = sig[fc % 2]
            for kc in range(8):
                P.I("pe", lambda e, pa=pa, kc=kc, fc=fc, u=u: e.matmul(pa[:], W[:, kc, fc * 128:(fc + 1) * 128], u[:, kc, :], start=(kc == 0), stop=(kc == 7)),
                    w=[pa], r=[u, ("W", fc // 4)])
            for kc in range(8):
                P.I("pe", lambda e, pg=pg, kc=kc, fc=fc, u=u: e.matmul(pg[:], W[:, kc, 1024 + fc * 128:1024 + (fc + 1) * 128], u[:, kc, :], start=(kc == 0), stop=(kc == 7)),
                    w=[pg], r=[u, ("W", 2 + fc // 4)])
            P.I("act", lambda e, pg=pg, sg_=sg_, fc=fc: e.activation(out=sg_[:], in_=pg[:], func=AF.Sigmoid, bias=bin_[:, 8 + fc:9 + fc]), w=[sg_], r=[pg, bin_])
            P.I("dve", lambda e, pa=pa, sg_=sg_, fc=fc, o=o: e.scalar_tensor_tensor(out=o[:, fc, :], in0=pa[:], scalar=bin_[:, fc:fc + 1], in1=sg_[:],
                                                                                 op0=ALU.add, op1=ALU.mult), w=[o], r=[pa, sg_, bin_])
        P.D("sp", C.cHG[:, :, 16 + g * 512:16 + (g + 1) * 512].rearrange("k p t -> p k t"), o[:], r=[o], w=[("hg", g)])
    P.D("sp", C.cHG[:, :, 16 + T:32 + T].rearrange("k p t -> p k t"), zt[:], r=[zt], w=[("hghalo",)])
    P.end_phase()


def phase_conv_dw(P, C):
    P.begin_phase()
    T = C.T
    idb = make_ident(P)
    ones = make_ones(P)
    wT = P.sb("wT", [128, 8, 31], F32)
    P.D("sp", wT[:], C.conv_wdwT, w=[wT])
    cols = P.sb("cols", [128, 24], F32)
    P.D("sp", cols[:], C.conv_cols, w=[cols])
    D = P.sb("D", [128, 248, 128], BF16)
    for j in range(31):
        for fc in range(8):
            ve = "dve" if (j + fc) % 2 == 0 else "pool"
            P.I(ve, lambda e, j=j, fc=fc: e.tensor_scalar(out=D[:, j * 8 + fc, :], in0=idb[:], scalar1=wT[:, fc, j:j + 1], scalar2=None, op0=ALU.mult),
                w=[("D", j * 8 + fc)], r=[idb, wT])
    hw = _alt(P, "hw", [128, 8, 542], BF16)
    psc = _palt(P, "psc", [128, 512], F32, 3)
    psm = P.ps("psm", [128, 512], F32)
    pss = P.ps("pss", [128, 512], F32)
    xss = _alt(P, "xs", [128, 8, 512], F32)
    xb = P.sb("xb", [128, 8, 512], BF16)
    sq = P.sb("sq", [128, 8, 512], BF16)
    mt = P.sb("mt", [128, 512], F32)
    m2 = P.sb("m2", [128, 512], F32)
    rs = P.sb("rs", [128, 512], F32)
    tt_ = _alt(P, "tt", [128, 512], F32)
    yTg = _alt(P, "yTg", [128, 8, 512], BF16)
    for g in range(C.NG):
        w_ = hw[g % 2]
        yg = yTg[g % 2]
        xs = xss[g % 2]
        P.D("sp", w_[:], C.cHG[:, :, 1 + g * 512:1 + g * 512 + 542].rearrange("k p t -> p k t"), w=[w_])
        for fc in range(8):
            pc = psc[fc % 3]
            for j in range(31):
                P.I("pe", lambda e, pc=pc, fc=fc, j=j, w_=w_: e.matmul(pc[:], D[:, j * 8 + fc, :], w_[:, fc, j:j + 512], start=(j == 0), stop=(j == 30)),
                    w=[pc], r=[w_] + ([("D", j * 8 + fc)] if g == 0 else []))
            P.I("act", lambda e, pc=pc, fc=fc: e.activation(out=xs[:, fc, :], in_=pc[:], func=AF.Identity, bias=cols[:, fc:fc + 1]),
                w=[(xs.name, fc)], r=[pc, cols])
        xkeys = [(xs.name, fc) for fc in range(8)]
        P.I("act", lambda e: e.activation(out=sq[:], in_=xs[:], func=AF.Square), w=[sq], r=xkeys)
        P.I("dve", lambda e: e.tensor_copy(out=xb[:], in_=xs[:]), w=[xb], r=xkeys)
        for kc in range(8):
            P.I("pe", lambda e, kc=kc: e.matmul(psm[:], ones[:], xb[:, kc, :], start=(kc == 0), stop=(kc == 7)), w=[psm], r=[xb, ones])
        for kc in range(8):
            P.I("pe", lambda e, kc=kc: e.matmul(pss[:], ones[:], sq[:, kc, :], start=(kc == 0), stop=(kc == 7)), w=[pss], r=[sq, ones])
        P.I("act", lambda e: e.activation(out=mt[:], in_=psm[:], func=AF.Copy, scale=1.0 / 1024.0), w=[mt], r=[psm])
        P.I("dve", lambda e: e.tensor_tensor(out=m2[:], in0=mt[:], in1=mt[:], op=ALU.mult), w=[m2], r=[mt])
        P.I("dve", lambda e: e.scalar_tensor_tensor(out=rs[:], in0=pss[:], scalar=1.0 / 1024.0, in1=m2[:], op0=ALU.mult, op1=ALU.subtract),
            w=[rs], r=[pss, m2])
        P.I("act", lambda e: e.activation(out=rs[:], in_=rs[:], func=AF.Sqrt, bias=EPS), w=[rs], r=[rs])
        P.I("dve", lambda e: e.reciprocal(out=rs[:], in_=rs[:]), w=[rs], r=[rs])
        for fc in range(8):
            t = tt_[fc % 2]
            ve = "dve" if fc % 2 == 0 else "pool"
            P.I(ve, lambda e, t=t, fc=fc: e.tensor_tensor(out=t[:], in0=xs[:, fc, :], in1=mt[:], op=ALU.subtract), w=[t], r=[(xs.name, fc), mt])
            P.I(ve, lambda e, t=t: e.tensor_tensor(out=t[:], in0=t[:], in1=rs[:], op=ALU.mult), w=[t], r=[t, rs])
            P.I("act", lambda e, t=t, fc=fc, yg=yg: e.activation(out=yg[:, fc, :], in_=t[:], func=AF.Silu, scale=cols[:, 8 + fc:9 + fc], bias=cols[:, 16 + fc:17 + fc]),
                w=[yg], r=[t, cols])
        P.D("sp", C.cyT[:, :, g * 512:(g + 1) * 512].rearrange("k p t -> p k t"), yg[:], r=[yg], w=[("ysrc", g)])
    P.end_phase()


def layer_conv(P, C, li, h_src, h_dst):
    phase_conv_glu(P, C, li, h_src)
    phase_conv_dw(P, C)
    phase_outproj(P, C, li, 1, C.conv_w_out[0], 8, C.cyT, h_src, h_dst, bias_src=C.conv_bout)


def declare_extra(nc, C, kinds, din, dint):
    T = C.T
    if "gla" in kinds:
        C.gla_w_in = din("gla_w_in", [1, 1024, 3072])
        C.gla_w1 = din("gla_w1", [2, 1024, 16])
        C.gla_w2 = din("gla_w2", [2, 16, 512])
        C.gla_b = din("gla_b", [2, 512])
        C.gla_ng = din("gla_ng", [128, 256])
        C.gla_w_out = din("gla_w_out", [1, 1024, 1024])
        C.gQF = dint("gQF", [4, 128, T]); C.gQB = dint("gQB", [4, 128, T])
        C.gKF = dint("gKF", [4, 128, T]); C.gKB = dint("gKB", [4, 128, T])
        C.gkEF = dint("gkEF", [T, 512]); C.gkEB = dint("gkEB", [T, 512])
        C.gv = dint("gv", [T, 1024]); C.gsg = dint("gsg", [T, 1024]); C.gP1 = dint("gP1", [T, 1024])
        C.gyT = dint("gyT", [8, 128, T])
        C.gDEC = dint("gDEC", [2, 128, 4 * C.NCH], F32)
        C.xin2 = dint("xin2", [512, 256], F32)
        C.xout2 = dint("xout2", [1024, 256], F32)
    if "conv" in kinds:
        C.conv_w_in = din("conv_w_in", [1, 1024, 2048])
        C.conv_bin = din("conv_bin", [128, 16])
        C.conv_wdwT = din("conv_wdwT", [128, 8, 31])
        C.conv_cols = din("conv_cols", [128, 24])
        C.conv_w_out = din("conv_w_out", [1, 1024, 1024])
        C.conv_bout = din("conv_bout", [128, 8])
        C.cHG = dint("cHG", [8, 128, T + 32])
        C.cyT = dint("cyT", [8, 128, T])
        C.xin3 = dint("xin3", [1024, 16])
        C.xout3 = dint("xout3", [2048, 16])


def extra_in_maps(m, inputs, T, b, half, kinds):
    f = lambda k: np.asarray(inputs[k], np.float32)
    if "gla" in kinds:
        m["gla_w_in"] = f("gla_w_in")
        sw = (lambda a: a) if half == 0 else (lambda a: a[::-1])
        m["gla_w1"] = np.ascontiguousarray(sw(f("gla_gate_w1")[0]))
        m["gla_w2"] = np.ascontiguousarray(sw(f("gla_gate_w2")[0]))
        m["gla_b"] = np.ascontiguousarray(sw(f("gla_gate_b")[0]))
        m["gla_ng"] = np.ascontiguousarray(np.broadcast_to(f("gla_norm_gain")[0][None, :], (128, 256)))
        m["gla_w_out"] = f("gla_w_out")
    if "conv" in kinds:
        m["conv_w_in"] = f("conv_w_in")
        m["conv_bin"] = np.ascontiguousarray(f("conv_b_in")[0].reshape(16, 128).T)
        wdw = f("conv_w_dw")[0]
        if half == 1:
            wdw = wdw[::-1]
        m["conv_wdwT"] = np.ascontiguousarray(wdw.reshape(31, 8, 128).transpose(2, 1, 0))
        rows = np.stack([f("conv_b_dw")[0], f("conv_ln_gain")[0], f("conv_ln_bias")[0]], axis=0)
        m["conv_cols"] = np.ascontiguousarray(rows.reshape(3, 8, 128).transpose(2, 0, 1).reshape(128, 24))
        m["conv_w_out"] = f("conv_w_out")
        m["conv_bout"] = np.ascontiguousarray(f("conv_b_out")[0].reshape(8, 128).T)


CST_LAYOUT = [("maskLF", 128), ("maskLB", 128), ("A1", 128), ("A2", 128), ("A3", 128), ("EF", 512), ("EB", 512),
              ("c127", 1), ("cs", 1), ("c128", 1), ("invf", 1), ("sel", 2),
              ("triU", 129), ("triL", 129), ("sL", 128), ("sU", 128)]


def cst_offsets():
    off = {}
    o = 0
    for nm, w in CST_LAYOUT:
        off[nm] = (o, w)
        o += w
    return off, o


def make_cst(half):
    off, n = cst_offsets()
    c = np.zeros((128, n), np.float32)
    s = np.arange(128)[:, None].astype(np.float64)
    t = np.arange(128)[None, :].astype(np.float64)

    def put(nm, a):
        o, w = off[nm]
        c[:, o:o + w] = np.broadcast_to(a, (128, w))
    put("maskLF", (s <= t) if half == 0 else (s < t))
    put("maskLB", (s > t) if half == 0 else (s >= t))
    put("A1", -(s + 1) + 0 * t)
    put("A2", s - t)
    put("A3", -(t + 1) + 0 * s)
    tt = (np.arange(512) % 128)[None, :]
    put("EF", tt + 1.0)
    put("EB", 128.0 - tt)
    put("c127", 127.0 - s)
    put("cs", s)
    put("c128", 128.0)
    put("invf", (10000.0 ** (-(np.arange(128, dtype=np.float32) / np.float32(128)))).astype(np.float32)[:, None])
    put("sel", np.array([[0.0, 1.0]]) if half == 0 else np.array([[1.0, 0.0]]))
    put("triU", np.concatenate([(s <= t), np.ones((128, 1))], axis=1))
    put("triL", np.concatenate([(s >= t), np.ones((128, 1))], axis=1))
    put("sL", (s > t))
    put("sU", (s < t))
    return c


def build(T, plan):
    nc = bass.Bass("TRN2", target_bir_lowering=False)
    C = Ctx()
    C.T, C.NG, C.NCH = T, T // 512, T // 128
    C.cst_off, ncst = cst_offsets()

    def din(name, shape, dt=F32):
        return nc.dram_tensor(name, list(shape), dt, kind="ExternalInput").ap()

    def dint(name, shape, dt=BF16):
        return nc.dram_tensor(name, list(shape), dt).ap()
    C.xT = din("xT", [8, 128, T])
    C.posr = din("posr", [128, T], I32)
    C.gcol = din("gcol", [128, 128])
    C.cst = din("cst", [128, ncst])
    kinds = set(k for k, _, _ in plan)
    C.ret_dl = din("ret_dl", [2, 128, 8])
    if any(k.startswith("p_") for k in kinds):
        kinds = kinds | {"ret"}
    if "ret" in kinds:
        C.ret_w_in = din("ret_w_in", [2, 1024, 6144])
        C.ret_w_out = din("ret_w_out", [2, 2048, 1024])
    if "mlp" in kinds:
        C.mlp_w_up = din("mlp_w_up", [4, 1024, 4096])
        C.mlp_w_down = din("mlp_w_down", [4, 4096, 1024])
    declare_extra(nc, C, kinds, din, dint)
    C.outT = nc.dram_tensor("outT", [8, 128, T], F32, kind="ExternalOutput").ap()
    C.hA = dint("hA", [8, 128, T], F32)
    C.qF = dint("qF", [8, 128, T])
    C.qB = dint("qB", [8, 128, T])
    C.kT = dint("kT", [8, 128, T])
    C.kB = dint("kB", [8, 128, T])
    C.kEF = dint("kEF", [T, 1024])
    C.kEB = dint("kEB", [T, 1024])
    C.v = dint("v", [T, 2048])
    C.sg = dint("sg", [T, 2048])
    C.P1 = dint("P1", [T, 2048])
    C.yT = dint("yT", [16, 128, T])
    C.hid = dint("hid", [32, 128, T])
    C.xin = dint("xin", [1024, 512], F32)
    C.xout = dint("xout", [2048, 512], F32)
    P = Prog(nc)
    nsteps = len(plan)
    src = C.xT
    for i, (kind, li, j) in enumerate(plan):
        dst = C.outT if i == nsteps - 1 else C.hA
        if kind == "ret":
            layer_ret(P, C, li, j, src, dst)
        elif kind == "mlp":
            layer_mlp(P, C, li, src, dst)
        elif kind == "p_qk":
            phase_ret_qk(P, C, li, j, src)
        elif kind == "p_vg":
            phase_vg(P, C, li, src, C.ret_w_in[j][:, 2048:6144], 2048, C.v, C.sg)
        elif kind == "p_s1":
            phase_ret_sweep1(P, C, j)
        elif kind == "p_s2":
            phase_ret_sweep2(P, C, j)
        elif kind == "p_out":
            phase_outproj(P, C, li, 1, C.ret_w_out[j], 16, C.yT, src, dst)
        elif kind == "gla":
            layer_gla(P, C, li, src, dst)
        elif kind == "conv":
            layer_conv(P, C, li, src, dst)
        src = dst
    P.begin_phase()
    P.end_phase(final=True)
    P.close()
    return nc, P


FULL_PLAN = [("ret", 0, 0), ("mlp", 0, 0), ("conv", 1, 0), ("mlp", 1, 0), ("gla", 2, 0), ("mlp", 2, 0), ("ret", 3, 1), ("mlp", 3, 0)]


def make_in_maps(inputs, T, ncores=4, kinds=("ret", "mlp", "conv", "gla")):
    x = np.asarray(inputs["x"], np.float32)
    pos = np.asarray(inputs["positions"], np.int32)
    ng = np.asarray(inputs["norm_gains"], np.float32)
    L = ng.shape[0]
    gcol = np.zeros((128, 128), np.float32)
    gcol[:, :L * 32] = ng.reshape(L, 4, 8, 128).transpose(3, 0, 1, 2).reshape(128, L * 32)
    rdl = np.asarray(inputs["ret_decay_logit"], np.float32)
    cst = make_cst(0)
    maps = []
    for b in range(ncores):
        m = {}
        m["xT"] = np.ascontiguousarray(x[b].reshape(T, 8, 128).transpose(1, 2, 0))
        m["posr"] = np.ascontiguousarray(np.broadcast_to(pos[b][None, :], (128, T)))
        m["gcol"] = gcol
        m["cst"] = cst
        m["ret_dl"] = np.ascontiguousarray(np.broadcast_to(rdl.reshape(2, 1, 8), (2, 128, 8)))
        big = []
        if "ret" in kinds:
            big += ["ret_w_in", "ret_w_out"]
        if "mlp" in kinds:
            big += ["mlp_w_up", "mlp_w_down"]
        for k in big:
            m[k] = np.asarray(inputs[k], np.float32)
        extra_in_maps(m, inputs, T, b, 0, kinds)
        maps.append(m)
    return maps


def gather_out(res, T, ncores=4):
    out = np.zeros((ncores, T, 1024), np.float32)
    for b in range(ncores):
        out[b] = res.results[b]["outT"].reshape(1024, T).T
    return out


_CACHE = {}


def kernel(**inputs):
    T = 8192
    if "nc" not in _CACHE:
        _CACHE["nc"] = build(T, FULL_PLAN)[0]
    nc = _CACHE["nc"]
    maps = make_in_maps(inputs, T, ncores=4)
    res = run_bass_kernel_spmd(nc, maps, core_ids=list(range(4)))
    return gather_out(res, T, 4)
```

```python
import numpy as np
from contextlib import ExitStack
import concourse.bass as bass
import concourse.mybir as mybir
from concourse.bass_utils import run_bass_kernel_spmd

F32 = mybir.dt.float32
BF16 = mybir.dt.bfloat16
I32 = mybir.dt.int32
AF = mybir.ActivationFunctionType
ALU = mybir.AluOpType

ENGS = ("pe", "act", "dve", "pool", "sp")


def _key(x):
    if isinstance(x, (str, tuple)):
        return x
    t = getattr(x, "tensor", None)
    return t.name if t is not None else x.name


class _Op:
    __slots__ = ("eng", "fn", "deps", "sig", "dma", "waits")


class Prog:
    N_DMA_SEMS = {"sp": 16, "pool": 12, "act": 4}

    def __init__(self, nc):
        self.nc = nc
        self.es = ExitStack()
        self.sem = {e: self.es.enter_context(nc.semaphore("s_" + e)) for e in ENGS if e != "sp"}
        self.sem["sp"] = self.es.enter_context(nc.semaphore("s_sp"))
        self.cnt = {e: 0 for e in ENGS}
        self.dsem = {q: [self.es.enter_context(nc.semaphore("d_%s%d" % (q, i))) for i in range(n)]
                     for q, n in self.N_DMA_SEMS.items()}
        self.dcnt = {q: [0] * n for q, n in self.N_DMA_SEMS.items()}
        self.dlast = {q: [None] * n for q, n in self.N_DMA_SEMS.items()}
        self.drr = {q: 0 for q in self.N_DMA_SEMS}
        self.known = {e: {} for e in ENGS}
        self.res_w = {}
        self.res_r = {}
        self.ops = []
        self.phase_es = None
        self.uid = 0
        self.last_sig = {e: None for e in ENGS}
        self.n_inst = 0

    def begin_phase(self):
        self.phase_es = ExitStack()
        self.nph = getattr(self, "nph", 0) + 1
        self.sem["pe"] = self.es.enter_context(self.nc.semaphore("s_pe%d" % self.nph))
        self.cnt["pe"] = 0

    def sb(self, name, shape, dtype):
        self.uid += 1
        return self.phase_es.enter_context(self.nc.sbuf_tensor("%s_%d" % (name, self.uid), list(shape), dtype))

    def ps(self, name, shape, dtype):
        self.uid += 1
        return self.phase_es.enter_context(self.nc.psum_tensor("%s_%d" % (name, self.uid), list(shape), dtype))

    def _deps(self, r, w):
        deps = []
        for x in r:
            k = _key(x)
            s = self.res_w.get(k)
            if s is not None:
                deps.append(s)
        for x in w:
            k = _key(x)
            s = self.res_w.get(k)
            if s is not None:
                deps.append(s)
            deps.extend(self.res_r.get(k, ()))
        return deps

    def _commit(self, sig, r, w):
        for x in r:
            self.res_r.setdefault(_key(x), []).append(sig)
        for x in w:
            k = _key(x)
            self.res_w[k] = sig
            self.res_r[k] = []

    def I(self, eng, fn, w=(), r=()):
        deps = self._deps(r, w)
        if eng == "pe":
            deps = [d for d in deps if d[0] is not self.sem["pe"]]
        self.cnt[eng] += 1
        sig = (self.sem[eng], self.cnt[eng])
        op = _Op()
        op.eng, op.fn, op.sig, op.dma = eng, fn, sig, False
        op.waits = self._waits(eng, deps)
        self.ops.append(op)
        self._commit(sig, r, w)
        self.last_sig[eng] = sig
        return sig

    def D(self, q, out, in_, w=(), r=(), **kw):
        return self.Dfn(q, lambda e: e.dma_start(out=out, in_=in_, **kw), w=w, r=r)

    def Dfn(self, q, fn, w=(), r=()):
        deps = self._deps(r, w)
        i = self.drr[q]
        self.drr[q] = (i + 1) % len(self.dsem[q])
        if self.dlast[q][i] is not None:
            deps.append(self.dlast[q][i])
        self.dcnt[q][i] += 1
        sig = (self.dsem[q][i], 16 * self.dcnt[q][i])
        self.dlast[q][i] = sig
        op = _Op()
        op.eng, op.sig, op.dma = q, sig, True
        op.fn = fn
        op.waits = self._waits(q, deps)
        self.ops.append(op)
        self._commit(sig, r, w)
        return sig

    def _waits(self, eng, deps):
        best = {}
        for (s, v) in deps:
            if best.get(id(s), (None, 0))[1] < v:
                best[id(s)] = (s, v)
        out = []
        kn = self.known[eng]
        for sid, (s, v) in best.items():
            if kn.get(sid, 0) >= v:
                continue
            kn[sid] = v
            out.append((s, v))
        return out

    def barrier(self):
        sigs = [s for s in self.last_sig.values() if s is not None]
        for q in self.dlast:
            sigs.extend(s for s in self.dlast[q] if s is not None)
        for e in ENGS:
            ws = self._waits(e, sigs)
            if ws:
                op = _Op()
                op.eng, op.fn, op.sig, op.dma, op.waits = e, None, None, False, ws
                self.ops.append(op)

    def wait_all(self, eng):
        sigs = [s for s in self.last_sig.values() if s is not None]
        for q in self.dlast:
            sigs.extend(s for s in self.dlast[q] if s is not None)
        ws = self._waits(eng, sigs)
        if ws:
            op = _Op()
            op.eng, op.fn, op.sig, op.dma, op.waits = eng, None, None, False, ws
            self.ops.append(op)

    def end_phase(self, final=False):
        self.barrier()
        if final:
            self.wait_all("sp")
        ops = self.ops
        self.ops = []
        per = {e: [o for o in ops if o.eng == e] for e in ENGS}
        self.n_inst += len(ops)
        if not hasattr(self, "remap"):
            self.remap = {}
            self.newcnt = {}
        eng_ids = set(id(x) for x in self.sem.values())
        targets = set()
        for o in ops:
            for (sm, v) in o.waits:
                if id(sm) in eng_ids:
                    targets.add((id(sm), v))
        for o in ops:
            if o.fn is None or o.dma:
                continue
            sm, v = o.sig
            if (id(sm), v) in targets:
                self.newcnt[id(sm)] = self.newcnt.get(id(sm), 0) + 1
                self.remap[(id(sm), v)] = self.newcnt[id(sm)]
        remap = self.remap

        def replay(e, lst):
            for o in lst:
                for (sm, v) in o.waits:
                    e.wait_ge(sm, remap.get((id(sm), v), v) if id(sm) in eng_ids else v)
                if o.fn is not None:
                    ins = o.fn(e)
                    if o.dma:
                        ins.then_inc(o.sig[0], 16)
                    elif (id(o.sig[0]), o.sig[1]) in remap:
                        ins.then_inc(o.sig[0], 1)

        with self.nc.Block() as block:
            if per["pe"]:
                block.tensor(lambda e: replay(e, per["pe"]))
            if per["act"]:
                block.scalar(lambda e: replay(e, per["act"]))
            if per["dve"]:
                block.vector(lambda e: replay(e, per["dve"]))
            if per["pool"]:
                block.gpsimd(lambda e: replay(e, per["pool"]))
            if per["sp"]:
                block.sync(lambda e: replay(e, per["sp"]))
        self.phase_es.close()
        self.phase_es = None
        self.res_w = {k: v for k, v in self.res_w.items() if isinstance(k, tuple)}
        self.res_r = {k: v for k, v in self.res_r.items() if isinstance(k, tuple)}

    def close(self):
        self.es.close()

PI = float(np.pi)
EPS = 1e-6


class Ctx:
    pass


def _alt(P, name, shape, dtype, n=2):
    return [P.sb("%s%d" % (name, i), shape, dtype) for i in range(n)]


def _palt(P, name, shape, dtype, n=2):
    return [P.ps("%s%d" % (name, i), shape, dtype) for i in range(n)]


def load_consts(P, C, names):
    out = {}
    for nm in names:
        off, w = C.cst_off[nm]
        t = P.sb("c_" + nm, [128, w], F32)
        P.D("sp", t[:], C.cst[:, off:off + w], w=[t], allow_slow_non_contiguous=True)
        out[nm] = t
    return out


def make_ident(P, dtype=BF16):
    idb = P.sb("ident", [128, 128], dtype)
    P.I("dve", lambda e: e.memset(idb[:], 0.0), w=[idb])
    P.I("pool", lambda e: e.affine_select(out=idb[:], in_=idb[:], pattern=[[-1, 128]], compare_op=ALU.not_equal,
                                          fill=1.0, base=0, channel_multiplier=1), w=[idb], r=[idb])
    return idb


def make_ones(P):
    o = P.sb("ones", [128, 128], BF16)
    P.I("dve", lambda e: e.memset(o[:], 1.0), w=[o])
    return o


def load_gcol(P, C, li, i):
    g = P.sb("gcol", [128, 8], F32)
    o = (li * 4 + i) * 8
    P.D("sp", g[:], C.gcol[:, o:o + 8], w=[g])
    return g


class Norm:
    def __init__(self, P, nk=8, G=512):
        self.P = P
        self.ones = make_ones(P)
        self.sq = P.sb("nsq", [128, nk, G], BF16)
        self.ps = P.ps("nps", [128, G], F32)
        self.rstd = P.sb("nrstd", [128, G], F32)
        self.nk = nk

    def stats_a(self, x):
        self.P.I("act", lambda e: e.activation(out=self.sq[:], in_=x[:], func=AF.Square), w=[self.sq], r=[x])

    def stats(self, x, d_model, skip_a=False):
        P = self.P
        nk = self.nk
        if not skip_a:
            self.stats_a(x)
        for kc in range(nk):
            P.I("pe", lambda e, kc=kc: e.matmul(self.ps[:], self.ones[:], self.sq[:, kc, :], start=(kc == 0), stop=(kc == nk - 1)),
                w=[self.ps], r=[self.sq, self.ones])
        P.I("act", lambda e: e.activation(out=self.rstd[:], in_=self.ps[:], func=AF.Sqrt, scale=1.0 / d_model, bias=EPS),
            w=[self.rstd], r=[self.ps])
        P.I("dve", lambda e: e.reciprocal(out=self.rstd[:], in_=self.rstd[:]), w=[self.rstd], r=[self.rstd])

    def pre(self, hT, gcol, uT, skip_a=False):
        P = self.P
        self.stats(hT, 1024.0, skip_a)
        for kc in range(8):
            P.I("dve", lambda e, kc=kc: e.scalar_tensor_tensor(out=uT[:, kc, :], in0=hT[:, kc, :], scalar=gcol[:, kc:kc + 1],
                                                              in1=self.rstd[:], op0=ALU.mult, op1=ALU.mult),
                w=[uT], r=[hT, gcol, self.rstd])

    def post(self, yo, gcol, hT, skip_a=False):
        P = self.P
        self.stats(yo, 1024.0, skip_a)
        for kc in range(8):
            P.I("dve", lambda e, kc=kc: e.scalar_tensor_tensor(out=yo[:, kc, :], in0=yo[:, kc, :], scalar=gcol[:, kc:kc + 1],
                                                              in1=self.rstd[:], op0=ALU.mult, op1=ALU.mult),
                w=[yo], r=[yo, gcol, self.rstd])
        P.I("pool", lambda e: e.tensor_tensor(out=hT[:], in0=hT[:], in1=yo[:], op=ALU.add), w=[hT], r=[hT, yo])


def hview(ap, g, G=512):
    return ap[:, :, g * G:(g + 1) * G].rearrange("k p t -> p k t")


def load_w(P, W, src, nk, ncols, blk=512, key="W"):
    for cb in range(ncols // blk):
        for k0 in range(0, nk, 8):
            k1 = min(nk, k0 + 8)
            P.D("pool", W[:, k0:k1, cb * blk:(cb + 1) * blk],
                src[k0 * 128:k1 * 128, cb * blk:(cb + 1) * blk].rearrange("(kc p) n -> p kc n", p=128),
                w=[(key, cb)])


def phase_outproj(P, C, li, gi, w_src, nk, y_src, h_src, h_dst, bias_src=None):
    P.begin_phase()
    W = P.sb("W", [128, nk, 1024], BF16)
    load_w(P, W, w_src, nk, 1024)
    gcol = load_gcol(P, C, li, gi)
    nrm = Norm(P)
    bcol = None
    if bias_src is not None:
        bcol = P.sb("bcol", [128, 8], F32)
        P.D("sp", bcol[:], bias_src, w=[bcol])
    yT = _alt(P, "yT", [128, nk, 512], BF16)
    hT = _alt(P, "hT", [128, 8, 512], F32)
    yos = _alt(P, "yo", [128, 8, 512], F32)
    pso = _palt(P, "pso", [128, 512], F32, 3)

    def finish_a(g):
        nrm.stats_a(yos[g % 2])

    def finish(g):
        nrm.post(yos[g % 2], gcol, hT[g % 2], skip_a=True)
        P.D("sp", hview(h_dst, g), hT[g % 2][:], r=[hT[g % 2]], w=[("h", g)])

    def load_y(g):
        P.D("sp", yT[g % 2][:], y_src[:, :, g * 512:(g + 1) * 512].rearrange("k p t -> p k t"), w=[yT[g % 2]], r=[("ysrc", g)])

    load_y(0)
    for g in range(C.NG + 1):
        if g < C.NG:
            y = yT[g % 2]
            h = hT[g % 2]
            yo = yos[g % 2]
            if g + 1 < C.NG:
                load_y(g + 1)
            P.D("sp", h[:], hview(h_src, g), w=[h], r=[("h", g)])
            for oc in range(8):
                ps = pso[oc % 3]
                for kc in range(nk):
                    P.I("pe", lambda e, ps=ps, kc=kc, oc=oc, y=y: e.matmul(ps[:], W[:, kc, oc * 128:(oc + 1) * 128], y[:, kc, :],
                                                                         start=(kc == 0), stop=(kc == nk - 1)),
                        w=[ps], r=[y, ("W", oc // 4)])
                if bcol is None:
                    P.I("act", lambda e, ps=ps, oc=oc, yo=yo: e.activation(out=yo[:, oc, :], in_=ps[:], func=AF.Copy), w=[yo], r=[ps])
                else:
                    P.I("act", lambda e, ps=ps, oc=oc, yo=yo: e.activation(out=yo[:, oc, :], in_=ps[:], func=AF.Identity,
                                                                        bias=bcol[:, oc:oc + 1]), w=[yo], r=[ps, bcol])
                if oc == 0 and g > 0:
                    finish_a(g - 1)
                if oc == 3 and g > 0:
                    finish(g - 1)
        else:
            finish_a(g - 1)
            finish(g - 1)
    P.end_phase()


def phase_mlp_up(P, C, li, h_src):
    P.begin_phase()
    W = P.sb("W", [128, 8, 4096], BF16)
    load_w(P, W, C.mlp_w_up[li], 8, 4096)
    gcol = load_gcol(P, C, li, 2)
    nrm = Norm(P)
    hT = _alt(P, "hT", [128, 8, 512], F32)
    uT = _alt(P, "uT", [128, 8, 512], BF16)
    hid = _alt(P, "hid", [128, 32, 512], BF16)
    psu = _palt(P, "psu", [128, 512], F32, 4)
    rtmp = _alt(P, "rtmp", [128, 512], F32)
    def prep_l(g):
        P.D("sp", hT[g % 2][:], hview(h_src, g), w=[hT[g % 2]], r=[("h", g)])

    def prep_a(g):
        nrm.stats_a(hT[g % 2])

    def prep_b(g):
        nrm.pre(hT[g % 2], gcol, uT[g % 2], skip_a=True)

    prep_l(0)
    prep_a(0)
    prep_b(0)
    for g in range(C.NG):
        u = uT[g % 2]
        hd = hid[g % 2]
        for hc in range(32):
            if hc == 0 and g + 1 < C.NG:
                prep_l(g + 1)
            if hc == 16 and g + 1 < C.NG:
                prep_a(g + 1)
            if hc == 22 and g + 1 < C.NG:
                prep_b(g + 1)
            ps = psu[hc % 4]
            for kc in range(8):
                P.I("pe", lambda e, ps=ps, kc=kc, hc=hc, u=u: e.matmul(ps[:], W[:, kc, hc * 128:(hc + 1) * 128], u[:, kc, :],
                                                                     start=(kc == 0), stop=(kc == 7)),
                    w=[ps], r=[u, ("W", hc // 4)])
            rt = rtmp[hc % 2]
            P.I("act", lambda e, ps=ps, rt=rt: e.activation(out=rt[:], in_=ps[:], func=AF.Relu), w=[rt], r=[ps])
            ve = "dve" if hc % 2 == 0 else "pool"
            P.I(ve, lambda e, rt=rt, hc=hc, hd=hd: e.tensor_tensor(out=hd[:, hc, :], in0=rt[:], in1=rt[:], op=ALU.mult),
                w=[(hd.name, hc)], r=[rt])
        P.D("sp", C.hid[:, :, g * 512:(g + 1) * 512].rearrange("k p t -> p k t"), hd[:],
            r=[(hd.name, hc) for hc in range(32)], w=[("ysrc", g)])
    P.end_phase()


def phase_mlp_down(P, C, li, h_src, h_dst):
    phase_outproj(P, C, li, 3, C.mlp_w_down[li], 32, C.hid, h_src, h_dst)


def ret_lg(P, C, j):
    dl = P.sb("dl", [128, 8], F32)
    lg = P.sb("lg", [128, 8], F32)
    P.D("sp", dl[:], C.ret_dl[j], w=[dl])
    P.I("act", lambda e: e.activation(out=lg[:], in_=dl[:], func=AF.Exp, scale=-1.0), w=[lg], r=[dl])
    P.I("act", lambda e: e.activation(out=lg[:], in_=lg[:], func=AF.Ln, bias=1.0), w=[lg], r=[lg])
    P.I("dve", lambda e: e.tensor_scalar(out=lg[:], in0=lg[:], scalar1=-1.0, scalar2=None, op0=ALU.mult), w=[lg], r=[lg])
    return lg


def rope_tables(P, C, g, K, cos, sin):
    pi_, pf, ang, ki, kf = K["pi"], K["pf"], K["ang"], K["ki"], K["kf"]
    P.D("sp", pi_[:], C.posr[:, g * 512:(g + 1) * 512], w=[pi_])
    P.I("dve", lambda e: e.tensor_copy(out=pf[:], in_=pi_[:]), w=[pf], r=[pi_])
    P.I("dve", lambda e: e.tensor_scalar(out=ang[:], in0=pf[:], scalar1=K["invf"][:, 0:1], scalar2=None, op0=ALU.mult),
        w=[ang], r=[pf, K["invf"]])
    P.I("dve", lambda e: e.tensor_scalar(out=ki[:], in0=ang[:], scalar1=float(1 / (2 * np.pi)), scalar2=None, op0=ALU.mult),
        w=[ki], r=[ang])
    P.I("dve", lambda e: e.tensor_copy(out=kf[:], in_=ki[:]), w=[kf], r=[ki])
    P.I("dve", lambda e: e.scalar_tensor_tensor(out=ang[:], in0=kf[:], scalar=float(-2 * np.pi), in1=ang[:], op0=ALU.mult, op1=ALU.add),
        w=[ang], r=[ang, kf])
    for (dst, shift) in ((sin, 0.0), (cos, PI / 2)):
        if shift != 0.0:
            P.I("dve", lambda e, dst=dst, shift=shift: e.tensor_scalar(out=dst[:], in0=ang[:], scalar1=shift, scalar2=None, op0=ALU.add),
                w=[dst], r=[ang])
            src = dst
        else:
            src = ang
        P.I("dve", lambda e, src=src: e.tensor_scalar(out=kf[:], in0=src[:], scalar1=PI, scalar2=float(2 * np.pi), op0=ALU.is_gt, op1=ALU.mult),
            w=[kf], r=[src])
        P.I("dve", lambda e, src=src, dst=dst: e.tensor_tensor(out=dst[:], in0=src[:], in1=kf[:], op=ALU.subtract), w=[dst], r=[src, kf])
        P.I("act", lambda e, dst=dst: e.activation(out=dst[:], in_=dst[:], func=AF.Sin), w=[dst], r=[dst])


def phase_ret_qk(P, C, li, j, h_src):
    P.begin_phase()
    W = P.sb("W", [128, 8, 2048], BF16)
    load_w(P, W, C.ret_w_in[j][:, 0:2048], 8, 2048)
    gcol = load_gcol(P, C, li, 0)
    nrm = Norm(P)
    idb = make_ident(P)
    cs = load_consts(P, C, ["EF", "EB", "c127", "cs", "invf"])
    lg = ret_lg(P, C, j)
    aF = P.sb("aF", [128, 4, 512], F32)
    aB = P.sb("aB", [128, 4, 512], F32)
    colE = P.sb("colE", [128, 8], F32)
    for h in range(4):
        P.I("act", lambda e, h=h: e.activation(out=aF[:, h, :], in_=cs["EF"][:], func=AF.Exp, scale=lg[:, h:h + 1]), w=[aF], r=[cs["EF"], lg])
        P.I("act", lambda e, h=h: e.activation(out=aB[:, h, :], in_=cs["EB"][:], func=AF.Exp, scale=lg[:, 4 + h:5 + h]), w=[aB], r=[cs["EB"], lg])
        P.I("act", lambda e, h=h: e.activation(out=colE[:, h:h + 1], in_=cs["c127"][:], func=AF.Exp, scale=lg[:, h:h + 1]), w=[colE], r=[cs["c127"], lg])
        P.I("act", lambda e, h=h: e.activation(out=colE[:, 4 + h:5 + h], in_=cs["cs"][:], func=AF.Exp, scale=lg[:, 4 + h:5 + h]), w=[colE], r=[cs["cs"], lg])
    P.I("dve", lambda e: e.tensor_scalar(out=aF[:], in0=aF[:], scalar1=0.0625, scalar2=None, op0=ALU.mult), w=[aF], r=[aF])
    P.I("dve", lambda e: e.tensor_scalar(out=aB[:], in0=aB[:], scalar1=0.0625, scalar2=None, op0=ALU.mult), w=[aB], r=[aB])
    K = {"pi": P.sb("pi", [128, 512], I32), "pf": P.sb("pf", [128, 512], F32), "ang": P.sb("ang", [128, 512], F32),
         "ki": P.sb("ki", [128, 512], I32), "kf": P.sb("kf", [128, 512], F32), "invf": cs["invf"]}
    cos = P.sb("cos", [128, 512], F32)
    sin = P.sb("sin", [128, 512], F32)
    hT = _alt(P, "hT", [128, 8, 512], F32)
    uT = _alt(P, "uT", [128, 8, 512], BF16)
    psq = _palt(P, "psq", [128, 2, 512], F32, 2)
    pst = _palt(P, "pst", [128, 256], BF16, 2)
    t12 = _alt(P, "t12", [128, 2, 512], F32)
    tmp = _alt(P, "tmp", [128, 4, 512], F32)
    rr = _alt(P, "rr", [128, 2, 512], F32)
    kb = _alt(P, "kb", [128, 2, 512], BF16)
    qo = _alt(P, "qo", [128, 4, 512], BF16)
    kE = _alt(P, "kE", [128, 2, 4, 256], BF16)
    it = 0
    def prep_l(g):
        P.D("sp", hT[g % 2][:], hview(h_src, g), w=[hT[g % 2]], r=[("h", g)])

    def prep_a(g):
        nrm.stats_a(hT[g % 2])

    def prep_b(g):
        nrm.pre(hT[g % 2], gcol, uT[g % 2], skip_a=True)

    prep_l(0)
    prep_a(0)
    prep_b(0)
    qk_pending = []
    for g in range(C.NG):
        u = uT[g % 2]
        rope_tables(P, C, g, K, cos, sin)
        for h in range(4):
            for qk in range(2):
                if h == 0 and qk == 0 and g + 1 < C.NG:
                    prep_l(g + 1)
                if h == 2 and qk == 0 and g + 1 < C.NG:
                    prep_a(g + 1)
                if h == 3 and qk == 0 and g + 1 < C.NG:
                    prep_b(g + 1)
                base = qk * 1024 + h * 256
                ps = psq[it % 2]
                t = t12[it % 2]
                tm = tmp[it % 2]
                r_ = rr[it % 2]
                ve = "pool" if it % 3 == 2 else "dve"
                it += 1
                for c in range(2):
                    for kc in range(8):
                        P.I("pe", lambda e, ps=ps, c=c, kc=kc, base=base, u=u: e.matmul(ps[:, c, :], W[:, kc, base + c * 128:base + (c + 1) * 128],
                                                                                     u[:, kc, :], start=(kc == 0), stop=(kc == 7)),
                            w=[ps], r=[u, ("W", base // 512)])
                for t_ in qk_pending:
                    t_()
                qk_pending.clear()
                P.I("act", lambda e, ps=ps, t=t: e.activation(out=t[:], in_=ps[:], func=AF.Copy), w=[t], r=[ps])
                P.I(ve, lambda e, t=t, tm=tm: e.tensor_tensor(out=tm[:, 0, :], in0=t[:, 0, :], in1=cos[:], op=ALU.mult), w=[tm], r=[t, cos])
                P.I(ve, lambda e, t=t, tm=tm: e.tensor_tensor(out=tm[:, 1, :], in0=t[:, 1, :], in1=sin[:], op=ALU.mult), w=[tm], r=[t, sin])
                P.I(ve, lambda e, t=t, tm=tm: e.tensor_tensor(out=tm[:, 2, :], in0=t[:, 1, :], in1=cos[:], op=ALU.mult), w=[tm], r=[t, cos])
                P.I(ve, lambda e, t=t, tm=tm: e.tensor_tensor(out=tm[:, 3, :], in0=t[:, 0, :], in1=sin[:], op=ALU.mult), w=[tm], r=[t, sin])
                if qk == 0:
                    q = qo[h % 2]
                    P.I(ve, lambda e, tm=tm, r_=r_: e.tensor_tensor(out=r_[:, 0, :], in0=tm[:, 0, :], in1=tm[:, 1, :], op=ALU.subtract), w=[r_], r=[tm])
                    P.I(ve, lambda e, tm=tm, r_=r_: e.tensor_tensor(out=r_[:, 1, :], in0=tm[:, 2, :], in1=tm[:, 3, :], op=ALU.add), w=[r_], r=[tm])
                    for c in range(2):
                        P.I(ve, lambda e, r_=r_, q=q, c=c, h=h: e.tensor_tensor(out=q[:, c, :], in0=r_[:, c, :], in1=aF[:, h, :], op=ALU.mult), w=[q], r=[r_, aF])
                        P.I(ve, lambda e, r_=r_, q=q, c=c, h=h: e.tensor_tensor(out=q[:, 2 + c, :], in0=r_[:, c, :], in1=aB[:, h, :], op=ALU.mult), w=[q], r=[r_, aB])
                    P.D("sp", C.qF[h * 2:h * 2 + 2, :, g * 512:(g + 1) * 512].rearrange("c p t -> p c t"), q[:, 0:2, :], r=[q], w=[("qF", g, h)])
                    P.D("sp", C.qB[h * 2:h * 2 + 2, :, g * 512:(g + 1) * 512].rearrange("c p t -> p c t"), q[:, 2:4, :], r=[q], w=[("qB", g, h)])
                else:
                    k = kb[h % 2]
                    ke = kE[h % 2]
                    P.I(ve, lambda e, tm=tm, k=k: e.tensor_tensor(out=k[:, 0, :], in0=tm[:, 0, :], in1=tm[:, 1, :], op=ALU.subtract), w=[k], r=[tm])
                    P.I(ve, lambda e, tm=tm, k=k: e.tensor_tensor(out=k[:, 1, :], in0=tm[:, 2, :], in1=tm[:, 3, :], op=ALU.add), w=[k], r=[tm])
                    P.D("sp", C.kT[h * 2:h * 2 + 2, :, g * 512:(g + 1) * 512].rearrange("c p t -> p c t"), k[:], r=[k], w=[("kT", g, h)])
                    def ktail(k=k, ke=ke, h=h, g=g):
                        for tt in range(4):
                            pt = pst[tt % 2]
                            for c in range(2):
                                P.I("pe", lambda e, pt=pt, k=k, c=c, tt=tt: e.transpose(out=pt[:, c * 128:(c + 1) * 128], in_=k[:, c, tt * 128:(tt + 1) * 128],
                                                                                     identity=idb[:]), w=[pt], r=[k, idb])
                            P.I("act", lambda e, pt=pt, ke=ke, tt=tt, h=h: e.activation(out=ke[:, 0, tt, :], in_=pt[:], func=AF.Copy, scale=colE[:, h:h + 1]),
                                w=[ke], r=[pt, colE])
                            P.I("act", lambda e, pt=pt, ke=ke, tt=tt, h=h: e.activation(out=ke[:, 1, tt, :], in_=pt[:], func=AF.Copy, scale=colE[:, 4 + h:5 + h]),
                                w=[ke], r=[pt, colE])
                        rows = slice(g * 512, (g + 1) * 512)
                        P.D("sp", C.kEF[rows, h * 256:(h + 1) * 256].rearrange("(tt p) n -> p tt n", p=128), ke[:, 0, :, :], r=[ke], w=[("kEF", g, h)])
                        P.D("sp", C.kEB[rows, h * 256:(h + 1) * 256].rearrange("(tt p) n -> p tt n", p=128), ke[:, 1, :, :], r=[ke], w=[("kEB", g, h)])
                    qk_pending.append(ktail)
    for t_ in qk_pending:
        t_()
    P.end_phase()


def phase_vg(P, C, li, h_src, w_src, ncols_v, v_dst, g_dst, G=None):
    P.begin_phase()
    nc_ = 2 * ncols_v
    W = P.sb("W", [128, 8, nc_], BF16)
    load_w(P, W, w_src, 8, nc_)
    gcol = load_gcol(P, C, li, 0)
    nrm = Norm(P)
    hT = _alt(P, "hT", [128, 8, 512], F32)
    uT = _alt(P, "uT", [128, 8, 512], BF16)
    vo = _alt(P, "vo", [128, nc_], BF16)
    psv = _palt(P, "psv", [128, 512], F32, 4)
    it = 0
    def prep_l(g):
        P.D("sp", hT[g % 2][:], hview(h_src, g), w=[hT[g % 2]], r=[("h", g)])

    def prep_a(g):
        nrm.stats_a(hT[g % 2])

    def prep_b(g):
        nrm.pre(hT[g % 2], gcol, uT[g % 2], skip_a=True)

    prep_l(0)
    prep_a(0)
    prep_b(0)
    for g in range(C.NG):
        u = uT[g % 2]
        for tt in range(4):
            if tt == 0 and g + 1 < C.NG:
                prep_l(g + 1)
            if tt == 2 and g + 1 < C.NG:
                prep_a(g + 1)
            if tt == 3 and g + 1 < C.NG:
                prep_b(g + 1)
            o = vo[tt % 2]
            for cb in range(nc_ // 512):
                ps = psv[it % 4]
                it += 1
                for kc in range(8):
                    P.I("pe", lambda e, ps=ps, kc=kc, tt=tt, cb=cb, u=u: e.matmul(ps[:], u[:, kc, tt * 128:(tt + 1) * 128], W[:, kc, cb * 512:(cb + 1) * 512],
                                                                               start=(kc == 0), stop=(kc == 7)), w=[ps], r=[u, ("W", cb)])
                fn = AF.Copy if cb * 512 < ncols_v else AF.Silu
                P.I("act", lambda e, ps=ps, o=o, cb=cb, fn=fn: e.activation(out=o[:, cb * 512:(cb + 1) * 512], in_=ps[:], func=fn), w=[o], r=[ps])
            rows = slice(g * 512 + tt * 128, g * 512 + (tt + 1) * 128)
            P.D("sp", v_dst[rows, :], o[:, 0:ncols_v], r=[o], w=[("v", g, tt)])
            P.D("sp", g_dst[rows, :], o[:, ncols_v:nc_], r=[o], w=[("sg", g, tt)])
    P.end_phase()


def exchange_states(P, C, S, nhc, DV, xin_t=None, xout_t=None):
    rows = nhc * 128
    xin_t = C.xin if xin_t is None else xin_t
    xout_t = C.xout if xout_t is None else xout_t
    P.D("sp", xin_t.rearrange("(k p) n -> p k n", p=128), S[:, 0:nhc, 0:DV],
        r=[("S", i) for i in range(nhc)], w=[("xin",)])
    xin = xin_t
    xout = xout_t
    P.Dfn('pool', lambda e: e.collective_compute("AllGather", ALU.bypass,
                                          replica_groups=[[0, 1], [2, 3], [4, 5], [6, 7]],
                                          ins=[xin], outs=[xout]), r=[("xin",)], w=[("xout",)])


def sweep(P, C, cfg, second):
    H, NC, DV = cfg["H"], cfg["NC"], cfg["DV"]
    HC = H * NC
    NG = C.NG
    pairs = cfg["pairs"] if not second else []
    npair = len(pairs)
    S = P.sb("S", [128, HC, DV], F32)
    Sb = P.sb("Sb", [128, HC, DV], BF16)
    idb = cfg["idb"]
    P.I("dve", lambda e: e.memset(S[:], 0.0), w=[("S", i) for i in range(HC)])
    P.I("pool", lambda e: e.memset(Sb[:], 0.0), w=[("Sb", i) for i in range(HC)])
    Qg = _alt(P, "Qg", [128, HC, 512], BF16)
    Kg = [_alt(P, "Kg%d" % i, [128, HC, 512], BF16) for i in range(npair)]
    Q2g = [_alt(P, "Q2g%d" % i, [128, HC, 512], BF16) for i in range(1, npair)]
    KEg = _alt(P, "KEg", [128, 4, HC * 128], BF16)
    Vg = _alt(P, "Vg", [128, 4, H * DV], BF16)
    if second:
        P1g = _alt(P, "P1g", [128, 4, H * DV], BF16)
    else:
        P1t = _alt(P, "P1t", [128, H * DV], BF16)
    psS = _palt(P, "psS", [128, 128], F32, 2) if not second else None
    psO = _palt(P, "psO", [128, DV], F32, 2)
    psU = _palt(P, "psU", [128, DV], F32, 4)
    scm = _alt(P, "scm", [128, 128], BF16)
    sct = _alt(P, "sct", [128, 128], F32)
    sct2 = _alt(P, "sct2", [128, 128], F32)
    order = list(range(NG)) if not second else list(reversed(range(NG)))

    def loads(gi):
        g = order[gi]
        b = gi % 2
        tsl = slice(g * 512, (g + 1) * 512)
        P.D("sp", Qg[b][:], cfg["Q"][:, :, tsl].rearrange("k p t -> p k t"), w=[Qg[b]], r=[("q",)])
        for i, (K_ap, Q_ap, mk) in enumerate(pairs):
            P.D("sp", Kg[i][b][:], K_ap[:, :, tsl].rearrange("k p t -> p k t"), w=[Kg[i][b]])
            if i > 0:
                P.D("sp", Q2g[i - 1][b][:], Q_ap[:, :, tsl].rearrange("k p t -> p k t"), w=[Q2g[i - 1][b]])
        P.D("sp", KEg[b][:], cfg["KE"][tsl, :].rearrange("(tt p) n -> p tt n", p=128), w=[KEg[b]])
        P.D("sp", Vg[b][:], cfg["V"][tsl, :].rearrange("(tt p) n -> p tt n", p=128), w=[Vg[b]])
        if second:
            P.D("sp", P1g[b][:], cfg["P1"][tsl, :].rearrange("(tt p) n -> p tt n", p=128), w=[P1g[b]])

    steps = []
    for gi, g in enumerate(order):
        tts = list(range(4)) if not second else [3, 2, 1, 0]
        for tt in tts:
            for h in range(H):
                steps.append((gi, g, tt, h))

    def stageA(i):
        gi, g, tt, h = steps[i]
        b = gi % 2
        csl = slice(tt * 128, (tt + 1) * 128)
        sm = scm[i % 2]
        for pi_, (K_ap, Q_ap, mk) in enumerate(pairs):
            pS = psS[pi_ % 2] if npair > 1 else psS[i % 2]
            qq = Qg[b] if pi_ == 0 else Q2g[pi_ - 1][b]
            for c in range(NC):
                P.I("pe", lambda e, pS=pS, kk=Kg[pi_][b], qq=qq, hc=h * NC + c, c=c, csl=csl:
                    e.matmul(pS[:], kk[:, hc, csl], qq[:, hc, csl], start=(c == 0), stop=(c == NC - 1)), w=[pS], r=[Kg[pi_][b], qq])
            mt = cfg["M"][h] if mk is None else cfg["masks"][mk]
            if pi_ == 0 and npair == 1:
                P.I("dve", lambda e, pS=pS, sm=sm, mt=mt: e.tensor_tensor(out=sm[:], in0=pS[:], in1=mt[:], op=ALU.mult), w=[sm], r=[pS, mt])
            elif pi_ == 0:
                st_ = sct[i % 2]
                P.I("dve", lambda e, pS=pS, st_=st_, mt=mt: e.tensor_tensor(out=st_[:], in0=pS[:], in1=mt[:], op=ALU.mult), w=[st_], r=[pS, mt])
            else:
                st_ = sct[i % 2]
                st2 = sct2[i % 2]
                P.I("dve", lambda e, pS=pS, st2=st2, mt=mt: e.tensor_tensor(out=st2[:], in0=pS[:], in1=mt[:], op=ALU.mult), w=[st2], r=[pS, mt])
                P.I("pool", lambda e, st_=st_, st2=st2, sm=sm: e.tensor_tensor(out=sm[:], in0=st_[:], in1=st2[:], op=ALU.add), w=[sm], r=[st_, st2])

    pending = []
    loads(0)
    if second:
        cfg["epi_load"](order[0], 0)
    if not second:
        stageA(0)
    for i, (gi, g, tt, h) in enumerate(steps):
        b = gi % 2
        n = g * 4 + tt
        csl = slice(tt * 128, (tt + 1) * 128)
        first_of_group = (i % (4 * H) == 0)
        last_of_group = (i % (4 * H) == 4 * H - 1)
        if first_of_group:
            if gi + 1 < NG:
                loads(gi + 1)
                if second:
                    cfg["epi_load"](order[gi + 1], (gi + 1) % 2)
        if not second and i + 1 < len(steps):
            stageA(i + 1)
        po = psO[i % 2]
        vsl = Vg[b][:, tt, h * DV:(h + 1) * DV]
        if not second:
            sm = scm[i % 2]
            P.I("pe", lambda e, po=po, sm=sm, vsl=vsl: e.matmul(po[:], sm[:], vsl, start=True, stop=False), w=[po], r=[sm, Vg[b]])
        else:
            P.I("pe", lambda e, po=po, b=b, tt=tt, h=h: e.matmul(po[:], idb[:], P1g[b][:, tt, h * DV:(h + 1) * DV], start=True, stop=False),
                w=[po], r=[idb, P1g[b]])
        for c in range(NC):
            hc = h * NC + c
            P.I("pe", lambda e, po=po, b=b, hc=hc, c=c, csl=csl: e.matmul(po[:], Qg[b][:, hc, csl], Sb[:, hc, :], start=False, stop=(c == NC - 1)),
                w=[po], r=[Qg[b], ("Sb", hc)])
        if not second:
            pt = P1t[n % 2]
            P.I("act", lambda e, po=po, pt=pt, h=h: e.activation(out=pt[:, h * DV:(h + 1) * DV], in_=po[:], func=AF.Copy), w=[pt], r=[po])
        else:
            tail = cfg["epi"](g, b, tt, h, po)
            for t_ in pending:
                t_()
            pending = [tail]
        for c in range(NC):
            hc = h * NC + c
            pu = psU[(i * NC + c) % 4]
            P.I("pe", lambda e, pu=pu, b=b, tt=tt, hc=hc, vsl=vsl: e.matmul(pu[:], KEg[b][:, tt, hc * 128:(hc + 1) * 128], vsl, start=True, stop=True),
                w=[pu], r=[KEg[b], Vg[b]])
            dcol = cfg["dec"](h, c, n)
            P.I("dve", lambda e, pu=pu, hc=hc, dcol=dcol: e.scalar_tensor_tensor(out=S[:, hc, :], in0=S[:, hc, :], scalar=dcol, in1=pu[:],
                                                                             op0=ALU.mult, op1=ALU.add),
                w=[("S", hc)], r=[("S", hc), pu, cfg["dec_res"]])
            P.I("pool", lambda e, hc=hc: e.tensor_copy(out=Sb[:, hc, :], in_=S[:, hc, :]), w=[("Sb", hc)], r=[("S", hc)])
        if not second and h == H - 1:
            P.D("sp", cfg["P1"][n * 128:(n + 1) * 128, :], P1t[n % 2][:], r=[P1t[n % 2]], w=[("P1", n)])
        if second and last_of_group:
            for t_ in pending:
                t_()
            pending = []
            cfg["epi_store"](g, b)
    return S


def phase_ret_sweep1(P, C, j):
    P.begin_phase()
    cs = load_consts(P, C, ["maskLF", "maskLB", "A1", "A2", "A3", "c128"])
    lg = ret_lg(P, C, j)
    M = {}
    for h in range(4):
        m = P.sb("M%d" % h, [128, 128], F32)
        t2 = P.sb("Mt%d" % h, [128, 128], F32)
        lf = lg[:, h:h + 1]
        lb = lg[:, 4 + h:5 + h]
        P.I("act", lambda e, m=m, lf=lf: e.activation(out=m[:], in_=cs["A1"][:], func=AF.Exp, scale=lf), w=[m], r=[cs["A1"], lg])
        P.I("dve", lambda e, m=m: e.tensor_tensor(out=m[:], in0=m[:], in1=cs["maskLF"][:], op=ALU.mult), w=[m], r=[m, cs["maskLF"]])
        P.I("dve", lambda e, t2=t2, lb=lb: e.tensor_scalar(out=t2[:], in0=cs["A2"][:], scalar1=lb, scalar2=None, op0=ALU.mult), w=[t2], r=[cs["A2"], lg])
        P.I("dve", lambda e, t2=t2, lf=lf: e.scalar_tensor_tensor(out=t2[:], in0=cs["A3"][:], scalar=lf, in1=t2[:], op0=ALU.mult, op1=ALU.add),
            w=[t2], r=[t2, cs["A3"], lg])
        P.I("act", lambda e, t2=t2: e.activation(out=t2[:], in_=t2[:], func=AF.Exp), w=[t2], r=[t2])
        P.I("dve", lambda e, t2=t2: e.tensor_tensor(out=t2[:], in0=t2[:], in1=cs["maskLB"][:], op=ALU.mult), w=[t2], r=[t2, cs["maskLB"]])
        P.I("dve", lambda e, m=m, t2=t2: e.tensor_tensor(out=m[:], in0=m[:], in1=t2[:], op=ALU.add), w=[m], r=[m, t2])
        M[h] = m
    decc = P.sb("decc", [128, 4], F32)
    for h in range(4):
        P.I("act", lambda e, h=h: e.activation(out=decc[:, h:h + 1], in_=cs["c128"][:], func=AF.Exp, scale=lg[:, h:h + 1]), w=[decc], r=[cs["c128"], lg])
    cfg = dict(H=4, NC=2, DV=512, pairs=[(C.kT, C.qF, None)], Q=C.qF, KE=C.kEF, V=C.v, P1=C.P1,
               dec=lambda h, c, n: decc[:, h:h + 1], dec_res=decc, M=M, idb=None)
    S = sweep(P, C, cfg, False)
    P.end_phase()


def make_epilogue(P, C, H, DV, sg_src, y_dst, gain_bc=None):
    idb = make_ident(P)
    NF = DV // 128
    SGgs = _alt(P, "SGg", [128, 4, H * DV], BF16)
    yTg = P.sb("yTg", [128, H * NF, 512], BF16)
    yt = _alt(P, "yt", [128, DV], BF16)
    yf = _alt(P, "yf", [128, DV], F32)
    ssc = P.sb("ssc", [128, 8], F32)
    junk = P.sb("junk", [128, DV], F32)
    psT = _palt(P, "psT", [128, DV], BF16, 2)
    st = {"k": 0}

    def epi_load(g, b):
        P.D("sp", SGgs[b][:], sg_src[g * 512:(g + 1) * 512, :].rearrange("(tt p) n -> p tt n", p=128), w=[SGgs[b]])

    def epi(g, b, tt, h, po):
        SGg = SGgs[b]
        k = st["k"]
        st["k"] += 1
        s = ssc[:, k % 8:k % 8 + 1]
        key = ("ssc", k % 8)
        y = yt[k % 2]
        pT = psT[k % 2]
        P.I("dve", lambda e: e.memset(s, 0.0), w=[key])
        P.I("act", lambda e: e.activation(out=junk[:], in_=po[:], func=AF.Square, accum_out=s), w=[junk, key], r=[po])
        P.I("act", lambda e: e.activation(out=s, in_=s, func=AF.Sqrt, scale=1.0 / DV, bias=EPS), w=[key], r=[key])
        P.I("dve", lambda e: e.reciprocal(out=s, in_=s), w=[key], r=[key])
        if gain_bc is None:
            P.I("dve", lambda e: e.scalar_tensor_tensor(out=y[:], in0=po[:], scalar=s, in1=SGg[:, tt, h * DV:(h + 1) * DV], op0=ALU.mult, op1=ALU.mult),
                w=[y], r=[po, key, SGg])
        else:
            f = yf[k % 2]
            P.I("dve", lambda e: e.scalar_tensor_tensor(out=f[:], in0=po[:], scalar=s, in1=gain_bc[:], op0=ALU.mult, op1=ALU.mult),
                w=[f], r=[po, key, gain_bc])
            P.I("pool", lambda e: e.tensor_tensor(out=y[:], in0=f[:], in1=SGg[:, tt, h * DV:(h + 1) * DV], op=ALU.mult), w=[y], r=[f, SGg])
        def tail():
            for q4 in range(NF):
                P.I("pe", lambda e, q4=q4: e.transpose(out=pT[:, q4 * 128:(q4 + 1) * 128], in_=y[:, q4 * 128:(q4 + 1) * 128], identity=idb[:]),
                    w=[pT], r=[y, idb])
            P.I("act", lambda e: e.activation(out=yTg[:, h * NF:(h + 1) * NF, tt * 128:(tt + 1) * 128],
                                              in_=pT[:].rearrange("p (q t) -> p q t", q=NF), func=AF.Copy), w=[yTg], r=[pT])
        return tail

    def epi_store(g, b):
        P.D("sp", y_dst[:, :, g * 512:(g + 1) * 512].rearrange("k p t -> p k t"), yTg[:], r=[yTg], w=[("ysrc", g)])

    return idb, epi_load, epi, epi_store


def phase_ret_sweep2(P, C, j):
    P.begin_phase()
    cs = load_consts(P, C, ["c128", "sel"])
    lg = ret_lg(P, C, j)
    decc = P.sb("decc", [128, 4], F32)
    for h in range(4):
        P.I("act", lambda e, h=h: e.activation(out=decc[:, h:h + 1], in_=cs["c128"][:], func=AF.Exp, scale=lg[:, 4 + h:5 + h]), w=[decc], r=[cs["c128"], lg])
    idb, epi_load, epi, epi_store = make_epilogue(P, C, 4, 512, C.sg, C.yT)
    cfg = dict(H=4, NC=2, DV=512, pairs=[], Q=C.qB, KE=C.kEB, V=C.v, P1=C.P1,
               dec=lambda h, c, n: decc[:, h:h + 1], dec_res=decc, idb=idb, sel=cs["sel"],
               epi_load=epi_load, epi=epi, epi_store=epi_store)
    sweep(P, C, cfg, True)
    P.end_phase()


def layer_ret(P, C, li, j, h_src, h_dst):
    phase_ret_qk(P, C, li, j, h_src)
    phase_vg(P, C, li, h_src, C.ret_w_in[j][:, 2048:6144], 2048, C.v, C.sg)
    phase_ret_sweep1(P, C, j)
    phase_ret_sweep2(P, C, j)
    phase_outproj(P, C, li, 1, C.ret_w_out[j], 16, C.yT, h_src, h_dst)


def layer_mlp(P, C, li, h_src, h_dst):
    phase_mlp_up(P, C, li, h_src)
    phase_mlp_down(P, C, li, h_src, h_dst)


def phase_gla_qk(P, C, li, h_src):
    P.begin_phase()
    NCH = C.NCH
    W = P.sb("W", [128, 8, 1024], BF16)
    load_w(P, W, C.gla_w_in[0][:, 0:1024], 8, 1024)
    w1 = P.sb("w1", [128, 2, 8, 16], BF16)
    for d in range(2):
        P.D("pool", w1[:, d, :, :], C.gla_w1[d].rearrange("(kc p) r -> p kc r", p=128), w=[w1])
    w2a = P.sb("w2a", [17, 2, 512], F32)
    for d in range(2):
        P.D("sp", w2a[0:16, d, :], C.gla_w2[d], w=[w2a])
        P.D("sp", w2a[16:17, d, :], C.gla_b[d:d + 1, :], w=[w2a])
    gcol = load_gcol(P, C, li, 0)
    nrm = Norm(P)
    cs = load_consts(P, C, ["triU", "triL", "sL", "sU"])
    tri = [cs["triU"], cs["triL"]]
    sX = [cs["sL"], cs["sU"]]
    DEC = [P.sb("DEC%d" % d, [128, 4, NCH], F32) for d in range(2)]
    hT = _alt(P, "hT", [128, 8, 512], F32)
    uT = _alt(P, "uT", [128, 8, 512], BF16)
    qs = P.sb("qs", [128, 4, 512], F32)
    ks = P.sb("ks", [128, 4, 512], F32)
    zTa = [P.sb("zTa%d" % d, [17, 512], F32) for d in range(2)]
    for d in range(2):
        P.I("dve", lambda e, d=d: e.memset(zTa[d][:], 1.0), w=[zTa[d]])
    lgt = [P.sb("lgt%d" % d, [128, 4, 512], F32) for d in range(2)]
    QX = [P.sb("QX%d" % d, [128, 4, 512], BF16) for d in range(2)]
    KX = [P.sb("KX%d" % d, [128, 4, 512], BF16) for d in range(2)]
    kE = [P.sb("kE%d" % d, [128, 4, 512], BF16) for d in range(2)]
    EQ = _alt(P, "EQ", [128, 4, 128], F32)
    EK = _alt(P, "EK", [128, 4, 128], F32)
    EE = _alt(P, "EE", [128, 512], F32)
    psq = _palt(P, "psq", [128, 512], F32, 2)
    psz = P.ps("psz", [16, 512], F32)
    psl = P.ps("psl", [128, 512], F32)
    psc = _palt(P, "psc", [128, 4, 128], F32, 2)
    pse = P.ps("pse", [128, 512], F32)
    QXd = [C.gQF, C.gQB]
    KXd = [C.gKF, C.gKB]
    kEd = [C.gkEF, C.gkEB]
    it = 0
    def prep_l(g):
        P.D("sp", hT[g % 2][:], hview(h_src, g), w=[hT[g % 2]], r=[("h", g)])

    def prep_a(g):
        nrm.stats_a(hT[g % 2])

    def prep_b(g):
        nrm.pre(hT[g % 2], gcol, uT[g % 2], skip_a=True)

    prep_l(0)
    prep_a(0)
    prep_b(0)
    for g in range(C.NG):
        u = uT[g % 2]
        tsl = slice(g * 512, (g + 1) * 512)
        for qk in range(2):
            dst = qs if qk == 0 else ks
            for h in range(4):
                ps = psq[h % 2]
                col = qk * 512 + h * 128
                for kc in range(8):
                    P.I("pe", lambda e, ps=ps, kc=kc, col=col, u=u: e.matmul(ps[:], W[:, kc, col:col + 128], u[:, kc, :], start=(kc == 0), stop=(kc == 7)),
                        w=[ps], r=[u, ("W", col // 512)])
                sc = float(128 ** -0.5) if qk == 0 else 1.0
                P.I("act", lambda e, ps=ps, dst=dst, h=h, sc=sc: e.activation(out=dst[:, h, :], in_=ps[:], func=AF.Copy, scale=sc), w=[dst], r=[ps])
        for d in range(2):
            for kc in range(8):
                P.I("pe", lambda e, d=d, kc=kc, u=u: e.matmul(psz[:], w1[:, d, kc, :], u[:, kc, :], start=(kc == 0), stop=(kc == 7)), w=[psz], r=[u, w1])
            P.I("act", lambda e, d=d: e.activation(out=zTa[d][0:16, :], in_=psz[:], func=AF.Copy), w=[zTa[d]], r=[psz])
            for tt in range(4):
                P.I("pe", lambda e, d=d, tt=tt: e.matmul(psl[:], zTa[d][:, tt * 128:(tt + 1) * 128], w2a[:, d, :], start=True, stop=True), w=[psl], r=[zTa[d], w2a])
                P.I("act", lambda e, d=d, tt=tt: e.activation(out=lgt[d][:, tt, :], in_=psl[:], func=AF.Exp, scale=-1.0), w=[lgt[d]], r=[psl])
            P.I("act", lambda e, d=d: e.activation(out=lgt[d][:], in_=lgt[d][:], func=AF.Ln, bias=1.0), w=[lgt[d]], r=[lgt[d]])
            P.I("dve", lambda e, d=d: e.tensor_scalar(out=lgt[d][:], in0=lgt[d][:], scalar1=-1.0 / 16.0, scalar2=None, op0=ALU.mult), w=[lgt[d]], r=[lgt[d]])
        for tt in range(4):
            if tt == 0 and g + 1 < C.NG:
                prep_l(g + 1)
            if tt == 2 and g + 1 < C.NG:
                prep_a(g + 1)
            if tt == 3 and g + 1 < C.NG:
                prep_b(g + 1)
            n = g * 4 + tt
            csl = slice(tt * 128, (tt + 1) * 128)
            for kc in range(8):
                P.I("pe", lambda e, kc=kc, csl=csl, u=u: e.matmul(pse[:], u[:, kc, csl], W[:, kc, 512:1024], start=(kc == 0), stop=(kc == 7)), w=[pse], r=[u, ("W", 1)])
            ktm = EE[0]
            P.I("act", lambda e, ktm=ktm: e.activation(out=ktm[:], in_=pse[:], func=AF.Copy), w=[ktm], r=[pse])
            for d in range(2):
                pc = psc[it % 2]
                eq = EQ[it % 2]
                ek = EK[it % 2]
                it += 1
                for h in range(4):
                    P.I("pe", lambda e, pc=pc, d=d, h=h, tt=tt: e.matmul(pc[:, h, :], lgt[d][:, tt, h * 128:(h + 1) * 128], tri[d][:, 0:128], start=True, stop=True),
                        w=[pc], r=[lgt[d], tri[d]])
                P.I("act", lambda e, pc=pc, eq=eq: e.activation(out=eq[:], in_=pc[:], func=AF.Exp), w=[eq], r=[pc])
                P.I("act", lambda e, pc=pc, ek=ek: e.activation(out=ek[:], in_=pc[:], func=AF.Exp, scale=-1.0), w=[ek], r=[pc])
                tcol = 127 if d == 0 else 0
                P.I("pool", lambda e, eq=eq, d=d, n=n, tcol=tcol: e.tensor_copy(out=DEC[d][:, :, n:n + 1], in_=eq[:, :, tcol:tcol + 1]), w=[DEC[d]], r=[eq])
                P.I("dve", lambda e, eq=eq, d=d, csl=csl: e.tensor_tensor(out=QX[d][:, :, csl], in0=qs[:, :, csl], in1=eq[:], op=ALU.mult), w=[QX[d]], r=[qs, eq])
                P.I("pool", lambda e, ek=ek, d=d, csl=csl: e.tensor_tensor(out=KX[d][:, :, csl], in0=ks[:, :, csl], in1=ek[:], op=ALU.mult), w=[KX[d]], r=[ks, ek])
                P.I("pe", lambda e, d=d, tt=tt: e.matmul(psl[:], sX[d][:], lgt[d][:, tt, :], start=True, stop=True), w=[psl], r=[sX[d], lgt[d]])
                ee = EE[1]
                P.I("act", lambda e, ee=ee: e.activation(out=ee[:], in_=psl[:], func=AF.Exp), w=[ee], r=[psl])
                P.I("dve", lambda e, ee=ee, d=d, tt=tt, ktm=ktm: e.tensor_tensor(out=kE[d][:, tt, :], in0=ktm[:], in1=ee[:], op=ALU.mult), w=[kE[d]], r=[ktm, ee])
        for d in range(2):
            P.D("sp", QXd[d][:, :, tsl].rearrange("k p t -> p k t"), QX[d][:], r=[QX[d]], w=[("gq", d, g)])
            P.D("sp", KXd[d][:, :, tsl].rearrange("k p t -> p k t"), KX[d][:], r=[KX[d]], w=[("gk", d, g)])
            P.D("sp", kEd[d][tsl, :].rearrange("(tt p) n -> p tt n", p=128), kE[d][:], r=[kE[d]], w=[("gke", d, g)])
    for d in range(2):
        P.D("sp", C.gDEC[d], DEC[d][:].rearrange("p h n -> p (h n)"), r=[DEC[d]], w=[("gdec", d)])
    P.end_phase()


def phase_gla_sweep1(P, C):
    P.begin_phase()
    cs = load_consts(P, C, ["maskLF", "maskLB"])
    DECt = P.sb("DECt", [128, 4 * C.NCH], F32)
    P.D("sp", DECt[:], C.gDEC[0], w=[DECt])
    NCH = C.NCH
    cfg = dict(H=4, NC=1, DV=256, pairs=[(C.gKF, C.gQF, "maskLF"), (C.gKB, C.gQB, "maskLB")], Q=C.gQF, KE=C.gkEF, V=C.gv, P1=C.gP1,
               dec=lambda h, c, n: DECt[:, h * NCH + n:h * NCH + n + 1], dec_res=DECt, masks=cs, idb=None, xin=C.xin2, xout=C.xout2)
    S = sweep(P, C, cfg, False)
    P.end_phase()


def phase_gla_sweep2(P, C):
    P.begin_phase()
    cs = load_consts(P, C, ["sel"])
    NCH = C.NCH
    DECt = P.sb("DECt", [128, 4 * NCH], F32)
    P.D("sp", DECt[:], C.gDEC[1], w=[DECt])
    gbc = P.sb("gbc", [128, 256], F32)
    P.D("sp", gbc[:], C.gla_ng, w=[gbc])
    idb, epi_load, epi, epi_store = make_epilogue(P, C, 4, 256, C.gsg, C.gyT, gain_bc=gbc)
    cfg = dict(H=4, NC=1, DV=256, pairs=[], Q=C.gQB, KE=C.gkEB, V=C.gv, P1=C.gP1,
               dec=lambda h, c, n: DECt[:, h * NCH + n:h * NCH + n + 1], dec_res=DECt, idb=idb, sel=cs["sel"],
               epi_load=epi_load, epi=epi, epi_store=epi_store, xin=C.xin2, xout=C.xout2)
    sweep(P, C, cfg, True)
    P.end_phase()


def layer_gla(P, C, li, h_src, h_dst):
    phase_gla_qk(P, C, li, h_src)
    phase_vg(P, C, li, h_src, C.gla_w_in[0][:, 1024:3072], 1024, C.gv, C.gsg)
    phase_gla_sweep1(P, C)
    phase_gla_sweep2(P, C)
    phase_outproj(P, C, li, 1, C.gla_w_out[0], 8, C.gyT, h_src, h_dst)


def phase_conv_glu(P, C, li, h_src):
    P.begin_phase()
    T = C.T
    W = P.sb("W", [128, 8, 2048], BF16)
    load_w(P, W, C.conv_w_in[0], 8, 2048)
    gcol = load_gcol(P, C, li, 0)
    nrm = Norm(P)
    bin_ = P.sb("bin", [128, 16], F32)
    P.D("sp", bin_[:], C.conv_bin, w=[bin_])
    hT = _alt(P, "hT", [128, 8, 512], F32)
    uT = _alt(P, "uT", [128, 8, 512], BF16)
    hg = _alt(P, "hg", [128, 8, 512], BF16)
    sig = _alt(P, "sig", [128, 512], F32)
    psa = _palt(P, "psa", [128, 512], F32, 2)
    psg = _palt(P, "psg", [128, 512], F32, 2)
    zt = P.sb("zt", [128, 8, 16], BF16)
    P.I("dve", lambda e: e.memset(zt[:], 0.0), w=[zt])
    P.D("sp", C.cHG[:, :, 0:16].rearrange("k p t -> p k t"), zt[:], r=[zt], w=[("hgpad",)])
    def prep_l(g):
        P.D("sp", hT[g % 2][:], hview(h_src, g), w=[hT[g % 2]], r=[("h", g)])

    def prep_a(g):
        nrm.stats_a(hT[g % 2])

    def prep_b(g):
        nrm.pre(hT[g % 2], gcol, uT[g % 2], skip_a=True)

    prep_l(0)
    prep_a(0)
    prep_b(0)
    for g in range(C.NG):
        u = uT[g % 2]
        o = hg[g % 2]
        for fc in range(8):
            if fc == 0 and g + 1 < C.NG:
                prep_l(g + 1)
            if fc == 4 and g + 1 < C.NG:
                prep_a(g + 1)
            if fc == 6 and g + 1 < C.NG:
                prep_b(g + 1)
            pa = psa[fc % 2]
            pg = psg[fc % 2]
            sg_ = sig[fc % 2]
            for kc in range(8):
                P.I("pe", lambda e, pa=pa, kc=kc, fc=fc, u=u: e.matmul(pa[:], W[:, kc, fc * 128:(fc + 1) * 128], u[:, kc, :], start=(kc == 0), stop=(kc == 7)),
                    w=[pa], r=[u, ("W", fc // 4)])
            for kc in range(8):
                P.I("pe", lambda e, pg=pg, kc=kc, fc=fc, u=u: e.matmul(pg[:], W[:, kc, 1024 + fc * 128:1024 + (fc + 1) * 128], u[:, kc, :], start=(kc == 0), stop=(kc == 7)),
                    w=[pg], r=[u, ("W", 2 + fc // 4)])
            P.I("act", lambda e, pg=pg, sg_=sg_, fc=fc: e.activation(out=sg_[:], in_=pg[:], func=AF.Sigmoid, bias=bin_[:, 8 + fc:9 + fc]), w=[sg_], r=[pg, bin_])
            P.I("dve", lambda e, pa=pa, sg_=sg_, fc=fc, o=o: e.scalar_tensor_tensor(out=o[:, fc, :], in0=pa[:], scalar=bin_[:, fc:fc + 1], in1=sg_[:],
                                                                                 op0=ALU.add, op1=ALU.mult), w=[o], r=[pa, sg_, bin_])
        P.D("sp", C.cHG[:, :, 16 + g * 512:16 + (g + 1) * 512].rearrange("k p t -> p k t"), o[:], r=[o], w=[("hg", g)])
    P.D("sp", C.cHG[:, :, 16 + T:32 + T].rearrange("k p t -> p k t"), zt[:], r=[zt], w=[("hghalo",)])
    P.end_phase()


def phase_conv_dw(P, C):
    P.begin_phase()
    T = C.T
    idb = make_ident(P)
    ones = make_ones(P)
    wT = P.sb("wT", [128, 8, 31], F32)
    P.D("sp", wT[:], C.conv_wdwT, w=[wT])
    cols = P.sb("cols", [128, 24], F32)
    P.D("sp", cols[:], C.conv_cols, w=[cols])
    D = P.sb("D", [128, 248, 128], BF16)
    for j in range(31):
        for fc in range(8):
            ve = "dve" if (j + fc) % 2 == 0 else "pool"
            P.I(ve, lambda e, j=j, fc=fc: e.tensor_scalar(out=D[:, j * 8 + fc, :], in0=idb[:], scalar1=wT[:, fc, j:j + 1], scalar2=None, op0=ALU.mult),
                w=[("D", j * 8 + fc)], r=[idb, wT])
    hw = _alt(P, "hw", [128, 8, 542], BF16)
    psc = _palt(P, "psc", [128, 512], F32, 3)
    psm = P.ps("psm", [128, 512], F32)
    pss = P.ps("pss", [128, 512], F32)
    xss = _alt(P, "xs", [128, 8, 512], F32)
    xb = P.sb("xb", [128, 8, 512], BF16)
    sq = P.sb("sq", [128, 8, 512], BF16)
    mt = P.sb("mt", [128, 512], F32)
    m2 = P.sb("m2", [128, 512], F32)
    rs = P.sb("rs", [128, 512], F32)
    tt_ = _alt(P, "tt", [128, 512], F32)
    yTg = _alt(P, "yTg", [128, 8, 512], BF16)
    for g in range(C.NG):
        w_ = hw[g % 2]
        yg = yTg[g % 2]
        xs = xss[g % 2]
        P.D("sp", w_[:], C.cHG[:, :, 1 + g * 512:1 + g * 512 + 542].rearrange("k p t -> p k t"), w=[w_])
        for fc in range(8):
            pc = psc[fc % 3]
            for j in range(31):
                P.I("pe", lambda e, pc=pc, fc=fc, j=j, w_=w_: e.matmul(pc[:], D[:, j * 8 + fc, :], w_[:, fc, j:j + 512], start=(j == 0), stop=(j == 30)),
                    w=[pc], r=[w_] + ([("D", j * 8 + fc)] if g == 0 else []))
            P.I("act", lambda e, pc=pc, fc=fc: e.activation(out=xs[:, fc, :], in_=pc[:], func=AF.Identity, bias=cols[:, fc:fc + 1]),
                w=[(xs.name, fc)], r=[pc, cols])
        xkeys = [(xs.name, fc) for fc in range(8)]
        P.I("act", lambda e: e.activation(out=sq[:], in_=xs[:], func=AF.Square), w=[sq], r=xkeys)
        P.I("dve", lambda e: e.tensor_copy(out=xb[:], in_=xs[:]), w=[xb], r=xkeys)
        for kc in range(8):
            P.I("pe", lambda e, kc=kc: e.matmul(psm[:], ones[:], xb[:, kc, :], start=(kc == 0), stop=(kc == 7)), w=[psm], r=[xb, ones])
        for kc in range(8):
            P.I("pe", lambda e, kc=kc: e.matmul(pss[:], ones[:], sq[:, kc, :], start=(kc == 0), stop=(kc == 7)), w=[pss], r=[sq, ones])
        P.I("act", lambda e: e.activation(out=mt[:], in_=psm[:], func=AF.Copy, scale=1.0 / 1024.0), w=[mt], r=[psm])
        P.I("dve", lambda e: e.tensor_tensor(out=m2[:], in0=mt[:], in1=mt[:], op=ALU.mult), w=[m2], r=[mt])
        P.I("dve", lambda e: e.scalar_tensor_tensor(out=rs[:], in0=pss[:], scalar=1.0 / 1024.0, in1=m2[:], op0=ALU.mult, op1=ALU.subtract),
            w=[rs], r=[pss, m2])
        P.I("act", lambda e: e.activation(out=rs[:], in_=rs[:], func=AF.Sqrt, bias=EPS), w=[rs], r=[rs])
        P.I("dve", lambda e: e.reciprocal(out=rs[:], in_=rs[:]), w=[rs], r=[rs])
        for fc in range(8):
            t = tt_[fc % 2]
            ve = "dve" if fc % 2 == 0 else "pool"
            P.I(ve, lambda e, t=t, fc=fc: e.tensor_tensor(out=t[:], in0=xs[:, fc, :], in1=mt[:], op=ALU.subtract), w=[t], r=[(xs.name, fc), mt])
            P.I(ve, lambda e, t=t: e.tensor_tensor(out=t[:], in0=t[:], in1=rs[:], op=ALU.mult), w=[t], r=[t, rs])
            P.I("act", lambda e, t=t, fc=fc, yg=yg: e.activation(out=yg[:, fc, :], in_=t[:], func=AF.Silu, scale=cols[:, 8 + fc:9 + fc], bias=cols[:, 16 + fc:17 + fc]),
                w=[yg], r=[t, cols])
        P.D("sp", C.cyT[:, :, g * 512:(g + 1) * 512].rearrange("k p t -> p k t"), yg[:], r=[yg], w=[("ysrc", g)])
    P.end_phase()


def layer_conv(P, C, li, h_src, h_dst):
    phase_conv_glu(P, C, li, h_src)
    phase_conv_dw(P, C)
    phase_outproj(P, C, li, 1, C.conv_w_out[0], 8, C.cyT, h_src, h_dst, bias_src=C.conv_bout)


def declare_extra(nc, C, kinds, din, dint):
    T = C.T
    if "gla" in kinds:
        C.gla_w_in = din("gla_w_in", [1, 1024, 3072])
        C.gla_w1 = din("gla_w1", [2, 1024, 16])
        C.gla_w2 = din("gla_w2", [2, 16, 512])
        C.gla_b = din("gla_b", [2, 512])
        C.gla_ng = din("gla_ng", [128, 256])
        C.gla_w_out = din("gla_w_out", [1, 1024, 1024])
        C.gQF = dint("gQF", [4, 128, T]); C.gQB = dint("gQB", [4, 128, T])
        C.gKF = dint("gKF", [4, 128, T]); C.gKB = dint("gKB", [4, 128, T])
        C.gkEF = dint("gkEF", [T, 512]); C.gkEB = dint("gkEB", [T, 512])
        C.gv = dint("gv", [T, 1024]); C.gsg = dint("gsg", [T, 1024]); C.gP1 = dint("gP1", [T, 1024])
        C.gyT = dint("gyT", [8, 128, T])
        C.gDEC = dint("gDEC", [2, 128, 4 * C.NCH], F32)
        C.xin2 = dint("xin2", [512, 256], F32)
        C.xout2 = dint("xout2", [1024, 256], F32)
    if "conv" in kinds:
        C.conv_w_in = din("conv_w_in", [1, 1024, 2048])
        C.conv_bin = din("conv_bin", [128, 16])
        C.conv_wdwT = din("conv_wdwT", [128, 8, 31])
        C.conv_cols = din("conv_cols", [128, 24])
        C.conv_w_out = din("conv_w_out", [1, 1024, 1024])
        C.conv_bout = din("conv_bout", [128, 8])
        C.cHG = dint("cHG", [8, 128, T + 32])
        C.cyT = dint("cyT", [8, 128, T])
        C.xin3 = dint("xin3", [1024, 16])
        C.xout3 = dint("xout3", [2048, 16])


def extra_in_maps(m, inputs, T, b, half, kinds):
    f = lambda k: np.asarray(inputs[k], np.float32)
    if "gla" in kinds:
        m["gla_w_in"] = f("gla_w_in")
        sw = (lambda a: a) if half == 0 else (lambda a: a[::-1])
        m["gla_w1"] = np.ascontiguousarray(sw(f("gla_gate_w1")[0]))
        m["gla_w2"] = np.ascontiguousarray(sw(f("gla_gate_w2")[0]))
        m["gla_b"] = np.ascontiguousarray(sw(f("gla_gate_b")[0]))
        m["gla_ng"] = np.ascontiguousarray(np.broadcast_to(f("gla_norm_gain")[0][None, :], (128, 256)))
        m["gla_w_out"] = f("gla_w_out")
    if "conv" in kinds:
        m["conv_w_in"] = f("conv_w_in")
        m["conv_bin"] = np.ascontiguousarray(f("conv_b_in")[0].reshape(16, 128).T)
        wdw = f("conv_w_dw")[0]
        if half == 1:
            wdw = wdw[::-1]
        m["conv_wdwT"] = np.ascontiguousarray(wdw.reshape(31, 8, 128).transpose(2, 1, 0))
        rows = np.stack([f("conv_b_dw")[0], f("conv_ln_gain")[0], f("conv_ln_bias")[0]], axis=0)
        m["conv_cols"] = np.ascontiguousarray(rows.reshape(3, 8, 128).transpose(2, 0, 1).reshape(128, 24))
        m["conv_w_out"] = f("conv_w_out")
        m["conv_bout"] = np.ascontiguousarray(f("conv_b_out")[0].reshape(8, 128).T)


CST_LAYOUT = [("maskLF", 128), ("maskLB", 128), ("A1", 128), ("A2", 128), ("A3", 128), ("EF", 512), ("EB", 512),
              ("c127", 1), ("cs", 1), ("c128", 1), ("invf", 1), ("sel", 2),
              ("triU", 129), ("triL", 129), ("sL", 128), ("sU", 128)]


def cst_offsets():
    off = {}
    o = 0
    for nm, w in CST_LAYOUT:
        off[nm] = (o, w)
        o += w
    return off, o


def make_cst(half):
    off, n = cst_offsets()
    c = np.zeros((128, n), np.float32)
    s = np.arange(128)[:, None].astype(np.float64)
    t = np.arange(128)[None, :].astype(np.float64)

    def put(nm, a):
        o, w = off[nm]
        c[:, o:o + w] = np.broadcast_to(a, (128, w))
    put("maskLF", (s <= t) if half == 0 else (s < t))
    put("maskLB", (s > t) if half == 0 else (s >= t))
    put("A1", -(s + 1) + 0 * t)
    put("A2", s - t)
    put("A3", -(t + 1) + 0 * s)
    tt = (np.arange(512) % 128)[None, :]
    put("EF", tt + 1.0)
    put("EB", 128.0 - tt)
    put("c127", 127.0 - s)
    put("cs", s)
    put("c128", 128.0)
    put("invf", (10000.0 ** (-(np.arange(128, dtype=np.float32) / np.float32(128)))).astype(np.float32)[:, None])
    put("sel", np.array([[0.0, 1.0]]) if half == 0 else np.array([[1.0, 0.0]]))
    put("triU", np.concatenate([(s <= t), np.ones((128, 1))], axis=1))
    put("triL", np.concatenate([(s >= t), np.ones((128, 1))], axis=1))
    put("sL", (s > t))
    put("sU", (s < t))
    return c


def build(T, plan):
    nc = bass.Bass("TRN2", target_bir_lowering=False)
    C = Ctx()
    C.T, C.NG, C.NCH = T, T // 512, T // 128
    C.cst_off, ncst = cst_offsets()

    def din(name, shape, dt=F32):
        return nc.dram_tensor(name, list(shape), dt, kind="ExternalInput").ap()

    def dint(name, shape, dt=BF16):
        return nc.dram_tensor(name, list(shape), dt).ap()
    C.xT = din("xT", [8, 128, T])
    C.posr = din("posr", [128, T], I32)
    C.gcol = din("gcol", [128, 128])
    C.cst = din("cst", [128, ncst])
    kinds = set(k for k, _, _ in plan)
    C.ret_dl = din("ret_dl", [2, 128, 8])
    if any(k.startswith("p_") for k in kinds):
        kinds = kinds | {"ret"}
    if "ret" in kinds:
        C.ret_w_in = din("ret_w_in", [2, 1024, 6144])
        C.ret_w_out = din("ret_w_out", [2, 2048, 1024])
    if "mlp" in kinds:
        C.mlp_w_up = din("mlp_w_up", [4, 1024, 4096])
        C.mlp_w_down = din("mlp_w_down", [4, 4096, 1024])
    declare_extra(nc, C, kinds, din, dint)
    C.outT = nc.dram_tensor("outT", [8, 128, T], F32, kind="ExternalOutput").ap()
    C.hA = dint("hA", [8, 128, T], F32)
    C.qF = dint("qF", [8, 128, T])
    C.qB = dint("qB", [8, 128, T])
    C.kT = dint("kT", [8, 128, T])
    C.kB = dint("kB", [8, 128, T])
    C.kEF = dint("kEF", [T, 1024])
    C.kEB = dint("kEB", [T, 1024])
    C.v = dint("v", [T, 2048])
    C.sg = dint("sg", [T, 2048])
    C.P1 = dint("P1", [T, 2048])
    C.yT = dint("yT", [16, 128, T])
    C.hid = dint("hid", [32, 128, T])
    C.xin = dint("xin", [1024, 512], F32)
    C.xout = dint("xout", [2048, 512], F32)
    P = Prog(nc)
    nsteps = len(plan)
    src = C.xT
    for i, (kind, li, j) in enumerate(plan):
        dst = C.outT if i == nsteps - 1 else C.hA
        if kind == "ret":
            layer_ret(P, C, li, j, src, dst)
        elif kind == "mlp":
            layer_mlp(P, C, li, src, dst)
        elif kind == "p_qk":
            phase_ret_qk(P, C, li, j, src)
        elif kind == "p_vg":
            phase_vg(P, C, li, src, C.ret_w_in[j][:, 2048:6144], 2048, C.v, C.sg)
        elif kind == "p_s1":
            phase_ret_sweep1(P, C, j)
        elif kind == "p_s2":
            phase_ret_sweep2(P, C, j)
        elif kind == "p_out":
            phase_outproj(P, C, li, 1, C.ret_w_out[j], 16, C.yT, src, dst)
        elif kind == "gla":
            layer_gla(P, C, li, src, dst)
        elif kind == "conv":
            layer_conv(P, C, li, src, dst)
        src = dst
    P.begin_phase()
    P.end_phase(final=True)
    P.close()
    return nc, P


FULL_PLAN = [("ret", 0, 0), ("mlp", 0, 0), ("conv", 1, 0), ("mlp", 1, 0), ("gla", 2, 0), ("mlp", 2, 0), ("ret", 3, 1), ("mlp", 3, 0)]


def make_in_maps(inputs, T, ncores=4, kinds=("ret", "mlp", "conv", "gla")):
    x = np.asarray(inputs["x"], np.float32)
    pos = np.asarray(inputs["positions"], np.int32)
    ng = np.asarray(inputs["norm_gains"], np.float32)
    L = ng.shape[0]
    gcol = np.zeros((128, 128), np.float32)
    gcol[:, :L * 32] = ng.reshape(L, 4, 8, 128).transpose(3, 0, 1, 2).reshape(128, L * 32)
    rdl = np.asarray(inputs["ret_decay_logit"], np.float32)
    cst = make_cst(0)
    maps = []
    for b in range(ncores):
        m = {}
        m["xT"] = np.ascontiguousarray(x[b].reshape(T, 8, 128).transpose(1, 2, 0))
        m["posr"] = np.ascontiguousarray(np.broadcast_to(pos[b][None, :], (128, T)))
        m["gcol"] = gcol
        m["cst"] = cst
        m["ret_dl"] = np.ascontiguousarray(np.broadcast_to(rdl.reshape(2, 1, 8), (2, 128, 8)))
        big = []
        if "ret" in kinds:
            big += ["ret_w_in", "ret_w_out"]
        if "mlp" in kinds:
            big += ["mlp_w_up", "mlp_w_down"]
        for k in big:
            m[k] = np.asarray(inputs[k], np.float32)
        extra_in_maps(m, inputs, T, b, 0, kinds)
        maps.append(m)
    return maps


def gather_out(res, T, ncores=4):
    out = np.zeros((ncores, T, 1024), np.float32)
    for b in range(ncores):
        out[b] = res.results[b]["outT"].reshape(1024, T).T
    return out


_CACHE = {}


def kernel(**inputs):
    T = 8192
    if "nc" not in _CACHE:
        _CACHE["nc"] = build(T, FULL_PLAN)[0]
    nc = _CACHE["nc"]
    maps = make_in_maps(inputs, T, ncores=4)
    res = run_bass_kernel_spmd(nc, maps, core_ids=list(range(4)))
    return gather_out(res, T, 4)
```
